# Optimizing a Trainium2 kernel written in Bass

```python
import math
import jax, jax.numpy as jnp
from jax import lax
import numpy as np

D_MODEL = 1024
BATCH = 8
SEQ = 2048
DEPTH = 1
DEC_BATCH = 16
DEC_SEQ = 2048
PAST_LEN = 128

GRID_W = 64
N_MEM = 256
MLSTM_HEADS = 4
MLSTM_DH = D_MODEL // 8
MLSTM_W = MLSTM_HEADS * MLSTM_DH
MLSTM_CHUNK = 128
ATTN_DH = 128
ATTN_HEADS = D_MODEL // ATTN_DH
KV_HEADS = 2
ATTN_GROUP = ATTN_HEADS // KV_HEADS
ATTN_W = ATTN_HEADS * ATTN_DH
Q_BLOCK = 128
ROPE_AXIS_DIM = ATTN_DH // 2
ROPE_THETA = 10000.0
MEM_HEADS = 4
MEM_DH = 128
MEM_W = MEM_HEADS * MEM_DH
N_BRANCH = 3
D_FF = ((8 * D_MODEL // 3 + 127) // 128) * 128
CONV_W = 3
DEEPNORM_ALPHA = (2.0 * DEPTH) ** 0.25
DEEPNORM_BETA = (8.0 * DEPTH) ** -0.25
LN_EPS = 1e-5
IN_SPLITS = (MLSTM_W, MLSTM_W, MLSTM_W, MLSTM_W, 4 * MLSTM_HEADS,
             ATTN_W, KV_HEADS * ATTN_DH, KV_HEADS * ATTN_DH, MEM_W, N_BRANCH * D_MODEL)
IN_W = sum(IN_SPLITS)

kernel_name = "hybrid_mlstm_gqa_mem_encoder"


def layer_norm(x, g, b):
    xf = x.astype(jnp.float32)
    mu = jnp.mean(xf, axis=-1, keepdims=True)
    xc = xf - mu
    var = jnp.mean(xc * xc, axis=-1, keepdims=True)
    return (xc * lax.rsqrt(var + LN_EPS) * g.astype(jnp.float32) + b.astype(jnp.float32)).astype(x.dtype)


def rms_norm(x, g):
    xf = x.astype(jnp.float32)
    return (xf * lax.rsqrt(jnp.mean(xf * xf, axis=-1, keepdims=True) + LN_EPS) * g.astype(jnp.float32)).astype(x.dtype)


def split_cols(x, sizes):
    out, start = [], 0
    for s in sizes:
        out.append(x[..., start:start + s])
        start += s
    return out


def dwconv_centred(x, w, b):
    T = x.shape[1]
    pad = CONV_W // 2
    xp = jnp.pad(x, ((0, 0), (pad, pad), (0, 0)))
    y = xp[:, 0:T] * w[0] + b
    for j in range(1, CONV_W):
        y = y + xp[:, j:j + T] * w[j]
    return y


def mlstm_chunkwise(q, k, v, log_i, log_f):
    B, T, H, dk = q.shape
    L = MLSTM_CHUNK
    NC = T // L

    def chunks(a):
        return jnp.moveaxis(a.reshape(B, NC, L, H, -1), 3, 1)

    q = chunks(q)
    k = chunks(k) * (dk ** -0.5)
    v = chunks(v)
    li = chunks(log_i[..., None])[..., 0]
    lf = chunks(log_f[..., None])[..., 0]
    bl = jnp.cumsum(lf, axis=-1)
    g = bl[..., -1]

    a = g[..., None] - bl + li
    ma = jnp.max(a, axis=-1)
    wa = jnp.exp(a - ma[..., None])
    kv_chunk = jnp.einsum('bhcl,bhcld,bhcle->bhcde', wa, k, v)
    n_chunk = jnp.einsum('bhcl,bhcld->bhcd', wa, k)

    def step(carry, inp):
        C, n, m = carry
        kv_c, n_c, g_c, ma_c = inp
        m_new = jnp.maximum(g_c + m, ma_c)
        sp = jnp.exp(g_c + m - m_new)
        sc = jnp.exp(ma_c - m_new)
        C_new = sp[..., None, None] * C + sc[..., None, None] * kv_c
        n_new = sp[..., None] * n + sc[..., None] * n_c
        return (C_new, n_new, m_new), (C, n, m)

    init = (jnp.zeros((B, H, dk, v.shape[-1]), jnp.float32),
            jnp.zeros((B, H, dk), jnp.float32),
            jnp.zeros((B, H), jnp.float32))
    xs = (jnp.moveaxis(kv_chunk, 2, 0), jnp.moveaxis(n_chunk, 2, 0),
          jnp.moveaxis(g, 2, 0), jnp.moveaxis(ma, 2, 0))
    _, (C_prev, n_prev, m_prev) = lax.scan(step, init, xs)
    C_prev = jnp.moveaxis(C_prev, 0, 2)
    n_prev = jnp.moveaxis(n_prev, 0, 2)
    m_prev = jnp.moveaxis(m_prev, 0, 2)

    D = bl[..., :, None] - bl[..., None, :] + li[..., None, :]
    mask = jnp.tril(jnp.ones((L, L), dtype=bool))
    D = jnp.where(mask, D, -jnp.inf)
    inter_log = bl + m_prev[..., None]
    m_t = jnp.maximum(inter_log, jnp.max(D, axis=-1))
    s = jnp.einsum('bhctd,bhcsd->bhcts', q, k) * jnp.exp(D - m_t[..., None])
    w_inter = jnp.exp(inter_log - m_t)
    num = (jnp.einsum('bhcts,bhcse->bhcte', s, v)
           + w_inter[..., None] * jnp.einsum('bhctd,bhcde->bhcte', q, C_prev))
    den = jnp.sum(s, axis=-1) + w_inter * jnp.einsum('bhctd,bhcd->bhct', q, n_prev)
    h = num / jnp.maximum(jnp.abs(den), jnp.exp(-m_t))[..., None]
    return jnp.moveaxis(h, 1, 3).reshape(B, T, H, -1)


def bidirectional_mlstm(q, k, v, gate_pre):
    B, T, H, _ = q.shape
    gt = gate_pre.reshape(B, T, 4, H)
    h_f = mlstm_chunkwise(q, k, v, gt[:, :, 0], jax.nn.log_sigmoid(gt[:, :, 1]))
    flip = lambda a: a[:, ::-1]
    h_b = flip(mlstm_chunkwise(flip(q), flip(k), flip(v), flip(gt[:, :, 2]),
                               flip(jax.nn.log_sigmoid(gt[:, :, 3]))))
    return h_f + h_b


def head_layer_norm(h, g):
    B, T, H, dv = h.shape
    mu = jnp.mean(h, axis=-1, keepdims=True)
    hc = h - mu
    var = jnp.mean(hc * hc, axis=-1, keepdims=True)
    return (hc * lax.rsqrt(var + LN_EPS)).reshape(B, T, H * dv) * g.astype(jnp.float32)


def axial_rope_angles(T):
    rows = T // GRID_W
    row = jnp.repeat(jnp.arange(rows, dtype=jnp.float32), GRID_W)
    col = jnp.tile(jnp.arange(GRID_W, dtype=jnp.float32), rows)
    inv_freq = ROPE_THETA ** (-jnp.arange(0, ROPE_AXIS_DIM, 2, dtype=jnp.float32) / ROPE_AXIS_DIM)
    return row[:, None] * inv_freq, col[:, None] * inv_freq


def rotate_axis(x, ang):
    cos = jnp.cos(ang)[None, :, None, :]
    sin = jnp.sin(ang)[None, :, None, :]
    half = ROPE_AXIS_DIM // 2
    x1, x2 = x[..., :half], x[..., half:]
    return jnp.concatenate([x1 * cos - x2 * sin, x2 * cos + x1 * sin], axis=-1)


def apply_axial_rope(x, ang_row, ang_col):
    xf = x.astype(jnp.float32)
    out = jnp.concatenate([rotate_axis(xf[..., :ROPE_AXIS_DIM], ang_row),
                           rotate_axis(xf[..., ROPE_AXIS_DIM:], ang_col)], axis=-1)
    return out.astype(x.dtype)


def blocked_attention(q, k, v):
    B, T, Hkv, G, dh = q.shape
    NB = T // Q_BLOCK
    qb = jnp.moveaxis(q.reshape(B, NB, Q_BLOCK, Hkv, G, dh), 1, 0)
    scale = dh ** -0.5

    def one_block(qblk):
        s = jnp.einsum('bqhgd,bkhd->bhgqk', qblk, k).astype(jnp.float32) * scale
        p = jax.nn.softmax(s, axis=-1).astype(v.dtype)
        return jnp.einsum('bhgqk,bkhd->bqhgd', p, v)

    o = lax.map(one_block, qb)
    return jnp.moveaxis(o, 0, 1).reshape(B, T, Hkv * G * dh)


def token_mixing(x, mem, w_in, mlstm_gate_bias, mlstm_conv_w, mlstm_conv_b, mlstm_norm_g,
                 attn_q_norm_g, attn_k_norm_g, w_mem_kv, w_branch_mlstm, w_branch_attn,
                 w_branch_mem, w_out):
    B, T, _ = x.shape
    proj = x @ w_in
    q_m, k_m, v_m, o_m, g_m, q_a, k_a, v_a, q_c, g_br = split_cols(proj, IN_SPLITS)

    qk_m = jax.nn.silu(dwconv_centred(jnp.concatenate([q_m, k_m], axis=-1), mlstm_conv_w, mlstm_conv_b))
    heads_m = lambda a: a.reshape(B, T, MLSTM_HEADS, MLSTM_DH).astype(jnp.float32)
    h_m = bidirectional_mlstm(heads_m(qk_m[..., :MLSTM_W]), heads_m(qk_m[..., MLSTM_W:]), heads_m(v_m),
                              (g_m + mlstm_gate_bias).astype(jnp.float32))
    h_m = head_layer_norm(h_m, mlstm_norm_g).astype(x.dtype) * jax.nn.sigmoid(o_m)

    ang_r, ang_c = axial_rope_angles(T)
    qa = apply_axial_rope(rms_norm(q_a.reshape(B, T, ATTN_HEADS, ATTN_DH), attn_q_norm_g), ang_r, ang_c)
    ka = apply_axial_rope(rms_norm(k_a.reshape(B, T, KV_HEADS, ATTN_DH), attn_k_norm_g), ang_r, ang_c)
    h_a = blocked_attention(qa.reshape(B, T, KV_HEADS, ATTN_GROUP, ATTN_DH), ka,
                            v_a.reshape(B, T, KV_HEADS, ATTN_DH))

    M = mem.shape[1]
    kv_c = mem @ w_mem_kv
    k_c = kv_c[..., :MEM_W].reshape(B, M, MEM_HEADS, MEM_DH)
    v_c = kv_c[..., MEM_W:].reshape(B, M, MEM_HEADS, MEM_DH)
    s_c = jnp.einsum('bthd,bmhd->bhtm', q_c.reshape(B, T, MEM_HEADS, MEM_DH), k_c).astype(jnp.float32) * (MEM_DH ** -0.5)
    p_c = jax.nn.softmax(s_c, axis=-1).astype(v_c.dtype)
    h_c = jnp.einsum('bhtm,bmhd->bthd', p_c, v_c).reshape(B, T, MEM_W)

    gates = jax.nn.sigmoid(g_br.reshape(B, T, N_BRANCH, D_MODEL))
    merged = (gates[:, :, 0] * (h_m @ w_branch_mlstm)
              + gates[:, :, 1] * (h_a @ w_branch_attn)
              + gates[:, :, 2] * (h_c @ w_branch_mem))
    return merged @ w_out


def conv_ffn(x, w_ffn_up, ffn_conv_w, ffn_conv_b, w_ffn_down):
    u = dwconv_centred(x @ w_ffn_up, ffn_conv_w, ffn_conv_b)
    return (jax.nn.gelu(u[..., :D_FF]) * u[..., D_FF:]) @ w_ffn_down


def encoder_trunk(x, mem, ln_in_g, ln_in_b, w_in, mlstm_gate_bias, mlstm_conv_w, mlstm_conv_b,
                  mlstm_norm_g, attn_q_norm_g, attn_k_norm_g, w_mem_kv, w_branch_mlstm,
                  w_branch_attn, w_branch_mem, w_out, ln1_g, ln1_b, w_ffn_up, ffn_conv_w,
                  ffn_conv_b, w_ffn_down, ln2_g, ln2_b):
    x = layer_norm(x, ln_in_g, ln_in_b)
    for l in range(DEPTH):
        mixed = token_mixing(x, mem, w_in[l], mlstm_gate_bias[l], mlstm_conv_w[l], mlstm_conv_b[l],
                             mlstm_norm_g[l], attn_q_norm_g[l], attn_k_norm_g[l], w_mem_kv[l],
                             w_branch_mlstm[l], w_branch_attn[l], w_branch_mem[l], w_out[l])
        x = layer_norm(DEEPNORM_ALPHA * x + mixed, ln1_g[l], ln1_b[l])
        ff = conv_ffn(x, w_ffn_up[l], ffn_conv_w[l], ffn_conv_b[l], w_ffn_down[l])
        x = layer_norm(DEEPNORM_ALPHA * x + ff, ln2_g[l], ln2_b[l])
    return x


def setup_inputs(seed: int = 0) -> dict:
    key = jax.random.key(seed)
    ks = jax.random.split(key, 32)
    f32 = jnp.float32
    nrm = lambda i, shape, scale: jax.random.normal(ks[i], shape, f32) * scale
    fb = jnp.linspace(3.0, 6.0, MLSTM_HEADS, dtype=f32)
    zh = jnp.zeros((MLSTM_HEADS,), f32)
    gate_base = jnp.concatenate([zh, fb, zh, fb])
    return {
        "x_prompt": nrm(0, (BATCH, SEQ, D_MODEL), 1.0),
        "x_sample": nrm(1, (DEC_BATCH, DEC_SEQ, D_MODEL), 1.0),
        "mem_prompt": nrm(2, (BATCH, N_MEM, D_MODEL), 1.0),
        "mem_sample": nrm(3, (DEC_BATCH, N_MEM, D_MODEL), 1.0),
        "ln_in_g": 1.0 + nrm(4, (D_MODEL,), 0.02),
        "ln_in_b": nrm(5, (D_MODEL,), 0.02),
        "w_in": nrm(6, (DEPTH, D_MODEL, IN_W), D_MODEL ** -0.5),
        "mlstm_gate_bias": gate_base + nrm(7, (DEPTH, 4 * MLSTM_HEADS), 0.1),
        "mlstm_conv_w": nrm(8, (DEPTH, CONV_W, 2 * MLSTM_W), CONV_W ** -0.5),
        "mlstm_conv_b": nrm(9, (DEPTH, 2 * MLSTM_W), 0.02),
        "mlstm_norm_g": 1.0 + nrm(10, (DEPTH, MLSTM_W), 0.02),
        "attn_q_norm_g": 1.0 + nrm(11, (DEPTH, ATTN_DH), 0.02),
        "attn_k_norm_g": 1.0 + nrm(12, (DEPTH, ATTN_DH), 0.02),
        "w_mem_kv": nrm(13, (DEPTH, D_MODEL, 2 * MEM_W), D_MODEL ** -0.5),
        "w_branch_mlstm": nrm(14, (DEPTH, MLSTM_W, D_MODEL), MLSTM_W ** -0.5 * DEEPNORM_BETA),
        "w_branch_attn": nrm(15, (DEPTH, ATTN_W, D_MODEL), ATTN_W ** -0.5 * DEEPNORM_BETA),
        "w_branch_mem": nrm(16, (DEPTH, MEM_W, D_MODEL), MEM_W ** -0.5 * DEEPNORM_BETA),
        "w_out": nrm(17, (DEPTH, D_MODEL, D_MODEL), D_MODEL ** -0.5 * DEEPNORM_BETA),
        "ln1_g": 1.0 + nrm(18, (DEPTH, D_MODEL), 0.02),
        "ln1_b": nrm(19, (DEPTH, D_MODEL), 0.02),
        "w_ffn_up": nrm(20, (DEPTH, D_MODEL, 2 * D_FF), D_MODEL ** -0.5 * DEEPNORM_BETA),
        "ffn_conv_w": nrm(21, (DEPTH, CONV_W, 2 * D_FF), CONV_W ** -0.5),
        "ffn_conv_b": nrm(22, (DEPTH, 2 * D_FF), 0.02),
        "w_ffn_down": nrm(23, (DEPTH, D_FF, D_MODEL), D_FF ** -0.5 * DEEPNORM_BETA),
        "ln2_g": 1.0 + nrm(24, (DEPTH, D_MODEL), 0.02),
        "ln2_b": nrm(25, (DEPTH, D_MODEL), 0.02),
    }


def reference(x_prompt, x_sample, mem_prompt, mem_sample, ln_in_g, ln_in_b, w_in, mlstm_gate_bias,
              mlstm_conv_w, mlstm_conv_b, mlstm_norm_g, attn_q_norm_g, attn_k_norm_g, w_mem_kv,
              w_branch_mlstm, w_branch_attn, w_branch_mem, w_out, ln1_g, ln1_b, w_ffn_up,
              ffn_conv_w, ffn_conv_b, w_ffn_down, ln2_g, ln2_b):
    weights = (ln_in_g, ln_in_b, w_in, mlstm_gate_bias, mlstm_conv_w, mlstm_conv_b, mlstm_norm_g,
               attn_q_norm_g, attn_k_norm_g, w_mem_kv, w_branch_mlstm, w_branch_attn, w_branch_mem,
               w_out, ln1_g, ln1_b, w_ffn_up, ffn_conv_w, ffn_conv_b, w_ffn_down, ln2_g, ln2_b)
    y_prompt = encoder_trunk(x_prompt, mem_prompt, *weights)
    y_sample = encoder_trunk(x_sample, mem_sample, *weights)
    return (y_prompt, y_sample)
```

```python
import math
import numpy as np
import concourse.bass as bass
import concourse.mybir as mybir
from concourse.bass_utils import run_bass_kernel_spmd

F32 = mybir.dt.float32
BF16 = mybir.dt.bfloat16
AF = mybir.ActivationFunctionType
ALU = mybir.AluOpType
AX = mybir.AxisListType

T = 2048
D = 1024
NT = 16
DFF = 2816
EPS = 1e-5
ALPHA = 2.0 ** 0.25
O_QM, O_KM, O_VM, O_OM, O_GM, O_QA, O_KA, O_VA, O_QC, O_GB = 0, 512, 1024, 1536, 2048, 2064, 3088, 3344, 3600, 4112


class Ctr:
    __slots__ = ("sem", "step", "val", "name")

    def __init__(self, sem, step, name):
        self.sem, self.step, self.val, self.name = sem, step, 0, name


class Buf:
    __slots__ = ("name", "w", "r")

    def __init__(self, name=""):
        self.name = name
        self.w = None
        self.r = {}


class Sched:
    def __init__(self, nc):
        self.nc = nc
        self.names = ("pe", "act", "dve", "pool", "sp")
        self.q = {k: [] for k in self.names}
        self.ctr = {}
        for k in ("pe", "act", "dve", "pool"):
            self.ctr[k] = Ctr(nc.alloc_semaphore(name="s_" + k), 1, k)
        self.known = {k: {} for k in self.names}
        self.dma_ctrs = []
        self.marks = []

    def dma_ctr(self, name):
        c = Ctr(self.nc.alloc_semaphore(name="d_" + name), 16, name)
        self.dma_ctrs.append(c)
        return c

    def op(self, eng, fn, reads=(), writes=(), ctr=None):
        c = ctr if ctr is not None else self.ctr[eng]
        own = self.ctr.get(eng)
        need = {}

        def req(dep, raw):
            if dep is None:
                return
            dc, dv = dep
            if dc is own and not raw and eng == "pe":
                return
            if need.get(dc, 0) < dv:
                need[dc] = dv

        for b in reads:
            req(b.w, True)
        for b in writes:
            req(b.w, False)
            for dc, dv in b.r.items():
                req((dc, dv), False)
        kn = self.known[eng]
        waits = []
        for dc, dv in need.items():
            if kn.get(dc, 0) >= dv:
                continue
            kn[dc] = dv
            waits.append((dc, dv))
        c.val += c.step
        for b in reads:
            if b.r.get(c, 0) < c.val:
                b.r[c] = c.val
        for b in writes:
            b.w = (c, c.val)
            b.r = {}
        self.q[eng].append((waits, fn, c))

    def emit(self, final=False):
        nc = self.nc
        q = self.q
        self.marks.append({k: self.ctr[k].val for k in ("pe", "act", "dve", "pool")})
        self.q = {k: [] for k in self.names}
        with nc.Block(no_gpsimd_drain=True) as block:
            def body(name):
                def f(e):
                    for waits, fn, c in q[name]:
                        for dc, dv in waits:
                            e.wait_ge(dc.sem, dv)
                        fn(e).then_inc(c.sem, c.step)
                    if name == "sp":
                        for c in self.dma_ctrs:
                            if c.val:
                                e.wait_ge(c.sem, c.val)
                return f
            block.tensor(body("pe"))
            block.scalar(body("act"))
            block.vector(body("dve"))
            block.gpsimd(body("pool"))
            block.sync(body("sp"))


def build(nseq, dbg=False):
    nc = bass.Bass("TRN2", target_bir_lowering=False)
    S = Sched(nc)

    def din(name, shape):
        return nc.dram_tensor(name, list(shape), F32, kind="ExternalInput").ap()

    x_d = din("x", [nseq, T, D])
    mem_d = din("mem", [nseq, 256, D])
    w_in = din("w_in", [D, 7184])
    w_mkv = din("w_mkv", [D, 1024])
    w_bm = din("w_bm", [512, D])
    w_ba = din("w_ba", [1024, D])
    w_bc = din("w_bc", [512, D])
    w_out = din("w_out", [D, D])
    w_up = din("w_up", [D, 2 * DFF])
    w_dn = din("w_dn", [DFF, D])
    lnin_g = din("lnin_g", [D]); lnin_b = din("lnin_b", [D])
    ln1_g = din("ln1_g", [D]); ln1_b = din("ln1_b", [D])
    ln2_g = din("ln2_g", [D]); ln2_b = din("ln2_b", [D])
    gbias_d = din("gbias", [16])
    mconv_w = din("mconv_w", [3, 1024]); mconv_b = din("mconv_b", [1024])
    mnorm_g = din("mnorm_g", [512])
    qg_d = din("qg", [128]); kg_d = din("kg", [128])
    fconv_w = din("fconv_w", [3, 2 * DFF]); fconv_b = din("fconv_b", [2 * DFF])
    c_ident = din("c_ident", [128, 128])
    c_maskF = din("c_maskF", [128, 128])
    c_maskB = din("c_maskB", [128, 128])
    c_ropeC = din("c_ropeC", [T, 128])
    c_ropeS = din("c_ropeS", [T, 128])
    y_d = nc.dram_tensor("y", [nseq, T, D], F32, kind="ExternalOutput").ap()
    x1s_d = nc.dram_tensor("x1s", [T, D], F32, kind=("ExternalOutput" if dbg else "Internal")).ap()
    if dbg:
        dbg_h = nc.dram_tensor("dbg_h", [128, 16, T], BF16, kind="ExternalOutput").ap()

    class Arena:
        def __init__(self):
            self.segs = []
            self.n = 0

        def set(self, segs):
            self.segs = [[lo, hi] for lo, hi in segs]

        def __call__(self, name, shape, dt):
            nb = 2 if dt == BF16 else 4
            size = nb
            for s_ in shape[1:]:
                size *= s_
            size = (size + 31) // 32 * 32
            for sg in self.segs:
                if sg[0] + size <= sg[1]:
                    off = sg[0]
                    sg[0] += size
                    self.n += 1
                    return nc.alloc_sbuf_tensor_at("%s_%d" % (name, self.n), list(shape), dt, offset=off)
            raise RuntimeError("arena full: %s %s" % (name, shape))

    BASE = 16512
    TOP = 229344
    sb = Arena()
    sb.set([(BASE, TOP)])

    class _CM:
        def __init__(self, t):
            self.t = t

        def __enter__(self):
            return self.t

        def __exit__(self, *a):
            return False

    def sbt(name, shape, dt):
        return _CM(sb(name, shape, dt))

    def mm(out, lhsT, rhs, start, stop, R, W, sgc=False):
        if sgc:
            S.op("pe", lambda e: e.matmul(out, lhsT=lhsT, rhs=rhs, start=start, stop=stop, skip_group_check=True), reads=R, writes=W)
        else:
            S.op("pe", lambda e: e.matmul(out, lhsT=lhsT, rhs=rhs, start=start, stop=stop), reads=R, writes=W)

    def tr(out, in_, ident, R, W):
        S.op("pe", lambda e: e.transpose(out=out, in_=in_, identity=ident), reads=R, writes=W)

    def act(out, in_, func, R, W, bias=None, scale=None, accum=None):
        kw = {}
        if bias is not None:
            kw["bias"] = bias
        if scale is not None:
            kw["scale"] = scale
        if accum is not None:
            kw["accum_out"] = accum
        S.op("act", lambda e: e.activation(out=out, in_=in_, func=func, **kw), reads=R, writes=W)

    def ts(eng, out, in0, s1, s2, op0, op1, R, W):
        if op1 is None:
            S.op(eng, lambda e: e.tensor_scalar(out=out, in0=in0, scalar1=s1, scalar2=None, op0=op0), reads=R, writes=W)
        else:
            S.op(eng, lambda e: e.tensor_scalar(out=out, in0=in0, scalar1=s1, scalar2=s2, op0=op0, op1=op1), reads=R, writes=W)

    def tt(eng, out, in0, in1, op, R, W):
        S.op(eng, lambda e: e.tensor_tensor(out=out, in0=in0, in1=in1, op=op), reads=R, writes=W)

    def stt(out, in0, scalar, in1, op0, op1, R, W):
        S.op("dve", lambda e: e.scalar_tensor_tensor(out=out, in0=in0, scalar=scalar, in1=in1, op0=op0, op1=op1), reads=R, writes=W)

    def cp(eng, out, in_, R, W):
        if eng == "act":
            S.op("act", lambda e: e.copy(out=out, in_=in_), reads=R, writes=W)
        else:
            S.op(eng, lambda e: e.tensor_copy(out=out, in_=in_), reads=R, writes=W)

    def recip(out, in_, R, W):
        S.op("dve", lambda e: e.reciprocal(out=out, in_=in_), reads=R, writes=W)

    def memset(eng, ap, val, W):
        S.op(eng, lambda e: e.memset(ap, val), writes=W)

    def mdma(out, in_, R, W):
        k = mc_i[0] % NMC
        mc_i[0] += 1
        dma(out, in_, R, list(W) + [B_mc[k]], C_mc[k])

    def dma(out, in_, R, W, ctr, slow=False):
        if slow:
            S.op("sp", lambda e: e.dma_start(out=out, in_=in_, allow_slow_non_contiguous=True), reads=R, writes=W, ctr=ctr)
        else:
            S.op("sp", lambda e: e.dma_start(out=out, in_=in_), reads=R, writes=W, ctr=ctr)

    def pipeline(n, stages, oldest_first=True, order=None):
        k = len(stages)
        for it in range(n + k - 1):
            if order is None:
                order = list(reversed(range(k))) if oldest_first else list(range(k))
            for si in order:
                t_ = it - si
                if 0 <= t_ < n:
                    stages[si](t_)

    ps = nc.alloc_psum_tensor("ps", [128, 7, 512], F32)
    psb = nc.alloc_psum_tensor("psb", [128, 1024], BF16)
    PB = [Buf("ps%d" % i) for i in range(7)]
    PBB = Buf("psb")

    ident_f = sb("ident_f", [128, 128], F32)
    ident_b = sb("ident_b", [128, 128], BF16)
    maskF = sb("maskF", [128, 128], F32)
    maskB = sb("maskB", [128, 128], F32)
    ones_f = sb("ones_f", [128, 128], F32)
    B_const = Buf("const")
    lnin_fm = sb("lnin_fm", [128, 2, 8], F32)
    ln1_fm = sb("ln1_fm", [128, 2, 8], F32)
    mconv_fm = sb("mconv_fm", [128, 4, 8], F32)
    fconv_fm = sb("fconv_fm", [128, 4, 44], F32)
    negc = sb("negc", [128, 1], F32)
    ctmp = sb("ctmp", [128, 4], F32)
    mhalf = sb("mhalf", [128, 16], F32)
    gbt_in = sb("gbt_in", [128, 2, D], F32)

    NSTG = 3
    stg = [sb("stg%d" % i, [128, 2048], F32) for i in range(NSTG)]
    B_stg = [Buf("stg%d" % i) for i in range(NSTG)]
    C_stg = [S.dma_ctr("stg%d" % i) for i in range(NSTG)]
    stg_i = [0]

    xx = sb("xx", [128, 8, T], BF16)
    B_xx = [Buf("xx%d" % g) for g in range(4)]
    stats_in = sb("stats_in", [128, NT, 2], F32)
    B_stats_in = Buf("stats_in")

    c_misc = S.dma_ctr("misc")
    NMC = 8
    C_mc = [S.dma_ctr("mc%d" % i) for i in range(NMC)]
    B_mc = [Buf("mc%d" % i) for i in range(NMC)]
    mc_i = [0]
    C_slots = {}

    def slot_ctrs(name):
        if name not in C_slots:
            C_slots[name] = [S.dma_ctr(name + "0"), S.dma_ctr(name + "1")]
        return C_slots[name]
    P0_END = sb.segs[0][0]
    H0 = P0_END
    hmT = nc.alloc_sbuf_tensor_at("hmT", [128, 4, T], BF16, offset=H0)
    hcT = nc.alloc_sbuf_tensor_at("hcT", [128, 4, T], BF16, offset=H0 + 16384)
    haT = nc.alloc_sbuf_tensor_at("haT", [128, 8, T], BF16, offset=H0 + 32768)
    H_END = H0 + 65536
    M0 = H_END
    mT = nc.alloc_sbuf_tensor_at("mT", [128, 8, T], BF16, offset=M0)
    M_END = M0 + 32768
    assert M_END + 40 * 1024 < TOP, (P0_END, M_END, TOP)
    sb.set([(P0_END, TOP)])

    WS_TOTAL = 160 * 1024
    wscr = nc.dram_tensor("wscr", [128, WS_TOTAL], BF16, kind="Internal").ap()
    wcache = {}
    ws_off = [0]
    NWC = 6
    C_wc = [S.dma_ctr("wc%d" % i) for i in range(NWC)]
    B_wc = [Buf("wc%d" % i) for i in range(NWC)]
    wc_i = [0]

    def load_w(dst, src, n_a, n_b, B_dst, key):
        L = n_a * n_b
        k = wc_i[0] % NWC
        wc_i[0] += 1
        if key in wcache:
            off, Bc = wcache[key]
            dma(dst, wscr[:, off:off + L].rearrange("p (a b) -> p a b", b=n_b), [Bc], [B_dst, B_wc[k]], C_wc[k])
            return
        i = stg_i[0] % NSTG
        stg_i[0] += 1
        sv = stg[i][:, 0:L].rearrange("p (a b) -> p a b", b=n_b)
        dma(sv, src, [], [B_stg[i]], C_stg[i])
        cp("pool" if i % 2 == 0 else "act", dst, sv, [B_stg[i]], [B_dst])
        off = ws_off[0]
        ws_off[0] += L
        assert ws_off[0] <= WS_TOTAL
        Bc = Buf("wscr")
        wcache[key] = (off, Bc)
        dma(wscr[:, off:off + L].rearrange("p (a b) -> p a b", b=n_b), dst, [B_dst], [Bc, B_wc[k]], C_wc[k])

    def win_cols(c0, n):
        return w_in[:, c0:c0 + n].rearrange("(kc p) n -> p kc n", p=128)

    dma(ident_f[:], c_ident, [], [B_const], c_misc)
    mdma(maskF[:], c_maskF, [], [B_const])
    mdma(maskB[:], c_maskB, [], [B_const])
    cm = S.dma_ctr("m3")
    dma(lnin_fm[:, 0, :], lnin_g.rearrange("(kc p) -> p kc", p=128), [], [B_const], cm, slow=True)
    for (dst, src) in [(lnin_fm[:, 1, :], lnin_b.rearrange("(kc p) -> p kc", p=128)),
                       (ln1_fm[:, 0, :], ln1_g.rearrange("(kc p) -> p kc", p=128)),
                       (ln1_fm[:, 1, :], ln1_b.rearrange("(kc p) -> p kc", p=128)),
                       (mconv_fm[:, 3, :], mconv_b.rearrange("(kc p) -> p kc", p=128)),
                       (fconv_fm[:, 3, :], fconv_b.rearrange("(kc p) -> p kc", p=128))]:
        dma(dst, src, [], [B_const], cm, slow=True)
    for j in range(3):
        dma(mconv_fm[:, j, :], mconv_w[j, :].rearrange("(kc p) -> p kc", p=128), [], [B_const], cm, slow=True)
        dma(fconv_fm[:, j, :], fconv_w[j, :].rearrange("(kc p) -> p kc", p=128), [], [B_const], cm, slow=True)
    cp("pool", ident_b[:], ident_f[:], [B_const], [B_const])
    memset("pool", ones_f[:], 1.0, [B_const])
    memset("pool", mhalf[:], -0.5, [B_const])
    for i_, v_ in enumerate([lnin_g, lnin_b]):
        mdma(gbt_in[:, i_, :], v_.partition_broadcast(128), [], [B_const])
    ts("dve", gbt_in[:], gbt_in[:], ALPHA, None, ALU.mult, None, [B_const], [B_const])
    memset("pool", ctmp[:, 2:3], -0.5 * math.log(128.0), [B_const])
    gtmp = sb("gtmp", [128, 2, 128], F32)
    if True:
        Bg = Buf("gtmp")
        mdma(gtmp[:, 0, :], qg_d.partition_broadcast(128), [], [Bg])
        mdma(gtmp[:, 1, :], kg_d.partition_broadcast(128), [], [Bg])
        S.op("dve", lambda e: e.tensor_reduce(out=ctmp[:, 0:2], in_=gtmp[:], axis=AX.X, op=ALU.max, apply_absolute_value=True),
             reads=[Bg], writes=[B_const])
        stt(negc[:], ctmp[:, 0:1], -math.sqrt(128.0), ctmp[:, 1:2], ALU.mult, ALU.mult, [B_const], [B_const])
        S.emit()

    def layer_norm_stats(src_ap, st, mv, rstd, nmr, Bsrc, Bst):
        for i in range(2):
            S.op("dve", lambda e, i=i: e.bn_stats(out=st[:, i, :], in_=src_ap[:, i * 512:(i + 1) * 512]), reads=[Bsrc], writes=[Bst])
        S.op("dve", lambda e: e.bn_aggr(out=mv, in_=st), reads=[Bst], writes=[Bst])
        ts("dve", rstd, mv[:, 1:2], EPS, None, ALU.add, None, [Bst], [Bst])
        tt("pool", rstd, rstd, mhalf[:, 0:1], ALU.pow, [Bst, B_const], [Bst])
        stt(nmr, mv[:, 0:1], -1.0, rstd, ALU.mult, ALU.mult, [Bst], [Bst])

    for seq in range(nseq):
        xs = x_d[seq]
        sb.set([(P0_END, TOP)])
        xt = sb("a_xt", [128, 2, D], F32)
        xh = sb("a_xh", [128, 2, D], F32)
        stt_t = sb("a_st", [128, 2, 16], F32)
        if True:
            B_xt = [Buf(), Buf()]; B_xh = [Buf(), Buf()]; B_st = [Buf(), Buf()]
            C_xt = slot_ctrs("axt")
            def a_1(t_):
                s = t_ % 2
                dma(xt[:, s, :], xs[t_ * 128:(t_ + 1) * 128, :], [], [B_xt[s]], C_xt[s])
                st = stt_t[:, s, 0:12].rearrange("p (a b) -> p a b", b=6)
                mv = stt_t[:, s, 12:14]
                for i in range(2):
                    S.op("dve", lambda e, i=i: e.bn_stats(out=st[:, i, :], in_=xt[:, s, i * 512:(i + 1) * 512]), reads=[B_xt[s]], writes=[B_st[s]])
                S.op("dve", lambda e: e.bn_aggr(out=mv, in_=st), reads=[B_st[s]], writes=[B_st[s]])
                ts("dve", stats_in[:, t_, 0:1], mv[:, 1:2], EPS, None, ALU.add, None, [B_st[s]], [B_st[s], B_stats_in])
                tt("pool", stats_in[:, t_, 0:1], stats_in[:, t_, 0:1], mhalf[:, 0:1], ALU.pow, [B_st[s], B_const], [B_st[s], B_stats_in])

            def a_2(t_):
                s = t_ % 2
                mv = stt_t[:, s, 12:14]
                rstd = stats_in[:, t_, 0:1]
                nmr = stats_in[:, t_, 1:2]
                stt(nmr, mv[:, 0:1], -1.0, rstd, ALU.mult, ALU.mult, [B_st[s]], [B_st[s], B_stats_in])
                act(xh[:, s, :], xt[:, s, :], AF.Identity, [B_xt[s], B_st[s]], [B_xh[s]], bias=nmr, scale=rstd)

            def a_3(t_):
                s = t_ % 2
                b0 = 2 * s
                for kc in range(8):
                    tr(ps[:, b0 + kc // 4, (kc % 4) * 128:(kc % 4 + 1) * 128], xh[:, s, kc * 128:(kc + 1) * 128], ident_f[:],
                       [B_xh[s], B_const], [PB[b0 + kc // 4]])
                for kc in range(8):
                    if kc % 2 == 0:
                        ts("dve", xx[:, kc, t_ * 128:(t_ + 1) * 128], ps[:, b0 + kc // 4, (kc % 4) * 128:(kc % 4 + 1) * 128],
                           lnin_fm[:, 0, kc:kc + 1], lnin_fm[:, 1, kc:kc + 1], ALU.mult, ALU.add,
                           [B_const], [PB[b0 + kc // 4], B_xx[t_ // 4]])
                    else:
                        act(xx[:, kc, t_ * 128:(t_ + 1) * 128], ps[:, b0 + kc // 4, (kc % 4) * 128:(kc % 4 + 1) * 128], AF.Identity,
                            [B_const], [PB[b0 + kc // 4], B_xx[t_ // 4]], bias=lnin_fm[:, 1, kc:kc + 1], scale=lnin_fm[:, 0, kc:kc + 1])

            pipeline(NT, [a_1, a_2, a_3], oldest_first=True)

            S.emit()

        if True:
            B_hm = Buf("hm"); B_ha = Buf("ha"); B_hc = Buf("hc")

            sb.set([(H0 + 16384, TOP)])
            gw = sb("b_gw", [128, 8, 16], BF16)
            gbB = sb("b_gb", [128, 16], F32)
            GT = sb("b_GT", [128, 16, 16], F32)
            SPt = sb("b_SP", [128, 16, 8], F32)
            UU = sb("b_U", [128, 4, 16, 8], F32)
            gtm = sb("b_tmp", [128, 16, 8], F32)
            mgB = sb("b_mg", [128, 512], F32)
            if True:
                B_gw = Buf(); B_g = Buf("gates")
                load_w(gw[:], win_cols(O_GM, 16), 8, 16, B_gw, ("gw",))
                mdma(gbB[:], gbias_d.partition_broadcast(128), [], [B_g])
                mdma(mgB[:], mnorm_g.partition_broadcast(128), [], [B_g])
                for t_ in range(NT):
                    for kc in range(8):
                        mm(ps[:, 0, t_ * 16:(t_ + 1) * 16], xx[:, kc, t_ * 128:(t_ + 1) * 128], gw[:, kc, :], kc == 0, kc == 7,
                           [B_xx[t_ // 4], B_gw], [PB[0]])
                tt("dve", GT[:], ps[:, 0, 0:256].rearrange("p (a b) -> p a b", b=16), gbB[:].unsqueeze(1).broadcast_to([128, 16, 16]),
                   ALU.add, [B_g], [PB[0], B_g])
                GT5 = GT[:].rearrange("p c (d k h) -> p c d k h", d=2, k=2, h=4)
                SP4 = SPt[:].rearrange("p c (d h) -> p c d h", d=2)
                act(SP4, GT5[:, :, :, 1, :], AF.Exp, [B_g], [B_g], scale=-1.0)
                act(SPt[:], SPt[:], AF.Ln, [B_g], [B_g], bias=1.0)
                mm(ps[:, 1, 0:64], maskF[:], SP4[:, :, 0, :], True, True, [B_const, B_g], [PB[1]])
                mm(ps[:, 1, 64:128], maskB[:], SP4[:, :, 1, :], True, True, [B_const, B_g], [PB[1]])
                mm(ps[:, 1, 128:256], ones_f[:], SPt[:], True, True, [B_const, B_g], [PB[1]])
                U4 = UU[:].rearrange("p k c (d h) -> p k c d h", d=2)
                g4 = gtm[:].rearrange("p c (d h) -> p c d h", d=2)
                for d_ in range(2):
                    tt("dve", g4[:, :, d_, :], ps[:, 1, d_ * 64:(d_ + 1) * 64].rearrange("p (c h) -> p c h", h=4), GT5[:, :, d_, 0, :],
                       ALU.add, [B_g], [PB[1], B_g])
                    act(U4[:, 2, :, d_, :], ps[:, 1, d_ * 64:(d_ + 1) * 64].rearrange("p (c h) -> p c h", h=4), AF.Exp, [], [PB[1], B_g])
                act(UU[:, 0, :, :], gtm[:], AF.Exp, [B_g], [B_g], bias=ctmp[:, 2:3])
                tt("dve", gtm[:], gtm[:], ps[:, 1, 128:256].rearrange("p (c k) -> p c k", k=8), ALU.subtract, [B_g], [PB[1], B_g])
                act(UU[:, 1, :, :], gtm[:], AF.Exp, [B_g], [B_g], bias=ctmp[:, 2:3])
                act(UU[:, 3, :, :], ps[:, 1, 128:256].rearrange("p (c k) -> p c k", k=8), AF.Exp, [], [PB[1], B_g], scale=-1.0)

                mark_b = [list(s_) for s_ in sb.segs]
                B_w = [Buf() for _ in range(4)]; B_pre = [Buf(), Buf()]; B_dg = Buf(); B_qk = [Buf(), Buf()]
                B_va = Buf(); B_kw = Buf(); B_cm = [[Buf(), Buf()], [Buf(), Buf()]]; B_cb = [[Buf(), Buf()], [Buf(), Buf()]]
                B_at = [Buf(), Buf()]
                wb_ = sb("b_w", [128, 4, 8, 128], BF16)
                pre = sb("b_pre", [128, 2, 2050], BF16)
                dg = sb("b_dg", [128, 6, 128], BF16)
                qkT = sb("b_qk", [128, 2, T], BF16)
                VA = sb("b_va", [128, 16, 129], BF16)
                KW = sb("b_kw", [128, 16, 2, 128], BF16)
                CM = sb("b_cm", [128, 2, 2, 129], F32)
                CB = sb("b_cb", [128, 2, 2, 129], BF16)
                AT = sb("b_at", [128, 2, 16, 128], BF16)
                OS_l = [sb("b_os", [128, 16, 128], F32) for _ in range(2)]
                RAW_l = [sb("b_raw", [128, 2, 16, 129], F32) for _ in range(2)]
                sm2_l = [sb("b_sm2", [128, 2, 2, 16, 1], F32) for _ in range(2)]
                lnst_l = [sb("b_ln", [128, 16, 8], F32) for _ in range(2)]
                HB_l = [sb("b_hb", [128, 16, 128], BF16) for _ in range(2)]
                B_os_l = [Buf(), Buf()]; B_raw_l = [[Buf(), Buf()], [Buf(), Buf()]]; B_sm2_l = [Buf(), Buf()]; B_ln_l = [Buf(), Buf()]; B_hb_l = [Buf(), Buf()]

                def head_P(h):
                    OS = OS_l[h % 2]; RAW = RAW_l[h % 2]; sm2 = sm2_l[h % 2]; lnst = lnst_l[h % 2]; HB = HB_l[h % 2]
                    B_os = B_os_l[h % 2]; B_raw = B_raw_l[h % 2]; B_sm2 = B_sm2_l[h % 2]; B_ln = B_ln_l[h % 2]; B_hb = B_hb_l[h % 2]
                    for i, o in enumerate([O_QM, O_KM, O_VM, O_OM]):
                        load_w(wb_[:, i, :, :], win_cols(o + h * 128, 128), 8, 128, B_w[i], ("bw", h, i))
                    if h == 0:
                        memset("pool", pre[:], 0.0, B_pre)
                        memset("pool", VA[:, :, 128:129], 1.0, [B_va])
                    for i in range(2):
                        for j in range(3):
                            act(dg[:, i * 3 + j, :], ident_f[:], AF.Identity, [B_const], [B_dg], scale=mconv_fm[:, j, i * 4 + h:i * 4 + h + 1])
                    for i in range(2):
                        for g in range(4):
                            bk = 2 + (i * 4 + g) % 2
                            for kc in range(8):
                                mm(ps[:, bk, :], wb_[:, i, kc, :], xx[:, kc, g * 512:(g + 1) * 512], kc == 0, kc == 7,
                                   [B_w[i], B_xx[g]], [PB[bk]])
                            cp("act", pre[:, i, 1 + g * 512:1 + (g + 1) * 512], ps[:, bk, :], [], [PB[bk], B_pre[i]])
                    for i in (2, 3):
                        for g in range(4):
                            bk = 0 + (i * 4 + g) % 2
                            for t4 in range(4):
                                t_ = g * 4 + t4
                                for kc in range(8):
                                    mm(ps[:, bk, t4 * 128:(t4 + 1) * 128], xx[:, kc, t_ * 128:(t_ + 1) * 128], wb_[:, i, kc, :], kc == 0, kc == 7,
                                       [B_xx[g], B_w[i]], [PB[bk]])
                            src = ps[:, bk, :].rearrange("p (a b) -> p a b", b=128)
                            if i == 2:
                                cp("act", VA[:, g * 4:(g + 1) * 4, 0:128], src, [], [PB[bk], B_va])
                            else:
                                act(OS[:, g * 4:(g + 1) * 4, :], src, AF.Sigmoid, [], [PB[bk], B_os])
                    for i in range(2):
                        for g in range(4):
                            bk = 4 + (i * 4 + g) % 2
                            for j in range(3):
                                mm(ps[:, bk, :], dg[:, i * 3 + j, :], pre[:, i, g * 512 + j:g * 512 + j + 512], j == 0, j == 2,
                                   [B_dg, B_pre[i]], [PB[bk]])
                            act(qkT[:, i, g * 512:(g + 1) * 512], ps[:, bk, :], AF.Silu, [B_const], [PB[bk], B_qk[i]],
                                bias=mconv_fm[:, 3, i * 4 + h:i * 4 + h + 1])
                    for c8 in range(2):
                        for c in range(c8 * 8, c8 * 8 + 8):
                            tr(psb[:, (c % 8) * 128:(c % 8 + 1) * 128], qkT[:, 1, c * 128:(c + 1) * 128], ident_b[:], [B_qk[1], B_const], [PBB])
                        for c in range(c8 * 8, c8 * 8 + 8):
                            for d_ in range(2):
                                act(KW[:, c, d_, :], psb[:, (c % 8) * 128:(c % 8 + 1) * 128], AF.Identity, [B_g], [PBB, B_kw],
                                    scale=UU[:, 1, c, d_ * 4 + h:d_ * 4 + h + 1])

                def head_S(h):
                    OS = OS_l[h % 2]; RAW = RAW_l[h % 2]; sm2 = sm2_l[h % 2]; lnst = lnst_l[h % 2]; HB = HB_l[h % 2]
                    B_os = B_os_l[h % 2]; B_raw = B_raw_l[h % 2]; B_sm2 = B_sm2_l[h % 2]; B_ln = B_ln_l[h % 2]; B_hb = B_hb_l[h % 2]
                    for c in range(16):
                        sl = slice(c * 128, (c + 1) * 128)
                        pS = c % 2
                        mm(ps[:, pS, 0:128], qkT[:, 1, sl], qkT[:, 0, sl], True, True, [B_qk[0], B_qk[1]], [PB[pS]])
                        for d_ in range(2):
                            stt(AT[:, d_, c, :], ps[:, pS, 0:128], UU[:, 0, c, d_ * 4 + h:d_ * 4 + h + 1], (maskF if d_ == 0 else maskB)[:], ALU.mult, ALU.mult,
                                [B_g, B_const], [PB[pS], B_at[d_]])
                    for d_ in range(2):
                        memset("pool", CM[:, d_, 1, :], 0.0, [B_cm[d_][1]])
                    for ci in range(16):
                        for d_ in range(2):
                            c = ci if d_ == 0 else 15 - ci
                            dh = d_ * 4 + h
                            sl = slice(c * 128, (c + 1) * 128)
                            pO = (2 * ci + d_) % 2
                            pK = 2 + 2 * d_ + ci % 2
                            par = ci % 2
                            if ci < 15:
                                mm(ps[:, pK, 0:129], KW[:, c, d_, :], VA[:, c, :], True, True, [B_kw, B_va], [PB[pK]])
                            mm(ps[:, pO, 0:129], AT[:, d_, c, :], VA[:, c, :], True, ci == 0, [B_at[d_], B_va], [PB[pO]])
                            if ci > 0:
                                mm(ps[:, pO, 0:129], qkT[:, 0, sl], CB[:, d_, par ^ 1, :], False, True, [B_qk[0], B_cb[d_][par ^ 1]], [PB[pO]])
                            cp("act", RAW[:, d_, c, :], ps[:, pO, 0:129], [], [PB[pO], B_raw[d_]])
                            if ci < 15:
                                stt(CM[:, d_, par, :], CM[:, d_, par ^ 1, :], UU[:, 3, c, dh:dh + 1], ps[:, pK, 0:129], ALU.mult, ALU.add,
                                    [B_g, B_cm[d_][par ^ 1]], [PB[pK], B_cm[d_][par]])
                                cp("act", CB[:, d_, par, :], CM[:, d_, par, :], [B_cm[d_][par]], [B_cb[d_][par]])


                def head_T(h):
                    OS = OS_l[h % 2]; RAW = RAW_l[h % 2]; sm2 = sm2_l[h % 2]; lnst = lnst_l[h % 2]; HB = HB_l[h % 2]
                    B_os = B_os_l[h % 2]; B_raw = B_raw_l[h % 2]; B_sm2 = B_sm2_l[h % 2]; B_ln = B_ln_l[h % 2]; B_hb = B_hb_l[h % 2]
                    den = RAW[:, :, :, 128:129]
                    ENv = UU[:, 2, :, :].rearrange("p c (d k) -> p d c k", d=2)[:, :, :, h:h + 1]
                    tt("dve", sm2[:, 0, :, :, :], den, ENv, ALU.max, B_raw + [B_g], [B_sm2])
                    stt(sm2[:, 1, :, :, :], den, -1.0, sm2[:, 0, :, :, :], ALU.mult, ALU.max, B_raw + [B_sm2], [B_sm2])
                    recip(sm2[:, 0, :, :, :], sm2[:, 1, :, :, :], [B_sm2], [B_sm2])
                    for d_ in range(2):
                        tt("dve", RAW[:, d_, :, 0:128], RAW[:, d_, :, 0:128], sm2[:, 0, d_, :, :].broadcast_to([128, 16, 128]), ALU.mult,
                           [B_raw[d_], B_sm2], [B_raw[d_]])
                    H = RAW[:, 0, :, 0:128]
                    tt("dve", H, H, RAW[:, 1, :, 0:128], ALU.add, B_raw, [B_raw[0]])
                    B_H = [B_raw[0]] * 16

                    for c in range(16):
                        S.op("dve", lambda e, c=c: e.bn_stats(out=lnst[:, c, 0:6], in_=H[:, c, :]), reads=[B_H[c]], writes=[B_ln])
                    for c in range(16):
                        S.op("dve", lambda e, c=c: e.bn_aggr(out=lnst[:, c, 6:8], in_=lnst[:, c, 0:6]), reads=[B_ln], writes=[B_ln])
                    ts("dve", lnst[:, :, 7:8], lnst[:, :, 7:8], EPS, None, ALU.add, None, [B_ln], [B_ln])
                    tt("pool", lnst[:, :, 7:8], lnst[:, :, 7:8], mhalf[:, 0:16].unsqueeze(2), ALU.pow, [B_ln, B_const], [B_ln])
                    for c in range(16):
                        ts("dve", H[:, c, :], H[:, c, :], lnst[:, c, 6:7], lnst[:, c, 7:8], ALU.subtract, ALU.mult, [B_ln, B_H[c]], [B_H[c]])
                    tt("dve", H, H, mgB[:, h * 128:(h + 1) * 128].unsqueeze(1).broadcast_to([128, 16, 128]), ALU.mult, [B_raw[0], B_g], [B_raw[0]])
                    tt("dve", HB[:], H, OS[:], ALU.mult, [B_raw[0], B_os], [B_hb])
                    for c in range(16):
                        tr(psb[:, (c % 8) * 128:(c % 8 + 1) * 128], HB[:, c, :], ident_b[:], [B_hb, B_const], [PBB])
                        if c % 8 == 7:
                            cp("act", hmT[:, h, (c - 7) * 128:(c + 1) * 128], psb[:], [], [PBB, B_hm])

                for h in range(5):
                    if h < 4:
                        head_P(h)
                    if h >= 1:
                        head_T(h - 1)
                    if h < 4:
                        head_S(h)
                S.emit()

            if True:
                for hk in range(2):
                    sb.set([(H_END, TOP)])
                    QT = sb("c_QT", [128, 4, T], BF16); KT = sb("c_KT", [128, T], BF16); VA = sb("c_VA", [128, 16, 129], BF16)
                    mark_c = [list(s_) for s_ in sb.segs]
                    rope = sb("c_rope", [128, 2, 16, 128], F32); gqk = sb("c_g", [128, 5, 128], F32)
                    wq = sb("c_wq", [128, 2, 8, 256], BF16); wkv = sb("c_wkv", [128, 2, 8, 128], BF16)
                    XQ = sb("c_XQ", [128, 2, 5, 128], F32); R1 = sb("c_R1", [128, 2, 5, 128], F32); R2 = sb("c_R2", [128, 2, 5, 128], F32)
                    QR = sb("c_QR", [128, 2, 5, 128], BF16); ss = sb("c_ss", [128, 2, 8], F32)
                    B_rope = [Buf() for _ in range(7)]
                    if True:
                        B_wq = Buf(); B_wkv = Buf(); B_QT = Buf(); B_KT = Buf(); B_VA = Buf()
                        B_XQ = [Buf(), Buf()]; B_R1 = [Buf(), Buf()]; B_R2 = [Buf(), Buf()]; B_QR = [Buf(), Buf()]; B_ss = [Buf(), Buf()]
                        B_PT = [Buf(), Buf()]; B_HA = [Buf(), Buf()]; B_rr = [Buf(), Buf()]
                        load_w(wq[:, 0, :, :], win_cols(O_QA + hk * 512, 256), 8, 256, B_wq, ("cq", hk, 0))
                        load_w(wq[:, 1, :, :], win_cols(O_QA + hk * 512 + 256, 256), 8, 256, B_wq, ("cq", hk, 1))
                        load_w(wkv[:, 0, :, :], win_cols(O_KA + hk * 128, 128), 8, 128, B_wkv, ("ck", hk))
                        load_w(wkv[:, 1, :, :], win_cols(O_VA + hk * 128, 128), 8, 128, B_wkv, ("cv", hk))
                        for i_ in range(5):
                            mdma(gqk[:, i_, :], (qg_d if i_ < 4 else kg_d).partition_broadcast(128), [], [B_rope[i_]])
                        mdma(rope[:, 0, :, :], c_ropeC.rearrange("(t p) d -> p t d", p=128), [], [B_rope[5]])
                        mdma(rope[:, 1, :, :], c_ropeS.rearrange("(t p) d -> p t d", p=128), [], [B_rope[6]])
                        memset("pool", VA[:, :, 128:129], 1.0, [B_VA])
                        def c1_A(t_):
                            s = t_ % 2
                            bq = 0 + s
                            bkv = 2 + s
                            tsl = slice(t_ * 128, (t_ + 1) * 128)
                            for hf in range(2):
                                for kc in range(8):
                                    mm(ps[:, bq, hf * 256:(hf + 1) * 256], xx[:, kc, tsl], wq[:, hf, kc, :], kc == 0, kc == 7,
                                       [B_xx[t_ // 4], B_wq], [PB[bq]])
                            for i in range(2):
                                for kc in range(8):
                                    mm(ps[:, bkv, i * 128:(i + 1) * 128], xx[:, kc, tsl], wkv[:, i, kc, :], kc == 0, kc == 7,
                                       [B_xx[t_ // 4], B_wkv], [PB[bkv]])
                            cp("act", XQ[:, s, 0:4, :], ps[:, bq, :].rearrange("p (a b) -> p a b", b=128), [], [PB[bq], B_XQ[s]])
                            cp("act", XQ[:, s, 4, :], ps[:, bkv, 0:128], [], [PB[bkv], B_XQ[s]])
                            cp("act", VA[:, t_, 0:128], ps[:, bkv, 128:256], [], [PB[bkv], B_VA])
                            for i in range(5):
                                src_ = ps[:, bq, i * 128:(i + 1) * 128] if i < 4 else ps[:, bkv, 0:128]
                                act(R2[:, s, i, :], src_, AF.Square, [], [PB[bq] if i < 4 else PB[bkv], B_R2[s], B_ss[s]], accum=ss[:, s, i:i + 1])

                        def c1_B(t_):
                            s = t_ % 2
                            tt("dve", XQ[:, s, :, :], XQ[:, s, :, :], gqk[:], ALU.mult, [B_XQ[s]] + B_rope, [B_XQ[s]])
                            tt("dve", R1[:, s, :, :], XQ[:, s, :, :], rope[:, 0, t_, :].unsqueeze(1).broadcast_to([128, 5, 128]), ALU.mult,
                               [B_XQ[s]] + B_rope, [B_R1[s]])
                            xv = XQ[:, s, :, :].rearrange("p h (a s d) -> p h a s d", a=2, s=2)
                            rv = R2[:, s, :, :].rearrange("p h (a s d) -> p h a s d", a=2, s=2)
                            sv = rope[:, 1, t_, :].rearrange("p (a s d) -> p a s d", a=2, s=2)
                            for hf in range(2):
                                tt("dve", rv[:, :, :, hf, :], xv[:, :, :, 1 - hf, :], sv[:, :, hf, :].unsqueeze(1).broadcast_to([128, 5, 2, 32]),
                                   ALU.mult, [B_XQ[s]] + B_rope, [B_R2[s]])
                            ts("dve", ss[:, s, 0:5], ss[:, s, 0:5], 1.0 / 128.0, EPS, ALU.mult, ALU.add, [B_ss[s]], [B_ss[s]])
                            tt("pool", ss[:, s, 0:5], ss[:, s, 0:5], mhalf[:, 0:5], ALU.pow, [B_ss[s], B_const], [B_ss[s]])
                            tt("dve", R1[:, s, :, :], R1[:, s, :, :], R2[:, s, :, :], ALU.add, [B_R1[s], B_R2[s]], [B_R1[s]])
                            tt("dve", QR[:, s, :, :], R1[:, s, :, :], ss[:, s, 0:5].unsqueeze(2).broadcast_to([128, 5, 128]), ALU.mult,
                               [B_R1[s], B_ss[s]], [B_QR[s]])

                        def c1_C(t_):
                            s = t_ % 2
                            tsl = slice(t_ * 128, (t_ + 1) * 128)
                            for i in range(5):
                                tr(psb[:, i * 128:(i + 1) * 128], QR[:, s, i, :], ident_b[:], [B_QR[s], B_const], [PBB])
                            cp("act", QT[:, :, tsl], psb[:, 0:512].rearrange("p (a b) -> p a b", b=128), [], [PBB, B_QT])
                            cp("act", KT[:, tsl], psb[:, 512:640], [], [PBB, B_KT])

                        pipeline(NT, [c1_A, c1_B, c1_C], oldest_first=True)
                        S.emit()
                        sb.set(mark_c)
                        PT = sb("c_PT", [128, 2, 16, 512], BF16); HA = sb("c_ha", [128, 2, 4, 128], BF16); rr = sb("c_rr", [128, 2, 4], F32)

                        def c2_SP(it):
                            qb = it
                            s = qb % 2
                            qsl = slice((qb % NT) * 128, (qb % NT + 1) * 128)
                            pq = it - 1
                            sp_ = pq % 2
                            for kt in range(NT):
                                if qb < NT:
                                    bA = 2 + 2 * ((kt // 2) % 2)
                                    bk = bA + kt % 2
                                    mm(ps[:, bk, :], KT[:, kt * 128:(kt + 1) * 128], QT[:, :, qsl], True, True, [B_KT, B_QT], [PB[bk]])
                                    if kt % 2 == 1:
                                        act(PT[:, s, kt - 1:kt + 1, :], ps[:, bA:bA + 2, :], AF.Exp, [B_const], [PB[bA], PB[bA + 1], B_PT[s]],
                                            bias=negc[:], scale=1.0 / math.sqrt(128.0))
                                if pq >= 0:
                                    for g in range(4):
                                        ob = g // 2
                                        oc = (g % 2) * 256
                                        mm(ps[:, ob, oc:oc + 129], PT[:, sp_, kt, g * 128:(g + 1) * 128], VA[:, kt, :], kt == 0 and g % 2 == 0, kt == NT - 1,
                                           [B_PT[sp_], B_VA], [PB[ob]], sgc=True)
                            if pq >= 0:
                                for g in range(4):
                                    ob = g // 2
                                    oc = (g % 2) * 256
                                    recip(rr[:, sp_, g:g + 1], ps[:, ob, oc + 128:oc + 129], [], [PB[ob], B_rr[sp_]])
                                    ts("dve", HA[:, sp_, g, :], ps[:, ob, oc:oc + 128], rr[:, sp_, g:g + 1], None, ALU.mult, None, [B_rr[sp_]], [PB[ob], B_HA[sp_]])

                        def c2_T(qb):
                            s = qb % 2
                            qsl = slice(qb * 128, (qb + 1) * 128)
                            for g in range(4):
                                tr(psb[:, g * 128:(g + 1) * 128], HA[:, s, g, :], ident_b[:], [B_HA[s], B_const], [PBB])
                            cp("dve", haT[:, hk * 4:(hk + 1) * 4, qsl], psb[:, 0:512].rearrange("p (a b) -> p a b", b=128), [], [PBB, B_ha])

                        for it in range(NT + 2):
                            if it <= NT:
                                c2_SP(it)
                            if it - 2 >= 0:
                                c2_T(it - 2)

                        S.emit()


            sb.set([(H_END, TOP)])
            memt = sb("d_mem", [128, 2, D], F32)
            memT = sb("d_memT", [128, 8, 256], BF16)
            wkv = sb("d_wkv", [128, 8, 1024], BF16)
            wqc = sb("d_wq", [128, 8, 512], BF16)
            kcT = sb("d_kT", [128, 4, 256], BF16)
            vc = sb("d_v", [128, 2, 512], BF16)
            qcT = sb("d_qT", [128, 4, T], BF16)
            E = sb("d_E", [128, 2, 4, 256], F32)
            Pn = sb("d_P", [128, 2, 4, 256], BF16)
            PcT = sb("d_PT", [128, 2, 8, 128], BF16)
            dsm = sb("d_s", [128, 2, 8], F32)
            dsm2 = sb("d_s2", [128, 2, 8], F32)
            if True:
                B_mem = Buf(); B_memT = Buf(); B_wkv = Buf(); B_wq = Buf(); B_k = Buf(); B_v = Buf(); B_q = Buf()
                B_E = [Buf(), Buf()]; B_P = [Buf(), Buf()]; B_PcT = [Buf(), Buf()]; B_s = [Buf(), Buf()]; B_s2 = [Buf(), Buf()]
                mdma(memt[:], mem_d[seq].rearrange("(a p) d -> p a d", p=128), [], [B_mem])
                for q4 in range(4):
                    load_w(wkv[:, 2 * q4:2 * q4 + 2, :], w_mkv[q4 * 256:(q4 + 1) * 256, :].rearrange("(kc p) n -> p kc n", p=128), 2, 1024, B_wkv, ("dkv", q4))
                for q4 in range(2):
                    load_w(wqc[:, 4 * q4:4 * q4 + 4, :], w_in[q4 * 512:(q4 + 1) * 512, O_QC:O_QC + 512].rearrange("(kc p) n -> p kc n", p=128), 4, 512, B_wq, ("dq", q4))
                for a in range(2):
                    for kc in range(8):
                        bk = (a * 8 + kc) // 4
                        tr(ps[:, bk, (kc % 4) * 128:(kc % 4 + 1) * 128], memt[:, a, kc * 128:(kc + 1) * 128], ident_f[:], [B_mem, B_const], [PB[bk]])
                    for hh in range(2):
                        cp("dve", memT[:, hh * 4:(hh + 1) * 4, a * 128:(a + 1) * 128],
                           ps[:, a * 2 + hh, :].rearrange("p (a b) -> p a b", b=128), [], [PB[a * 2 + hh], B_memT])
                for hh in range(4):
                    bk = 4 + hh % 2
                    for kc in range(8):
                        mm(ps[:, bk, 0:256], wkv[:, kc, hh * 128:(hh + 1) * 128], memT[:, kc, :], kc == 0, kc == 7, [B_wkv, B_memT], [PB[bk]])
                    cp("act", kcT[:, hh, :], ps[:, bk, 0:256], [], [PB[bk], B_k])
                for a in range(2):
                    bk = 4 + a
                    for kc in range(8):
                        mm(ps[:, bk, :], memT[:, kc, a * 128:(a + 1) * 128], wkv[:, kc, 512:1024], kc == 0, kc == 7, [B_wkv, B_memT], [PB[bk]])
                    cp("act", vc[:, a, :], ps[:, bk, :], [], [PB[bk], B_v])
                for hh in range(4):
                    for g in range(4):
                        bk = (hh * 4 + g) % 4
                        for kc in range(8):
                            mm(ps[:, bk, :], wqc[:, kc, hh * 128:(hh + 1) * 128], xx[:, kc, g * 512:(g + 1) * 512], kc == 0, kc == 7,
                               [B_wq, B_xx[g]], [PB[bk]])
                        cp("act" if g % 2 else "dve", qcT[:, hh, g * 512:(g + 1) * 512], ps[:, bk, :], [], [PB[bk], B_q])
                sc = 1.0 / math.sqrt(128.0)

                def d_1(t_):
                    s = t_ % 2
                    tsl = slice(t_ * 128, (t_ + 1) * 128)
                    for hh in range(4):
                        bS = 2 * s + hh // 2
                        mm(ps[:, bS, (hh % 2) * 256:(hh % 2 + 1) * 256], qcT[:, hh, tsl], kcT[:, hh, :], True, True, [B_q, B_k], [PB[bS]])
                    S4 = ps[:, 2 * s:2 * s + 2, :].rearrange("p a (b m) -> p (a b) m", m=256)
                    S.op("dve", lambda e: e.tensor_reduce(out=dsm[:, s, 0:4], in_=S4, axis=AX.X, op=ALU.max),
                         reads=[], writes=[PB[2 * s], PB[2 * s + 1], B_s[s]])
                    ts("dve", dsm[:, s, 4:8], dsm[:, s, 0:4], -sc, None, ALU.mult, None, [B_s[s]], [B_s[s]])
                    for hh in range(4):
                        bS = 2 * s + hh // 2
                        act(E[:, s, hh, :], ps[:, bS, (hh % 2) * 256:(hh % 2 + 1) * 256], AF.Exp, [B_s[s]], [PB[bS], B_E[s], B_s2[s]],
                            bias=dsm[:, s, 4 + hh:5 + hh], scale=sc, accum=dsm2[:, s, hh:hh + 1])

                def d_2(t_):
                    s = t_ % 2
                    recip(dsm2[:, s, 4:8], dsm2[:, s, 0:4], [B_s2[s]], [B_s2[s]])
                    tt("dve", Pn[:, s, :, :], E[:, s, :, :], dsm2[:, s, 4:8].unsqueeze(2).broadcast_to([128, 4, 256]), ALU.mult, [B_E[s], B_s2[s]], [B_P[s]])
                    for hh in range(4):
                        for a in range(2):
                            tr(psb[:, (hh * 2 + a) * 128:(hh * 2 + a + 1) * 128], Pn[:, s, hh, a * 128:(a + 1) * 128], ident_b[:], [B_P[s], B_const], [PBB])

                def d_3(t_):
                    s = t_ % 2
                    tsl = slice(t_ * 128, (t_ + 1) * 128)
                    cp("act", PcT[:, s, :, :], psb[:].rearrange("p (a b) -> p a b", b=128), [], [PBB, B_PcT[s]])
                    bO = 4 + s
                    for hh in range(4):
                        for a in range(2):
                            mm(ps[:, bO, hh * 128:(hh + 1) * 128], vc[:, a, hh * 128:(hh + 1) * 128], PcT[:, s, hh * 2 + a, :], a == 0, a == 1,
                               [B_v, B_PcT[s]], [PB[bO]])

                def d_4(t_):
                    s = t_ % 2
                    tsl = slice(t_ * 128, (t_ + 1) * 128)
                    bO = 4 + s
                    cp("act", hcT[:, :, tsl], ps[:, bO, :].rearrange("p (a b) -> p a b", b=128), [], [PB[bO], B_hc])

                pipeline(NT, [d_1, d_2, d_3, d_4], order=[3, 2, 0, 1])
                S.emit()


            if dbg and seq == 0:
                cd = S.dma_ctr("dbg")
                dma(dbg_h[:, 0:4, :], hmT[:], [B_hm], [], cd)
                dma(dbg_h[:, 4:12, :], haT[:], [B_ha], [], cd)
                dma(dbg_h[:, 12:16, :], hcT[:], [B_hc], [], cd)
                S.emit()

            if True:
                B_mT = [Buf() for _ in range(4)]
                sb.set([(M_END, TOP)])
                wg = sb("e_wg", [128, 2, 3, 8, 128], BF16)
                wbr = sb("e_wb", [128, 2, 16, 128], BF16)
                SG = sb("e_sg", [128, 2, 3, 512], F32)
                Mt = sb("e_M", [128, 2, 2, 512], F32)
                if True:
                    B_wg = [Buf(), Buf()]; B_wb = [Buf(), Buf()]; B_sg = [Buf(), Buf()]; B_M = [Buf(), Buf()]

                    def load_e(cc):
                        s = cc % 2
                        for k in range(3):
                            load_w(wg[:, s, k, :, :], win_cols(O_GB + k * 1024 + cc * 128, 128), 8, 128, B_wg[s], ("eg", cc, k))
                        load_w(wbr[:, s, 0:4, :], w_bm[:, cc * 128:(cc + 1) * 128].rearrange("(kc p) n -> p kc n", p=128), 4, 128, B_wb[s], ("ebm", cc))
                        load_w(wbr[:, s, 4:12, :], w_ba[:, cc * 128:(cc + 1) * 128].rearrange("(kc p) n -> p kc n", p=128), 8, 128, B_wb[s], ("eba", cc))
                        load_w(wbr[:, s, 12:16, :], w_bc[:, cc * 128:(cc + 1) * 128].rearrange("(kc p) n -> p kc n", p=128), 4, 128, B_wb[s], ("ebc", cc))

                    load_e(0)
                    for cc in range(8):
                        s = cc % 2
                        if cc + 1 < 8:
                            load_e(cc + 1)
                        for g in range(4):
                            u = (cc * 4 + g) % 2
                            gsl = slice(g * 512, (g + 1) * 512)
                            for k in range(3):
                                for kc in range(8):
                                    mm(ps[:, k, :], wg[:, s, k, kc, :], xx[:, kc, gsl], kc == 0, kc == 7, [B_wg[s], B_xx[g]], [PB[k]])
                                act(SG[:, u, k, :], ps[:, k, :], AF.Sigmoid, [], [PB[k], B_sg[u]])
                            for kc in range(4):
                                mm(ps[:, 3, :], wbr[:, s, kc, :], hmT[:, kc, gsl], kc == 0, kc == 3, [B_wb[s], B_hm], [PB[3]])
                            for kc in range(8):
                                mm(ps[:, 4, :], wbr[:, s, 4 + kc, :], haT[:, kc, gsl], kc == 0, kc == 7, [B_wb[s], B_ha], [PB[4]])
                            for kc in range(4):
                                mm(ps[:, 5, :], wbr[:, s, 12 + kc, :], hcT[:, kc, gsl], kc == 0, kc == 3, [B_wb[s], B_hc], [PB[5]])
                            tt("dve", Mt[:, u, 0, :], ps[:, 3, :], SG[:, u, 0, :], ALU.mult, [B_sg[u]], [PB[3], B_M[u]])
                            tt("dve", Mt[:, u, 1, :], ps[:, 4, :], SG[:, u, 1, :], ALU.mult, [B_sg[u]], [PB[4], B_M[u]])
                            tt("dve", Mt[:, u, 0, :], Mt[:, u, 0, :], Mt[:, u, 1, :], ALU.add, [B_M[u]], [B_M[u]])
                            tt("dve", Mt[:, u, 1, :], ps[:, 5, :], SG[:, u, 2, :], ALU.mult, [B_sg[u]], [PB[5], B_M[u]])
                            tt("dve", mT[:, cc, gsl], Mt[:, u, 0, :], Mt[:, u, 1, :], ALU.add, [B_M[u]], [B_mT[g]])
                    S.emit()
                sb.set([(H0, H_END), (M_END, TOP)])
                wo = sb("e_wo", [128, 8, D], BF16)
                xt = sb("e_xt", [128, 2, D], F32)
                Z = sb("e_z", [128, 2, D], F32)
                ZH = sb("e_zh", [128, 2, D], F32)
                X1 = sb("e_x1", [128, 2, D], F32)
                est = sb("e_st", [128, 2, 16], F32)
                if True:
                    B_wo = Buf(); B_gb = Buf(); B_xt = [Buf(), Buf()]; B_z = [Buf(), Buf()]; B_zh = [Buf(), Buf()]; B_x1 = [Buf(), Buf()]
                    B_st = [Buf(), Buf()]
                    C_xt = slot_ctrs("ext")
                    C_x1 = slot_ctrs("ex1")
                    B_x1s = [Buf() for _ in range(NT)]
                    for q4 in range(4):
                        load_w(wo[:, 2 * q4:2 * q4 + 2, :], w_out[q4 * 256:(q4 + 1) * 256, :].rearrange("(kc p) n -> p kc n", p=128), 2, 1024, B_wo, ("wo", q4))
                    gbt = gbt_in
                    B_gb = B_const
                    def e2_a(t_):
                        s = t_ % 2
                        tsl = slice(t_ * 128, (t_ + 1) * 128)
                        dma(xt[:, s, :], xs[tsl, :], [], [B_xt[s]], C_xt[s])
                        b0 = 2 * s
                        for hf in range(2):
                            for kc in range(8):
                                mm(ps[:, b0 + hf, :], mT[:, kc, tsl], wo[:, kc, hf * 512:(hf + 1) * 512], kc == 0, False,
                                   [B_mT[t_ // 4], B_wo], [PB[b0 + hf]])
                            mm(ps[:, b0 + hf, :], ones_f[0:1, :], gbt[0:1, 1, hf * 512:(hf + 1) * 512], False, True, [B_const, B_gb], [PB[b0 + hf]])
                        act(xt[:, s, :], xt[:, s, :], AF.Identity, [B_xt[s], B_stats_in], [B_xt[s]], bias=stats_in[:, t_, 1:2], scale=stats_in[:, t_, 0:1])
                        tt("dve", xt[:, s, :], xt[:, s, :], gbt[:, 0, :], ALU.mult, [B_xt[s], B_gb], [B_xt[s]])

                    def e2_a2(t_):
                        s = t_ % 2
                        b0 = 2 * s
                        for hf in range(2):
                            tt("dve", Z[:, s, hf * 512:(hf + 1) * 512], ps[:, b0 + hf, :], xt[:, s, hf * 512:(hf + 1) * 512], ALU.add,
                               [B_xt[s]], [PB[b0 + hf], B_z[s]])
                        st = est[:, s, 0:12].rearrange("p (a b) -> p a b", b=6)
                        for i in range(2):
                            S.op("dve", lambda e, i=i: e.bn_stats(out=st[:, i, :], in_=Z[:, s, i * 512:(i + 1) * 512]), reads=[B_z[s]], writes=[B_st[s]])
                        S.op("dve", lambda e: e.bn_aggr(out=est[:, s, 12:14], in_=st), reads=[B_st[s]], writes=[B_st[s]])
                        ts("dve", est[:, s, 14:15], est[:, s, 13:14], EPS, None, ALU.add, None, [B_st[s]], [B_st[s]])
                        tt("pool", est[:, s, 14:15], est[:, s, 14:15], mhalf[:, 0:1], ALU.pow, [B_st[s], B_const], [B_st[s]])

                    def e2_b(t_):
                        s = t_ % 2
                        stt(est[:, s, 15:16], est[:, s, 12:13], -1.0, est[:, s, 14:15], ALU.mult, ALU.mult, [B_st[s]], [B_st[s]])
                        act(ZH[:, s, :], Z[:, s, :], AF.Identity, [B_z[s], B_st[s]], [B_zh[s]], bias=est[:, s, 15:16], scale=est[:, s, 14:15])

                    def e2_c(t_):
                        s = t_ % 2
                        tsl = slice(t_ * 128, (t_ + 1) * 128)
                        for kc in range(8):
                            bk = 4 + (kc // 4)
                            tr(ps[:, bk, (kc % 4) * 128:(kc % 4 + 1) * 128], ZH[:, s, kc * 128:(kc + 1) * 128], ident_f[:], [B_zh[s], B_const], [PB[bk]])
                        for kc in range(8):
                            bk = 4 + (kc // 4)
                            act(xx[:, kc, tsl], ps[:, bk, (kc % 4) * 128:(kc % 4 + 1) * 128], AF.Identity, [B_const], [PB[bk], B_xx[t_ // 4]],
                                bias=ln1_fm[:, 1, kc:kc + 1], scale=ln1_fm[:, 0, kc:kc + 1])
                        dma(x1s_d[tsl, :], ZH[:, s, :], [B_zh[s]], [B_x1s[t_]], C_x1[s])

                    pipeline(NT, [e2_a, e2_a2, e2_b, e2_c], oldest_first=True)
                    S.emit()


        sb.set([(H0, TOP)])
        wd = sb("f_wd", [128, 22, D], BF16)
        hT = sb("f_hT", [128, 22, 1024], BF16)
        wu = sb("f_wu", [128, 2, 2, 8, 128], BF16)
        Ab = sb("f_A", [128, 1024], BF16)
        halo = sb("f_halo", [128, 2, 4], F32)
        B_halo = [Buf(), Buf()]
        FPAIR = [0, 2, 5]
        gbt = sb("f_gb", [128, 4, D], F32)
        X1 = sb("f_x1", [128, 2, D], F32)
        Z = sb("f_z", [128, 2, D], F32)
        Y = Z
        fst = sb("f_st", [128, 2, 16], F32)
        if True:
            B_wd = Buf(); B_hT = Buf(); B_wu = [Buf(), Buf()]; B_pre = [Buf(), Buf()]; B_dg = [Buf(), Buf()]; B_A = [Buf(), Buf()]
            B_gb = Buf(); B_x1 = [Buf(), Buf()]; B_z = [Buf(), Buf()]; B_y = B_z; B_st = [Buf(), Buf()]
            C_x1 = slot_ctrs("fx1")
            C_y = slot_ctrs("fy")
            def load_u(j):
                s = j % 2
                load_w(wu[:, s, 0, :, :], w_up[:, j * 128:(j + 1) * 128].rearrange("(kc p) n -> p kc n", p=128), 8, 128, B_wu[s], ("wu", j, 0))
                load_w(wu[:, s, 1, :, :], w_up[:, DFF + j * 128:DFF + (j + 1) * 128].rearrange("(kc p) n -> p kc n", p=128), 8, 128, B_wu[s], ("wu", j, 1))

            load_u(0)
            B_gbl = [Buf() for _ in range(4)]
            for i, v_ in enumerate([ln2_g, ln2_b, ln1_g, ln1_b]):
                mdma(gbt[:, i, :], v_.partition_broadcast(128), [], [B_gbl[i]])
            ts("dve", gbt[:, 2:4, :], gbt[:, 2:4, :], ALPHA, None, ALU.mult, None, B_gbl, [B_gb])

            for half in range(2):
                h0 = half * 1024

                def f_U(j, half=half, h0=h0):
                    s = j % 2
                    if j + 1 < 22:
                        load_u(j + 1)
                    elif half == 0:
                        load_u(0)
                    if half == 0 and j % 2 == 0:
                        load_w(wd[:, j:j + 2, :], w_dn[j * 128:(j + 2) * 128, :].rearrange("(a p) n -> p a n", p=128), 2, 1024, B_wd, ("wd", j))
                    htok = 1024 if half == 0 else 1023
                    for part in range(2):
                        pb = FPAIR[(2 * j + part) % 3]
                        for g in range(2):
                            bk = pb + g
                            for kc in range(8):
                                mm(ps[:, bk, :], wu[:, s, part, kc, :], xx[:, kc, h0 + g * 512:h0 + (g + 1) * 512], kc == 0, kc == 7,
                                   [B_wu[s], B_xx[half * 2 + g]], [PB[bk]])
                        for kc in range(8):
                            mm(ps[:, 4, 2 * part:2 * part + 2], wu[:, s, part, kc, :], xx[:, kc, htok:htok + 2] if half == 0 else xx[:, kc, htok - 1:htok + 1],
                               kc == 0, kc == 7, [B_wu[s], B_xx[htok // 512], B_xx[(htok - 1) // 512]], [PB[4]])
                    cp("act", halo[:, s, :], ps[:, 4, 0:4], [], [PB[4], B_halo[s]])

                def f_V(j, half=half):
                    for part in range(2):
                        ch = part * 22 + j
                        w0 = fconv_fm[:, 0, ch:ch + 1]; w1 = fconv_fm[:, 1, ch:ch + 1]; w2 = fconv_fm[:, 2, ch:ch + 1]; bb = fconv_fm[:, 3, ch:ch + 1]
                        pb = FPAIR[(2 * j + part) % 3]
                        s = j % 2
                        v = ps[:, pb:pb + 2, :].rearrange("p a b -> p (a b)")
                        PBv = [PB[pb], PB[pb + 1]]
                        t0 = X1[:, part, :]; t1 = Z[:, part, :]
                        Bt0 = B_x1[part]; Bt1 = B_z[part]
                        hv = halo[:, s, 2 * part:2 * part + 1] if half == 0 else halo[:, s, 2 * part + 1:2 * part + 2]
                        act(t0, v, AF.Identity, [B_const], PBv + [Bt0], bias=bb, scale=w1)
                        stt(t1[:, 1:1024], v[:, 0:1023], w0, t0[:, 1:1024], ALU.mult, ALU.add, [B_const, Bt0], PBv + [Bt1])
                        if half == 0:
                            cp("dve", t1[:, 0:1], t0[:, 0:1], [Bt0], [Bt1])
                        else:
                            stt(t1[:, 0:1], hv, w0, t0[:, 0:1], ALU.mult, ALU.add, [B_const, Bt0, B_halo[s]], [Bt1])
                        stt(t0[:, 0:1023], v[:, 1:1024], w2, t1[:, 0:1023], ALU.mult, ALU.add, [B_const, Bt1], PBv + [Bt0])
                        if half == 0:
                            stt(t0[:, 1023:1024], hv, w2, t1[:, 1023:1024], ALU.mult, ALU.add, [B_const, Bt1, B_halo[s]], [Bt0])
                        else:
                            cp("dve", t0[:, 1023:1024], t1[:, 1023:1024], [Bt1], [Bt0])
                        if part == 0:
                            act(Ab[:], t0, AF.Gelu_apprx_tanh, [Bt0], [B_A[0]])
                        else:
                            tt("dve", hT[:, j, :], t0, Ab[:], ALU.mult, [Bt0, B_A[0]], [B_hT])

                for j in range(22):
                    f_U(j)
                    f_V(j)


                def f_a(t8, half=half):
                    t_ = half * 8 + t8
                    s = t_ % 2
                    tsl = slice(t_ * 128, (t_ + 1) * 128)
                    lsl = slice(t8 * 128, (t8 + 1) * 128)
                    dma(X1[:, s, :], x1s_d[tsl, :], [B_x1s[t_]], [B_x1[s]], C_x1[s])
                    b0 = 3 + 2 * s
                    for hf in range(2):
                        for j in range(22):
                            mm(ps[:, b0 + hf, :], hT[:, j, lsl], wd[:, j, hf * 512:(hf + 1) * 512], j == 0, False, [B_hT, B_wd], [PB[b0 + hf]])
                        mm(ps[:, b0 + hf, :], ones_f[0:1, :], gbt[0:1, 3, hf * 512:(hf + 1) * 512], False, True, [B_const, B_gb], [PB[b0 + hf]])
                    tt("dve", X1[:, s, :], X1[:, s, :], gbt[:, 2, :], ALU.mult, [B_x1[s], B_gb], [B_x1[s]])
                    for hf in range(2):
                        tt("dve", Z[:, s, hf * 512:(hf + 1) * 512], ps[:, b0 + hf, :], X1[:, s, hf * 512:(hf + 1) * 512], ALU.add,
                           [B_x1[s]], [PB[b0 + hf], B_z[s]])

                def f_b(t8, half=half):
                    t_ = half * 8 + t8
                    s = t_ % 2
                    tsl = slice(t_ * 128, (t_ + 1) * 128)
                    st = fst[:, s, 0:12].rearrange("p (a b) -> p a b", b=6)
                    layer_norm_stats(Z[:, s, :], st, fst[:, s, 12:14], fst[:, s, 14:15], fst[:, s, 15:16], B_z[s], B_st[s])
                    act(Y[:, s, :], Z[:, s, :], AF.Identity, [B_z[s], B_st[s]], [B_z[s]], bias=fst[:, s, 15:16], scale=fst[:, s, 14:15])
                    tt("dve", Y[:, s, :], Y[:, s, :], gbt[:, 0, :], ALU.mult, [B_y[s], B_gb], [B_y[s]])
                    tt("dve", Y[:, s, :], Y[:, s, :], gbt[:, 1, :], ALU.add, [B_y[s], B_gb], [B_y[s]])
                    dma(y_d[seq, tsl, :], Y[:, s, :], [B_y[s]], [], C_y[s])

                pipeline(8, [f_a, f_b], oldest_first=True)

            S.emit(final=(seq == nseq - 1))
    global LAST_MARKS
    LAST_MARKS = S.marks
    return nc


LAST_MARKS = None

def _consts():
    ident = np.eye(128, dtype=np.float32)
    s = np.arange(128)[:, None]
    t = np.arange(128)[None, :]
    maskF = (s <= t).astype(np.float32)
    maskB = (s >= t).astype(np.float32)
    tok = np.arange(T)
    row = (tok // 64).astype(np.float32)
    col = (tok % 64).astype(np.float32)
    inv = (np.float32(10000.0) ** (-np.arange(0, 64, 2, dtype=np.float32) / np.float32(64))).astype(np.float32)
    ar = (row[:, None] * inv[None, :]).astype(np.float32)
    ac = (col[:, None] * inv[None, :]).astype(np.float32)
    C = np.concatenate([np.cos(ar), np.cos(ar), np.cos(ac), np.cos(ac)], axis=1).astype(np.float32)
    Sg = np.concatenate([-np.sin(ar), np.sin(ar), -np.sin(ac), np.sin(ac)], axis=1).astype(np.float32)
    return {"c_ident": ident, "c_maskF": maskF, "c_maskB": maskB, "c_ropeC": C, "c_ropeS": Sg}


def _shared_inputs(inp):
    f = lambda a: np.ascontiguousarray(np.asarray(a, dtype=np.float32))
    d = {
        "w_in": f(inp["w_in"][0]), "w_mkv": f(inp["w_mem_kv"][0]), "w_bm": f(inp["w_branch_mlstm"][0]),
        "w_ba": f(inp["w_branch_attn"][0]), "w_bc": f(inp["w_branch_mem"][0]), "w_out": f(inp["w_out"][0]),
        "w_up": f(inp["w_ffn_up"][0]), "w_dn": f(inp["w_ffn_down"][0]),
        "lnin_g": f(inp["ln_in_g"]), "lnin_b": f(inp["ln_in_b"]),
        "ln1_g": f(inp["ln1_g"][0]), "ln1_b": f(inp["ln1_b"][0]), "ln2_g": f(inp["ln2_g"][0]), "ln2_b": f(inp["ln2_b"][0]),
        "gbias": f(inp["mlstm_gate_bias"][0]), "mconv_w": f(inp["mlstm_conv_w"][0]), "mconv_b": f(inp["mlstm_conv_b"][0]),
        "mnorm_g": f(inp["mlstm_norm_g"][0]), "qg": f(inp["attn_q_norm_g"][0]), "kg": f(inp["attn_k_norm_g"][0]),
        "fconv_w": f(inp["ffn_conv_w"][0]), "fconv_b": f(inp["ffn_conv_b"][0]),
    }
    d.update(_consts())
    return d


def kernel(**inp):
    ncores = 8
    xs = np.concatenate([np.asarray(inp["x_prompt"], np.float32), np.asarray(inp["x_sample"], np.float32)], axis=0)
    ms = np.concatenate([np.asarray(inp["mem_prompt"], np.float32), np.asarray(inp["mem_sample"], np.float32)], axis=0)
    nseq = xs.shape[0] // ncores
    shared = _shared_inputs(inp)
    nc = build(nseq)
    in_maps = []
    for c in range(ncores):
        m = dict(shared)
        m["x"] = np.ascontiguousarray(xs[c * nseq:(c + 1) * nseq])
        m["mem"] = np.ascontiguousarray(ms[c * nseq:(c + 1) * nseq])
        in_maps.append(m)
    res = run_bass_kernel_spmd(nc, in_maps, core_ids=list(range(ncores)))
    y = np.concatenate([np.asarray(r["y"], np.float32) for r in res.results], axis=0)
    nb = inp["x_prompt"].shape[0]
    return (y[:nb], y[nb:])
```

```python
import math
import numpy as np
import concourse.bass as bass
import concourse.mybir as mybir
from concourse.bass_utils import run_bass_kernel_spmd

F32 = mybir.dt.float32
BF16 = mybir.dt.bfloat16
AF = mybir.ActivationFunctionType
ALU = mybir.AluOpType
AX = mybir.AxisListType

T = 2048
D = 1024
NT = 16
DFF = 2816
EPS = 1e-5
ALPHA = 2.0 ** 0.25
O_QM, O_KM, O_VM, O_OM, O_GM, O_QA, O_KA, O_VA, O_QC, O_GB = 0, 512, 1024, 1536, 2048, 2064, 3088, 3344, 3600, 4112


class Ctr:
    __slots__ = ("sem", "step", "val", "name")

    def __init__(self, sem, step, name):
        self.sem, self.step, self.val, self.name = sem, step, 0, name


class Buf:
    __slots__ = ("name", "w", "r")

    def __init__(self, name=""):
        self.name = name
        self.w = None
        self.r = {}


class Sched:
    def __init__(self, nc):
        self.nc = nc
        self.names = ("pe", "act", "dve", "pool", "sp")
        self.q = {k: [] for k in self.names}
        self.ctr = {}
        for k in ("pe", "act", "dve", "pool"):
            self.ctr[k] = Ctr(nc.alloc_semaphore(name="s_" + k), 1, k)
        self.known = {k: {} for k in self.names}
        self.dma_ctrs = []
        self.marks = []

    def dma_ctr(self, name):
        c = Ctr(self.nc.alloc_semaphore(name="d_" + name), 16, name)
        self.dma_ctrs.append(c)
        return c

    def op(self, eng, fn, reads=(), writes=(), ctr=None):
        c = ctr if ctr is not None else self.ctr[eng]
        own = self.ctr.get(eng)
        need = {}

        def req(dep, raw):
            if dep is None:
                return
            dc, dv = dep
            if dc is own and not raw and eng == "pe":
                return
            if need.get(dc, 0) < dv:
                need[dc] = dv

        for b in reads:
            req(b.w, True)
        for b in writes:
            req(b.w, False)
            for dc, dv in b.r.items():
                req((dc, dv), False)
        kn = self.known[eng]
        waits = []
        for dc, dv in need.items():
            if kn.get(dc, 0) >= dv:
                continue
            kn[dc] = dv
            waits.append((dc, dv))
        c.val += c.step
        for b in reads:
            if b.r.get(c, 0) < c.val:
                b.r[c] = c.val
        for b in writes:
            b.w = (c, c.val)
            b.r = {}
        self.q[eng].append((waits, fn, c))

    def emit(self, final=False):
        nc = self.nc
        q = self.q
        self.marks.append({k: self.ctr[k].val for k in ("pe", "act", "dve", "pool")})
        self.q = {k: [] for k in self.names}
        with nc.Block(no_gpsimd_drain=True) as block:
            def body(name):
                def f(e):
                    for waits, fn, c in q[name]:
                        for dc, dv in waits:
                            e.wait_ge(dc.sem, dv)
                        fn(e).then_inc(c.sem, c.step)
                    if name == "sp":
                        for c in self.dma_ctrs:
                            if c.val:
                                e.wait_ge(c.sem, c.val)
                return f
            block.tensor(body("pe"))
            block.scalar(body("act"))
            block.vector(body("dve"))
            block.gpsimd(body("pool"))
            block.sync(body("sp"))


def build(nseq, dbg=False):
    nc = bass.Bass("TRN2", target_bir_lowering=False)
    S = Sched(nc)

    def din(name, shape):
        return nc.dram_tensor(name, list(shape), F32, kind="ExternalInput").ap()

    x_d = din("x", [nseq, T, D])
    mem_d = din("mem", [nseq, 256, D])
    w_in = din("w_in", [D, 7184])
    w_mkv = din("w_mkv", [D, 1024])
    w_bm = din("w_bm", [512, D])
    w_ba = din("w_ba", [1024, D])
    w_bc = din("w_bc", [512, D])
    w_out = din("w_out", [D, D])
    w_up = din("w_up", [D, 2 * DFF])
    w_dn = din("w_dn", [DFF, D])
    lnin_g = din("lnin_g", [D]); lnin_b = din("lnin_b", [D])
    ln1_g = din("ln1_g", [D]); ln1_b = din("ln1_b", [D])
    ln2_g = din("ln2_g", [D]); ln2_b = din("ln2_b", [D])
    gbias_d = din("gbias", [16])
    mconv_w = din("mconv_w", [3, 1024]); mconv_b = din("mconv_b", [1024])
    mnorm_g = din("mnorm_g", [512])
    qg_d = din("qg", [128]); kg_d = din("kg", [128])
    fconv_w = din("fconv_w", [3, 2 * DFF]); fconv_b = din("fconv_b", [2 * DFF])
    c_ident = din("c_ident", [128, 128])
    c_maskF = din("c_maskF", [128, 128])
    c_maskB = din("c_maskB", [128, 128])
    c_ropeC = din("c_ropeC", [T, 128])
    c_ropeS = din("c_ropeS", [T, 128])
    y_d = nc.dram_tensor("y", [nseq, T, D], F32, kind="ExternalOutput").ap()
    x1s_d = nc.dram_tensor("x1s", [T, D], F32, kind=("ExternalOutput" if dbg else "Internal")).ap()
    if dbg:
        dbg_h = nc.dram_tensor("dbg_h", [128, 16, T], BF16, kind="ExternalOutput").ap()

    class Arena:
        def __init__(self):
            self.segs = []
            self.n = 0

        def set(self, segs):
            self.segs = [[lo, hi] for lo, hi in segs]

        def __call__(self, name, shape, dt):
            nb = 2 if dt == BF16 else 4
            size = nb
            for s_ in shape[1:]:
                size *= s_
            size = (size + 31) // 32 * 32
            for sg in self.segs:
                if sg[0] + size <= sg[1]:
                    off = sg[0]
                    sg[0] += size
                    self.n += 1
                    return nc.alloc_sbuf_tensor_at("%s_%d" % (name, self.n), list(shape), dt, offset=off)
            raise RuntimeError("arena full: %s %s" % (name, shape))

    BASE = 16512
    TOP = 229344
    sb = Arena()
    sb.set([(BASE, TOP)])

    class _CM:
        def __init__(self, t):
            self.t = t

        def __enter__(self):
            return self.t

        def __exit__(self, *a):
            return False

    def sbt(name, shape, dt):
        return _CM(sb(name, shape, dt))

    def mm(out, lhsT, rhs, start, stop, R, W, sgc=False):
        if sgc:
            S.op("pe", lambda e: e.matmul(out, lhsT=lhsT, rhs=rhs, start=start, stop=stop, skip_group_check=True), reads=R, writes=W)
        else:
            S.op("pe", lambda e: e.matmul(out, lhsT=lhsT, rhs=rhs, start=start, stop=stop), reads=R, writes=W)

    def tr(out, in_, ident, R, W):
        S.op("pe", lambda e: e.transpose(out=out, in_=in_, identity=ident), reads=R, writes=W)

    def act(out, in_, func, R, W, bias=None, scale=None, accum=None):
        kw = {}
        if bias is not None:
            kw["bias"] = bias
        if scale is not None:
            kw["scale"] = scale
        if accum is not None:
            kw["accum_out"] = accum
        S.op("act", lambda e: e.activation(out=out, in_=in_, func=func, **kw), reads=R, writes=W)

    def ts(eng, out, in0, s1, s2, op0, op1, R, W):
        if op1 is None:
            S.op(eng, lambda e: e.tensor_scalar(out=out, in0=in0, scalar1=s1, scalar2=None, op0=op0), reads=R, writes=W)
        else:
            S.op(eng, lambda e: e.tensor_scalar(out=out, in0=in0, scalar1=s1, scalar2=s2, op0=op0, op1=op1), reads=R, writes=W)

    def tt(eng, out, in0, in1, op, R, W):
        S.op(eng, lambda e: e.tensor_tensor(out=out, in0=in0, in1=in1, op=op), reads=R, writes=W)

    def stt(out, in0, scalar, in1, op0, op1, R, W):
        S.op("dve", lambda e: e.scalar_tensor_tensor(out=out, in0=in0, scalar=scalar, in1=in1, op0=op0, op1=op1), reads=R, writes=W)

    def cp(eng, out, in_, R, W):
        if eng == "act":
            S.op("act", lambda e: e.copy(out=out, in_=in_), reads=R, writes=W)
        else:
            S.op(eng, lambda e: e.tensor_copy(out=out, in_=in_), reads=R, writes=W)

    def recip(out, in_, R, W):
        S.op("dve", lambda e: e.reciprocal(out=out, in_=in_), reads=R, writes=W)

    def memset(eng, ap, val, W):
        S.op(eng, lambda e: e.memset(ap, val), writes=W)

    def mdma(out, in_, R, W):
        k = mc_i[0] % NMC
        mc_i[0] += 1
        dma(out, in_, R, list(W) + [B_mc[k]], C_mc[k])

    def dma(out, in_, R, W, ctr, slow=False):
        if slow:
            S.op("sp", lambda e: e.dma_start(out=out, in_=in_, allow_slow_non_contiguous=True), reads=R, writes=W, ctr=ctr)
        else:
            S.op("sp", lambda e: e.dma_start(out=out, in_=in_), reads=R, writes=W, ctr=ctr)

    def pipeline(n, stages, oldest_first=True, order=None):
        k = len(stages)
        for it in range(n + k - 1):
            if order is None:
                order = list(reversed(range(k))) if oldest_first else list(range(k))
            for si in order:
                t_ = it - si
                if 0 <= t_ < n:
                    stages[si](t_)

    ps = nc.alloc_psum_tensor("ps", [128, 7, 512], F32)
    psb = nc.alloc_psum_tensor("psb", [128, 1024], BF16)
    PB = [Buf("ps%d" % i) for i in range(7)]
    PBB = Buf("psb")

    ident_f = sb("ident_f", [128, 128], F32)
    ident_b = sb("ident_b", [128, 128], BF16)
    maskF = sb("maskF", [128, 128], F32)
    maskB = sb("maskB", [128, 128], F32)
    ones_f = sb("ones_f", [128, 128], F32)
    B_const = Buf("const")
    lnin_fm = sb("lnin_fm", [128, 2, 8], F32)
    ln1_fm = sb("ln1_fm", [128, 2, 8], F32)
    mconv_fm = sb("mconv_fm", [128, 4, 8], F32)
    fconv_fm = sb("fconv_fm", [128, 4, 44], F32)
    negc = sb("negc", [128, 1], F32)
    ctmp = sb("ctmp", [128, 4], F32)
    mhalf = sb("mhalf", [128, 16], F32)
    gbt_in = sb("gbt_in", [128, 2, D], F32)

    NSTG = 3
    stg = [sb("stg%d" % i, [128, 2048], F32) for i in range(NSTG)]
    B_stg = [Buf("stg%d" % i) for i in range(NSTG)]
    C_stg = [S.dma_ctr("stg%d" % i) for i in range(NSTG)]
    stg_i = [0]

    xx = sb("xx", [128, 8, T], BF16)
    B_xx = [Buf("xx%d" % g) for g in range(4)]
    stats_in = sb("stats_in", [128, NT, 2], F32)
    B_stats_in = Buf("stats_in")

    c_misc = S.dma_ctr("misc")
    NMC = 8
    C_mc = [S.dma_ctr("mc%d" % i) for i in range(NMC)]
    B_mc = [Buf("mc%d" % i) for i in range(NMC)]
    mc_i = [0]
    C_slots = {}

    def slot_ctrs(name):
        if name not in C_slots:
            C_slots[name] = [S.dma_ctr(name + "0"), S.dma_ctr(name + "1")]
        return C_slots[name]
    P0_END = sb.segs[0][0]
    H0 = P0_END
    hmT = nc.alloc_sbuf_tensor_at("hmT", [128, 4, T], BF16, offset=H0)
    hcT = nc.alloc_sbuf_tensor_at("hcT", [128, 4, T], BF16, offset=H0 + 16384)
    haT = nc.alloc_sbuf_tensor_at("haT", [128, 8, T], BF16, offset=H0 + 32768)
    H_END = H0 + 65536
    M0 = H_END
    mT = nc.alloc_sbuf_tensor_at("mT", [128, 8, T], BF16, offset=M0)
    M_END = M0 + 32768
    assert M_END + 40 * 1024 < TOP, (P0_END, M_END, TOP)
    sb.set([(P0_END, TOP)])

    WS_TOTAL = 160 * 1024
    wscr = nc.dram_tensor("wscr", [128, WS_TOTAL], BF16, kind="Internal").ap()
    wcache = {}
    ws_off = [0]
    NWC = 6
    C_wc = [S.dma_ctr("wc%d" % i) for i in range(NWC)]
    B_wc = [Buf("wc%d" % i) for i in range(NWC)]
    wc_i = [0]

    def load_w(dst, src, n_a, n_b, B_dst, key):
        L = n_a * n_b
        k = wc_i[0] % NWC
        wc_i[0] += 1
        if key in wcache:
            off, Bc = wcache[key]
            dma(dst, wscr[:, off:off + L].rearrange("p (a b) -> p a b", b=n_b), [Bc], [B_dst, B_wc[k]], C_wc[k])
            return
        i = stg_i[0] % NSTG
        stg_i[0] += 1
        sv = stg[i][:, 0:L].rearrange("p (a b) -> p a b", b=n_b)
        dma(sv, src, [], [B_stg[i]], C_stg[i])
        cp("act" if i % 3 != 2 else "dve", dst, sv, [B_stg[i]], [B_dst])
        off = ws_off[0]
        ws_off[0] += L
        assert ws_off[0] <= WS_TOTAL
        Bc = Buf("wscr")
        wcache[key] = (off, Bc)
        dma(wscr[:, off:off + L].rearrange("p (a b) -> p a b", b=n_b), dst, [B_dst], [Bc, B_wc[k]], C_wc[k])

    def win_cols(c0, n):
        return w_in[:, c0:c0 + n].rearrange("(kc p) n -> p kc n", p=128)

    dma(ident_f[:], c_ident, [], [B_const], c_misc)
    mdma(maskF[:], c_maskF, [], [B_const])
    mdma(maskB[:], c_maskB, [], [B_const])
    cm = S.dma_ctr("m3")
    dma(lnin_fm[:, 0, :], lnin_g.rearrange("(kc p) -> p kc", p=128), [], [B_const], cm, slow=True)
    for (dst, src) in [(lnin_fm[:, 1, :], lnin_b.rearrange("(kc p) -> p kc", p=128)),
                       (ln1_fm[:, 0, :], ln1_g.rearrange("(kc p) -> p kc", p=128)),
                       (ln1_fm[:, 1, :], ln1_b.rearrange("(kc p) -> p kc", p=128)),
                       (mconv_fm[:, 3, :], mconv_b.rearrange("(kc p) -> p kc", p=128)),
                       (fconv_fm[:, 3, :], fconv_b.rearrange("(kc p) -> p kc", p=128))]:
        dma(dst, src, [], [B_const], cm, slow=True)
    for j in range(3):
        dma(mconv_fm[:, j, :], mconv_w[j, :].rearrange("(kc p) -> p kc", p=128), [], [B_const], cm, slow=True)
        dma(fconv_fm[:, j, :], fconv_w[j, :].rearrange("(kc p) -> p kc", p=128), [], [B_const], cm, slow=True)
    cp("pool", ident_b[:], ident_f[:], [B_const], [B_const])
    memset("pool", ones_f[:], 1.0, [B_const])
    memset("pool", mhalf[:], -0.5, [B_const])
    for i_, v_ in enumerate([lnin_g, lnin_b]):
        mdma(gbt_in[:, i_, :], v_.partition_broadcast(128), [], [B_const])
    ts("dve", gbt_in[:], gbt_in[:], ALPHA, None, ALU.mult, None, [B_const], [B_const])
    memset("pool", ctmp[:, 2:3], -0.5 * math.log(128.0), [B_const])
    gtmp = sb("gtmp", [128, 2, 128], F32)
    if True:
        Bg = Buf("gtmp")
        mdma(gtmp[:, 0, :], qg_d.partition_broadcast(128), [], [Bg])
        mdma(gtmp[:, 1, :], kg_d.partition_broadcast(128), [], [Bg])
        S.op("dve", lambda e: e.tensor_reduce(out=ctmp[:, 0:2], in_=gtmp[:], axis=AX.X, op=ALU.max, apply_absolute_value=True),
             reads=[Bg], writes=[B_const])
        stt(negc[:], ctmp[:, 0:1], -math.sqrt(128.0), ctmp[:, 1:2], ALU.mult, ALU.mult, [B_const], [B_const])
        S.emit()

    def layer_norm_stats(src_ap, st, mv, rstd, nmr, Bsrc, Bst):
        for i in range(2):
            S.op("dve", lambda e, i=i: e.bn_stats(out=st[:, i, :], in_=src_ap[:, i * 512:(i + 1) * 512]), reads=[Bsrc], writes=[Bst])
        S.op("dve", lambda e: e.bn_aggr(out=mv, in_=st), reads=[Bst], writes=[Bst])
        ts("dve", rstd, mv[:, 1:2], EPS, None, ALU.add, None, [Bst], [Bst])
        tt("pool", rstd, rstd, mhalf[:, 0:1], ALU.pow, [Bst, B_const], [Bst])
        stt(nmr, mv[:, 0:1], -1.0, rstd, ALU.mult, ALU.mult, [Bst], [Bst])

    for seq in range(nseq):
        xs = x_d[seq]
        sb.set([(P0_END, TOP)])
        xt = sb("a_xt", [128, 2, D], F32)
        xh = sb("a_xh", [128, 2, D], F32)
        stt_t = sb("a_st", [128, 2, 16], F32)
        if True:
            B_xt = [Buf(), Buf()]; B_xh = [Buf(), Buf()]; B_st = [Buf(), Buf()]
            C_xt = slot_ctrs("axt")
            def a_1(t_):
                s = t_ % 2
                dma(xt[:, s, :], xs[t_ * 128:(t_ + 1) * 128, :], [], [B_xt[s]], C_xt[s])
                st = stt_t[:, s, 0:12].rearrange("p (a b) -> p a b", b=6)
                mv = stt_t[:, s, 12:14]
                for i in range(2):
                    S.op("dve", lambda e, i=i: e.bn_stats(out=st[:, i, :], in_=xt[:, s, i * 512:(i + 1) * 512]), reads=[B_xt[s]], writes=[B_st[s]])
                S.op("dve", lambda e: e.bn_aggr(out=mv, in_=st), reads=[B_st[s]], writes=[B_st[s]])
                ts("dve", stats_in[:, t_, 0:1], mv[:, 1:2], EPS, None, ALU.add, None, [B_st[s]], [B_st[s], B_stats_in])
                tt("pool", stats_in[:, t_, 0:1], stats_in[:, t_, 0:1], mhalf[:, 0:1], ALU.pow, [B_st[s], B_const], [B_st[s], B_stats_in])

            def a_2(t_):
                s = t_ % 2
                mv = stt_t[:, s, 12:14]
                rstd = stats_in[:, t_, 0:1]
                nmr = stats_in[:, t_, 1:2]
                stt(nmr, mv[:, 0:1], -1.0, rstd, ALU.mult, ALU.mult, [B_st[s]], [B_st[s], B_stats_in])
                act(xh[:, s, :], xt[:, s, :], AF.Identity, [B_xt[s], B_st[s]], [B_xh[s]], bias=nmr, scale=rstd)

            def a_3(t_):
                s = t_ % 2
                b0 = 2 * s
                for kc in range(8):
                    tr(ps[:, b0 + kc // 4, (kc % 4) * 128:(kc % 4 + 1) * 128], xh[:, s, kc * 128:(kc + 1) * 128], ident_f[:],
                       [B_xh[s], B_const], [PB[b0 + kc // 4]])
                for kc in range(8):
                    if kc % 2 == 0:
                        ts("dve", xx[:, kc, t_ * 128:(t_ + 1) * 128], ps[:, b0 + kc // 4, (kc % 4) * 128:(kc % 4 + 1) * 128],
                           lnin_fm[:, 0, kc:kc + 1], lnin_fm[:, 1, kc:kc + 1], ALU.mult, ALU.add,
                           [B_const], [PB[b0 + kc // 4], B_xx[t_ // 4]])
                    else:
                        act(xx[:, kc, t_ * 128:(t_ + 1) * 128], ps[:, b0 + kc // 4, (kc % 4) * 128:(kc % 4 + 1) * 128], AF.Identity,
                            [B_const], [PB[b0 + kc // 4], B_xx[t_ // 4]], bias=lnin_fm[:, 1, kc:kc + 1], scale=lnin_fm[:, 0, kc:kc + 1])

            pipeline(NT, [a_1, a_2, a_3], oldest_first=True)

            S.emit()

        if True:
            B_hm = Buf("hm"); B_ha = Buf("ha"); B_hc = Buf("hc")

            sb.set([(H0 + 16384, TOP)])
            gw = sb("b_gw", [128, 8, 16], BF16)
            gbB = sb("b_gb", [128, 16], F32)
            GT = sb("b_GT", [128, 16, 16], F32)
            SPt = sb("b_SP", [128, 16, 8], F32)
            UU = sb("b_U", [128, 4, 16, 8], F32)
            gtm = sb("b_tmp", [128, 16, 8], F32)
            mgB = sb("b_mg", [128, 512], F32)
            if True:
                B_gw = Buf(); B_g = Buf("gates")
                load_w(gw[:], win_cols(O_GM, 16), 8, 16, B_gw, ("gw",))
                mdma(gbB[:], gbias_d.partition_broadcast(128), [], [B_g])
                mdma(mgB[:], mnorm_g.partition_broadcast(128), [], [B_g])
                for t_ in range(NT):
                    for kc in range(8):
                        mm(ps[:, 0, t_ * 16:(t_ + 1) * 16], xx[:, kc, t_ * 128:(t_ + 1) * 128], gw[:, kc, :], kc == 0, kc == 7,
                           [B_xx[t_ // 4], B_gw], [PB[0]])
                tt("dve", GT[:], ps[:, 0, 0:256].rearrange("p (a b) -> p a b", b=16), gbB[:].unsqueeze(1).broadcast_to([128, 16, 16]),
                   ALU.add, [B_g], [PB[0], B_g])
                GT5 = GT[:].rearrange("p c (d k h) -> p c d k h", d=2, k=2, h=4)
                SP4 = SPt[:].rearrange("p c (d h) -> p c d h", d=2)
                act(SP4, GT5[:, :, :, 1, :], AF.Exp, [B_g], [B_g], scale=-1.0)
                act(SPt[:], SPt[:], AF.Ln, [B_g], [B_g], bias=1.0)
                mm(ps[:, 1, 0:64], maskF[:], SP4[:, :, 0, :], True, True, [B_const, B_g], [PB[1]])
                mm(ps[:, 1, 64:128], maskB[:], SP4[:, :, 1, :], True, True, [B_const, B_g], [PB[1]])
                mm(ps[:, 1, 128:256], ones_f[:], SPt[:], True, True, [B_const, B_g], [PB[1]])
                U4 = UU[:].rearrange("p k c (d h) -> p k c d h", d=2)
                g4 = gtm[:].rearrange("p c (d h) -> p c d h", d=2)
                for d_ in range(2):
                    tt("dve", g4[:, :, d_, :], ps[:, 1, d_ * 64:(d_ + 1) * 64].rearrange("p (c h) -> p c h", h=4), GT5[:, :, d_, 0, :],
                       ALU.add, [B_g], [PB[1], B_g])
                    act(U4[:, 2, :, d_, :], ps[:, 1, d_ * 64:(d_ + 1) * 64].rearrange("p (c h) -> p c h", h=4), AF.Exp, [], [PB[1], B_g])
                act(UU[:, 0, :, :], gtm[:], AF.Exp, [B_g], [B_g], bias=ctmp[:, 2:3])
                tt("dve", gtm[:], gtm[:], ps[:, 1, 128:256].rearrange("p (c k) -> p c k", k=8), ALU.subtract, [B_g], [PB[1], B_g])
                act(UU[:, 1, :, :], gtm[:], AF.Exp, [B_g], [B_g], bias=ctmp[:, 2:3])
                act(UU[:, 3, :, :], ps[:, 1, 128:256].rearrange("p (c k) -> p c k", k=8), AF.Exp, [], [PB[1], B_g], scale=-1.0)

                mark_b = [list(s_) for s_ in sb.segs]
                B_w = [Buf() for _ in range(4)]; B_pre = [Buf(), Buf()]; B_dg = Buf(); B_qk = [Buf(), Buf()]
                B_va = Buf(); B_kw = Buf(); B_cm = [[Buf(), Buf()], [Buf(), Buf()]]; B_cb = [[Buf(), Buf()], [Buf(), Buf()]]
                B_at = [Buf(), Buf()]
                wb_ = sb("b_w", [128, 4, 8, 128], BF16)
                pre = sb("b_pre", [128, 2, 2050], BF16)
                dg = sb("b_dg", [128, 6, 128], BF16)
                qkT = sb("b_qk", [128, 2, T], BF16)
                VA = sb("b_va", [128, 16, 129], BF16)
                KW = sb("b_kw", [128, 16, 2, 128], BF16)
                CM = sb("b_cm", [128, 2, 2, 129], F32)
                CB = sb("b_cb", [128, 2, 2, 129], BF16)
                AT = sb("b_at", [128, 2, 16, 128], BF16)
                OS_l = [sb("b_os", [128, 16, 128], F32) for _ in range(2)]
                RAW_l = [sb("b_raw", [128, 2, 16, 129], F32) for _ in range(2)]
                sm2_l = [sb("b_sm2", [128, 2, 2, 16, 1], F32) for _ in range(2)]
                lnst_l = [sb("b_ln", [128, 16, 8], F32) for _ in range(2)]
                HB_l = [sb("b_hb", [128, 16, 128], BF16) for _ in range(2)]
                B_os_l = [Buf(), Buf()]; B_raw_l = [[Buf(), Buf()], [Buf(), Buf()]]; B_sm2_l = [Buf(), Buf()]; B_ln_l = [Buf(), Buf()]; B_hb_l = [Buf(), Buf()]

                def head_P(h):
                    OS = OS_l[h % 2]; RAW = RAW_l[h % 2]; sm2 = sm2_l[h % 2]; lnst = lnst_l[h % 2]; HB = HB_l[h % 2]
                    B_os = B_os_l[h % 2]; B_raw = B_raw_l[h % 2]; B_sm2 = B_sm2_l[h % 2]; B_ln = B_ln_l[h % 2]; B_hb = B_hb_l[h % 2]
                    for i, o in enumerate([O_QM, O_KM, O_VM, O_OM]):
                        load_w(wb_[:, i, :, :], win_cols(o + h * 128, 128), 8, 128, B_w[i], ("bw", h, i))
                    if h == 0:
                        memset("pool", pre[:], 0.0, B_pre)
                        memset("pool", VA[:, :, 128:129], 1.0, [B_va])
                    for i in range(2):
                        for j in range(3):
                            act(dg[:, i * 3 + j, :], ident_f[:], AF.Identity, [B_const], [B_dg], scale=mconv_fm[:, j, i * 4 + h:i * 4 + h + 1])
                    for i in range(2):
                        for g in range(4):
                            bk = 2 + (i * 4 + g) % 2
                            for kc in range(8):
                                mm(ps[:, bk, :], wb_[:, i, kc, :], xx[:, kc, g * 512:(g + 1) * 512], kc == 0, kc == 7,
                                   [B_w[i], B_xx[g]], [PB[bk]])
                            cp("act", pre[:, i, 1 + g * 512:1 + (g + 1) * 512], ps[:, bk, :], [], [PB[bk], B_pre[i]])
                    for i in (2, 3):
                        for g in range(4):
                            bk = 0 + (i * 4 + g) % 2
                            for t4 in range(4):
                                t_ = g * 4 + t4
                                for kc in range(8):
                                    mm(ps[:, bk, t4 * 128:(t4 + 1) * 128], xx[:, kc, t_ * 128:(t_ + 1) * 128], wb_[:, i, kc, :], kc == 0, kc == 7,
                                       [B_xx[g], B_w[i]], [PB[bk]])
                            src = ps[:, bk, :].rearrange("p (a b) -> p a b", b=128)
                            if i == 2:
                                cp("act", VA[:, g * 4:(g + 1) * 4, 0:128], src, [], [PB[bk], B_va])
                            else:
                                act(OS[:, g * 4:(g + 1) * 4, :], src, AF.Sigmoid, [], [PB[bk], B_os])
                    for i in range(2):
                        for g in range(4):
                            bk = 4 + (i * 4 + g) % 2
                            for j in range(3):
                                mm(ps[:, bk, :], dg[:, i * 3 + j, :], pre[:, i, g * 512 + j:g * 512 + j + 512], j == 0, j == 2,
                                   [B_dg, B_pre[i]], [PB[bk]])
                            act(qkT[:, i, g * 512:(g + 1) * 512], ps[:, bk, :], AF.Silu, [B_const], [PB[bk], B_qk[i]],
                                bias=mconv_fm[:, 3, i * 4 + h:i * 4 + h + 1])
                    for c8 in range(2):
                        for c in range(c8 * 8, c8 * 8 + 8):
                            tr(psb[:, (c % 8) * 128:(c % 8 + 1) * 128], qkT[:, 1, c * 128:(c + 1) * 128], ident_b[:], [B_qk[1], B_const], [PBB])
                        for c in range(c8 * 8, c8 * 8 + 8):
                            for d_ in range(2):
                                act(KW[:, c, d_, :], psb[:, (c % 8) * 128:(c % 8 + 1) * 128], AF.Identity, [B_g], [PBB, B_kw],
                                    scale=UU[:, 1, c, d_ * 4 + h:d_ * 4 + h + 1])

                def head_S(h):
                    OS = OS_l[h % 2]; RAW = RAW_l[h % 2]; sm2 = sm2_l[h % 2]; lnst = lnst_l[h % 2]; HB = HB_l[h % 2]
                    B_os = B_os_l[h % 2]; B_raw = B_raw_l[h % 2]; B_sm2 = B_sm2_l[h % 2]; B_ln = B_ln_l[h % 2]; B_hb = B_hb_l[h % 2]
                    for c in range(16):
                        sl = slice(c * 128, (c + 1) * 128)
                        pS = c % 2
                        mm(ps[:, pS, 0:128], qkT[:, 1, sl], qkT[:, 0, sl], True, True, [B_qk[0], B_qk[1]], [PB[pS]])
                        for d_ in range(2):
                            stt(AT[:, d_, c, :], ps[:, pS, 0:128], UU[:, 0, c, d_ * 4 + h:d_ * 4 + h + 1], (maskF if d_ == 0 else maskB)[:], ALU.mult, ALU.mult,
                                [B_g, B_const], [PB[pS], B_at[d_]])
                    for d_ in range(2):
                        memset("pool", CM[:, d_, 1, :], 0.0, [B_cm[d_][1]])
                    for ci in range(16):
                        for d_ in range(2):
                            c = ci if d_ == 0 else 15 - ci
                            dh = d_ * 4 + h
                            sl = slice(c * 128, (c + 1) * 128)
                            pO = (2 * ci + d_) % 2
                            pK = 2 + 2 * d_ + ci % 2
                            par = ci % 2
                            if ci < 15:
                                mm(ps[:, pK, 0:129], KW[:, c, d_, :], VA[:, c, :], True, True, [B_kw, B_va], [PB[pK]])
                            mm(ps[:, pO, 0:129], AT[:, d_, c, :], VA[:, c, :], True, ci == 0, [B_at[d_], B_va], [PB[pO]])
                            if ci > 0:
                                mm(ps[:, pO, 0:129], qkT[:, 0, sl], CB[:, d_, par ^ 1, :], False, True, [B_qk[0], B_cb[d_][par ^ 1]], [PB[pO]])
                            cp("act", RAW[:, d_, c, :], ps[:, pO, 0:129], [], [PB[pO], B_raw[d_]])
                            if ci < 15:
                                stt(CM[:, d_, par, :], CM[:, d_, par ^ 1, :], UU[:, 3, c, dh:dh + 1], ps[:, pK, 0:129], ALU.mult, ALU.add,
                                    [B_g, B_cm[d_][par ^ 1]], [PB[pK], B_cm[d_][par]])
                                cp("act", CB[:, d_, par, :], CM[:, d_, par, :], [B_cm[d_][par]], [B_cb[d_][par]])


                def head_T(h):
                    OS = OS_l[h % 2]; RAW = RAW_l[h % 2]; sm2 = sm2_l[h % 2]; lnst = lnst_l[h % 2]; HB = HB_l[h % 2]
                    B_os = B_os_l[h % 2]; B_raw = B_raw_l[h % 2]; B_sm2 = B_sm2_l[h % 2]; B_ln = B_ln_l[h % 2]; B_hb = B_hb_l[h % 2]
                    den = RAW[:, :, :, 128:129]
                    ENv = UU[:, 2, :, :].rearrange("p c (d k) -> p d c k", d=2)[:, :, :, h:h + 1]
                    tt("dve", sm2[:, 0, :, :, :], den, ENv, ALU.max, B_raw + [B_g], [B_sm2])
                    stt(sm2[:, 1, :, :, :], den, -1.0, sm2[:, 0, :, :, :], ALU.mult, ALU.max, B_raw + [B_sm2], [B_sm2])
                    recip(sm2[:, 0, :, :, :], sm2[:, 1, :, :, :], [B_sm2], [B_sm2])
                    for d_ in range(2):
                        tt("dve", RAW[:, d_, :, 0:128], RAW[:, d_, :, 0:128], sm2[:, 0, d_, :, :].broadcast_to([128, 16, 128]), ALU.mult,
                           [B_raw[d_], B_sm2], [B_raw[d_]])
                    H = RAW[:, 0, :, 0:128]
                    tt("dve", H, H, RAW[:, 1, :, 0:128], ALU.add, B_raw, [B_raw[0]])
                    B_H = [B_raw[0]] * 16

                    for c in range(16):
                        S.op("dve", lambda e, c=c: e.bn_stats(out=lnst[:, c, 0:6], in_=H[:, c, :]), reads=[B_H[c]], writes=[B_ln])
                    for c in range(16):
                        S.op("dve", lambda e, c=c: e.bn_aggr(out=lnst[:, c, 6:8], in_=lnst[:, c, 0:6]), reads=[B_ln], writes=[B_ln])
                    ts("dve", lnst[:, :, 7:8], lnst[:, :, 7:8], EPS, None, ALU.add, None, [B_ln], [B_ln])
                    tt("pool", lnst[:, :, 7:8], lnst[:, :, 7:8], mhalf[:, 0:16].unsqueeze(2), ALU.pow, [B_ln, B_const], [B_ln])
                    for c in range(16):
                        ts("dve", H[:, c, :], H[:, c, :], lnst[:, c, 6:7], lnst[:, c, 7:8], ALU.subtract, ALU.mult, [B_ln, B_H[c]], [B_H[c]])
                    tt("dve", H, H, mgB[:, h * 128:(h + 1) * 128].unsqueeze(1).broadcast_to([128, 16, 128]), ALU.mult, [B_raw[0], B_g], [B_raw[0]])
                    tt("dve", HB[:], H, OS[:], ALU.mult, [B_raw[0], B_os], [B_hb])
                    for c in range(16):
                        tr(psb[:, (c % 8) * 128:(c % 8 + 1) * 128], HB[:, c, :], ident_b[:], [B_hb, B_const], [PBB])
                        if c % 8 == 7:
                            cp("act", hmT[:, h, (c - 7) * 128:(c + 1) * 128], psb[:], [], [PBB, B_hm])

                for h in range(5):
                    if h < 4:
                        head_P(h)
                    if h >= 1:
                        head_T(h - 1)
                    if h < 4:
                        head_S(h)
                S.emit()

            if True:
                for hk in range(2):
                    sb.set([(H_END, TOP)])
                    QT = sb("c_QT", [128, 4, T], BF16); KT = sb("c_KT", [128, T], BF16); VA = sb("c_VA", [128, 16, 129], BF16)
                    mark_c = [list(s_) for s_ in sb.segs]
                    rope = sb("c_rope", [128, 2, 16, 128], F32); gqk = sb("c_g", [128, 5, 128], F32)
                    wq = sb("c_wq", [128, 2, 8, 256], BF16); wkv = sb("c_wkv", [128, 2, 8, 128], BF16)
                    XQ = sb("c_XQ", [128, 2, 5, 128], F32); R1 = sb("c_R1", [128, 2, 5, 128], F32); R2 = sb("c_R2", [128, 2, 5, 128], F32)
                    QR = sb("c_QR", [128, 2, 5, 128], BF16); ss = sb("c_ss", [128, 2, 8], F32)
                    B_rope = [Buf() for _ in range(7)]
                    if True:
                        B_wq = Buf(); B_wkv = Buf(); B_QT = Buf(); B_KT = Buf(); B_VA = Buf()
                        B_XQ = [Buf(), Buf()]; B_R1 = [Buf(), Buf()]; B_R2 = [Buf(), Buf()]; B_QR = [Buf(), Buf()]; B_ss = [Buf(), Buf()]
                        B_PT = [Buf(), Buf()]; B_HA = [Buf(), Buf()]; B_rr = [Buf(), Buf()]
                        load_w(wq[:, 0, :, :], win_cols(O_QA + hk * 512, 256), 8, 256, B_wq, ("cq", hk, 0))
                        load_w(wq[:, 1, :, :], win_cols(O_QA + hk * 512 + 256, 256), 8, 256, B_wq, ("cq", hk, 1))
                        load_w(wkv[:, 0, :, :], win_cols(O_KA + hk * 128, 128), 8, 128, B_wkv, ("ck", hk))
                        load_w(wkv[:, 1, :, :], win_cols(O_VA + hk * 128, 128), 8, 128, B_wkv, ("cv", hk))
                        for i_ in range(5):
                            mdma(gqk[:, i_, :], (qg_d if i_ < 4 else kg_d).partition_broadcast(128), [], [B_rope[i_]])
                        mdma(rope[:, 0, :, :], c_ropeC.rearrange("(t p) d -> p t d", p=128), [], [B_rope[5]])
                        mdma(rope[:, 1, :, :], c_ropeS.rearrange("(t p) d -> p t d", p=128), [], [B_rope[6]])
                        memset("pool", VA[:, :, 128:129], 1.0, [B_VA])
                        def c1_A(t_):
                            s = t_ % 2
                            bq = 0 + s
                            bkv = 2 + s
                            tsl = slice(t_ * 128, (t_ + 1) * 128)
                            for hf in range(2):
                                for kc in range(8):
                                    mm(ps[:, bq, hf * 256:(hf + 1) * 256], xx[:, kc, tsl], wq[:, hf, kc, :], kc == 0, kc == 7,
                                       [B_xx[t_ // 4], B_wq], [PB[bq]])
                            for i in range(2):
                                for kc in range(8):
                                    mm(ps[:, bkv, i * 128:(i + 1) * 128], xx[:, kc, tsl], wkv[:, i, kc, :], kc == 0, kc == 7,
                                       [B_xx[t_ // 4], B_wkv], [PB[bkv]])
                            cp("act", XQ[:, s, 0:4, :], ps[:, bq, :].rearrange("p (a b) -> p a b", b=128), [], [PB[bq], B_XQ[s]])
                            cp("act", XQ[:, s, 4, :], ps[:, bkv, 0:128], [], [PB[bkv], B_XQ[s]])
                            cp("act", VA[:, t_, 0:128], ps[:, bkv, 128:256], [], [PB[bkv], B_VA])
                            for i in range(5):
                                src_ = ps[:, bq, i * 128:(i + 1) * 128] if i < 4 else ps[:, bkv, 0:128]
                                act(R2[:, s, i, :], src_, AF.Square, [], [PB[bq] if i < 4 else PB[bkv], B_R2[s], B_ss[s]], accum=ss[:, s, i:i + 1])

                        def c1_B(t_):
                            s = t_ % 2
                            tt("dve", XQ[:, s, :, :], XQ[:, s, :, :], gqk[:], ALU.mult, [B_XQ[s]] + B_rope, [B_XQ[s]])
                            tt("dve", R1[:, s, :, :], XQ[:, s, :, :], rope[:, 0, t_, :].unsqueeze(1).broadcast_to([128, 5, 128]), ALU.mult,
                               [B_XQ[s]] + B_rope, [B_R1[s]])
                            xv = XQ[:, s, :, :].rearrange("p h (a s d) -> p h a s d", a=2, s=2)
                            rv = R2[:, s, :, :].rearrange("p h (a s d) -> p h a s d", a=2, s=2)
                            sv = rope[:, 1, t_, :].rearrange("p (a s d) -> p a s d", a=2, s=2)
                            for hf in range(2):
                                tt("dve", rv[:, :, :, hf, :], xv[:, :, :, 1 - hf, :], sv[:, :, hf, :].unsqueeze(1).broadcast_to([128, 5, 2, 32]),
                                   ALU.mult, [B_XQ[s]] + B_rope, [B_R2[s]])
                            ts("dve", ss[:, s, 0:5], ss[:, s, 0:5], 1.0 / 128.0, EPS, ALU.mult, ALU.add, [B_ss[s]], [B_ss[s]])
                            tt("pool", ss[:, s, 0:5], ss[:, s, 0:5], mhalf[:, 0:5], ALU.pow, [B_ss[s], B_const], [B_ss[s]])
                            tt("dve", R1[:, s, :, :], R1[:, s, :, :], R2[:, s, :, :], ALU.add, [B_R1[s], B_R2[s]], [B_R1[s]])
                            tt("dve", QR[:, s, :, :], R1[:, s, :, :], ss[:, s, 0:5].unsqueeze(2).broadcast_to([128, 5, 128]), ALU.mult,
                               [B_R1[s], B_ss[s]], [B_QR[s]])

                        def c1_C(t_):
                            s = t_ % 2
                            tsl = slice(t_ * 128, (t_ + 1) * 128)
                            for i in range(5):
                                tr(psb[:, i * 128:(i + 1) * 128], QR[:, s, i, :], ident_b[:], [B_QR[s], B_const], [PBB])
                            cp("act", QT[:, :, tsl], psb[:, 0:512].rearrange("p (a b) -> p a b", b=128), [], [PBB, B_QT])
                            cp("act", KT[:, tsl], psb[:, 512:640], [], [PBB, B_KT])

                        pipeline(NT, [c1_A, c1_B, c1_C], oldest_first=True)
                        S.emit()
                        sb.set(mark_c)
                        PT = sb("c_PT", [128, 2, 16, 512], BF16); HA = sb("c_ha", [128, 2, 4, 128], BF16); rr = sb("c_rr", [128, 2, 4], F32)

                        def c2_SP(it):
                            qb = it
                            s = qb % 2
                            qsl = slice((qb % NT) * 128, (qb % NT + 1) * 128)
                            pq = it - 1
                            sp_ = pq % 2
                            for kt in range(NT):
                                if qb < NT:
                                    bA = 2 + 2 * ((kt // 2) % 2)
                                    bk = bA + kt % 2
                                    mm(ps[:, bk, :], KT[:, kt * 128:(kt + 1) * 128], QT[:, :, qsl], True, True, [B_KT, B_QT], [PB[bk]])
                                    if kt % 2 == 1:
                                        act(PT[:, s, kt - 1:kt + 1, :], ps[:, bA:bA + 2, :], AF.Exp, [B_const], [PB[bA], PB[bA + 1], B_PT[s]],
                                            bias=negc[:], scale=1.0 / math.sqrt(128.0))
                                if pq >= 0:
                                    for g in range(4):
                                        ob = g // 2
                                        oc = (g % 2) * 256
                                        mm(ps[:, ob, oc:oc + 129], PT[:, sp_, kt, g * 128:(g + 1) * 128], VA[:, kt, :], kt == 0 and g % 2 == 0, kt == NT - 1,
                                           [B_PT[sp_], B_VA], [PB[ob]], sgc=True)
                            if pq >= 0:
                                for g in range(4):
                                    ob = g // 2
                                    oc = (g % 2) * 256
                                    recip(rr[:, sp_, g:g + 1], ps[:, ob, oc + 128:oc + 129], [], [PB[ob], B_rr[sp_]])
                                    ts("dve", HA[:, sp_, g, :], ps[:, ob, oc:oc + 128], rr[:, sp_, g:g + 1], None, ALU.mult, None, [B_rr[sp_]], [PB[ob], B_HA[sp_]])

                        def c2_T(qb):
                            s = qb % 2
                            qsl = slice(qb * 128, (qb + 1) * 128)
                            for g in range(4):
                                tr(psb[:, g * 128:(g + 1) * 128], HA[:, s, g, :], ident_b[:], [B_HA[s], B_const], [PBB])
                            cp("dve", haT[:, hk * 4:(hk + 1) * 4, qsl], psb[:, 0:512].rearrange("p (a b) -> p a b", b=128), [], [PBB, B_ha])

                        for it in range(NT + 2):
                            if it <= NT:
                                c2_SP(it)
                            if it - 2 >= 0:
                                c2_T(it - 2)

                        S.emit()


            sb.set([(H_END, TOP)])
            memt = sb("d_mem", [128, 2, D], F32)
            memT = sb("d_memT", [128, 8, 256], BF16)
            wkv = sb("d_wkv", [128, 8, 1024], BF16)
            wqc = sb("d_wq", [128, 8, 512], BF16)
            kcT = sb("d_kT", [128, 4, 256], BF16)
            vc = sb("d_v", [128, 2, 512], BF16)
            qcT = sb("d_qT", [128, 4, T], BF16)
            E = sb("d_E", [128, 2, 4, 256], F32)
            Pn = sb("d_P", [128, 2, 4, 256], BF16)
            PcT = sb("d_PT", [128, 2, 8, 128], BF16)
            dsm = sb("d_s", [128, 2, 8], F32)
            dsm2 = sb("d_s2", [128, 2, 8], F32)
            if True:
                B_mem = Buf(); B_memT = Buf(); B_wkv = Buf(); B_wq = Buf(); B_k = Buf(); B_v = Buf(); B_q = Buf()
                B_E = [Buf(), Buf()]; B_P = [Buf(), Buf()]; B_PcT = [Buf(), Buf()]; B_s = [Buf(), Buf()]; B_s2 = [Buf(), Buf()]
                mdma(memt[:], mem_d[seq].rearrange("(a p) d -> p a d", p=128), [], [B_mem])
                for q4 in range(4):
                    load_w(wkv[:, 2 * q4:2 * q4 + 2, :], w_mkv[q4 * 256:(q4 + 1) * 256, :].rearrange("(kc p) n -> p kc n", p=128), 2, 1024, B_wkv, ("dkv", q4))
                for q4 in range(2):
                    load_w(wqc[:, 4 * q4:4 * q4 + 4, :], w_in[q4 * 512:(q4 + 1) * 512, O_QC:O_QC + 512].rearrange("(kc p) n -> p kc n", p=128), 4, 512, B_wq, ("dq", q4))
                for a in range(2):
                    for kc in range(8):
                        bk = (a * 8 + kc) // 4
                        tr(ps[:, bk, (kc % 4) * 128:(kc % 4 + 1) * 128], memt[:, a, kc * 128:(kc + 1) * 128], ident_f[:], [B_mem, B_const], [PB[bk]])
                    for hh in range(2):
                        cp("dve", memT[:, hh * 4:(hh + 1) * 4, a * 128:(a + 1) * 128],
                           ps[:, a * 2 + hh, :].rearrange("p (a b) -> p a b", b=128), [], [PB[a * 2 + hh], B_memT])
                for hh in range(4):
                    bk = 4 + hh % 2
                    for kc in range(8):
                        mm(ps[:, bk, 0:256], wkv[:, kc, hh * 128:(hh + 1) * 128], memT[:, kc, :], kc == 0, kc == 7, [B_wkv, B_memT], [PB[bk]])
                    cp("act", kcT[:, hh, :], ps[:, bk, 0:256], [], [PB[bk], B_k])
                for a in range(2):
                    bk = 4 + a
                    for kc in range(8):
                        mm(ps[:, bk, :], memT[:, kc, a * 128:(a + 1) * 128], wkv[:, kc, 512:1024], kc == 0, kc == 7, [B_wkv, B_memT], [PB[bk]])
                    cp("act", vc[:, a, :], ps[:, bk, :], [], [PB[bk], B_v])
                for hh in range(4):
                    for g in range(4):
                        bk = (hh * 4 + g) % 4
                        for kc in range(8):
                            mm(ps[:, bk, :], wqc[:, kc, hh * 128:(hh + 1) * 128], xx[:, kc, g * 512:(g + 1) * 512], kc == 0, kc == 7,
                               [B_wq, B_xx[g]], [PB[bk]])
                        cp("act" if g % 2 else "dve", qcT[:, hh, g * 512:(g + 1) * 512], ps[:, bk, :], [], [PB[bk], B_q])
                sc = 1.0 / math.sqrt(128.0)

                def d_1(t_):
                    s = t_ % 2
                    tsl = slice(t_ * 128, (t_ + 1) * 128)
                    for hh in range(4):
                        bS = 2 * s + hh // 2
                        mm(ps[:, bS, (hh % 2) * 256:(hh % 2 + 1) * 256], qcT[:, hh, tsl], kcT[:, hh, :], True, True, [B_q, B_k], [PB[bS]])
                    S4 = ps[:, 2 * s:2 * s + 2, :].rearrange("p a (b m) -> p (a b) m", m=256)
                    S.op("dve", lambda e: e.tensor_reduce(out=dsm[:, s, 0:4], in_=S4, axis=AX.X, op=ALU.max),
                         reads=[], writes=[PB[2 * s], PB[2 * s + 1], B_s[s]])
                    ts("dve", dsm[:, s, 4:8], dsm[:, s, 0:4], -sc, None, ALU.mult, None, [B_s[s]], [B_s[s]])
                    for hh in range(4):
                        bS = 2 * s + hh // 2
                        act(E[:, s, hh, :], ps[:, bS, (hh % 2) * 256:(hh % 2 + 1) * 256], AF.Exp, [B_s[s]], [PB[bS], B_E[s], B_s2[s]],
                            bias=dsm[:, s, 4 + hh:5 + hh], scale=sc, accum=dsm2[:, s, hh:hh + 1])

                def d_2(t_):
                    s = t_ % 2
                    recip(dsm2[:, s, 4:8], dsm2[:, s, 0:4], [B_s2[s]], [B_s2[s]])
                    tt("dve", Pn[:, s, :, :], E[:, s, :, :], dsm2[:, s, 4:8].unsqueeze(2).broadcast_to([128, 4, 256]), ALU.mult, [B_E[s], B_s2[s]], [B_P[s]])
                    for hh in range(4):
                        for a in range(2):
                            tr(psb[:, (hh * 2 + a) * 128:(hh * 2 + a + 1) * 128], Pn[:, s, hh, a * 128:(a + 1) * 128], ident_b[:], [B_P[s], B_const], [PBB])

                def d_3(t_):
                    s = t_ % 2
                    tsl = slice(t_ * 128, (t_ + 1) * 128)
                    cp("act", PcT[:, s, :, :], psb[:].rearrange("p (a b) -> p a b", b=128), [], [PBB, B_PcT[s]])
                    bO = 4 + s
                    for hh in range(4):
                        for a in range(2):
                            mm(ps[:, bO, hh * 128:(hh + 1) * 128], vc[:, a, hh * 128:(hh + 1) * 128], PcT[:, s, hh * 2 + a, :], a == 0, a == 1,
                               [B_v, B_PcT[s]], [PB[bO]])

                def d_4(t_):
                    s = t_ % 2
                    tsl = slice(t_ * 128, (t_ + 1) * 128)
                    bO = 4 + s
                    cp("act", hcT[:, :, tsl], ps[:, bO, :].rearrange("p (a b) -> p a b", b=128), [], [PB[bO], B_hc])

                pipeline(NT, [d_1, d_2, d_3, d_4], order=[3, 2, 0, 1])
                S.emit()


            if dbg and seq == 0:
                cd = S.dma_ctr("dbg")
                dma(dbg_h[:, 0:4, :], hmT[:], [B_hm], [], cd)
                dma(dbg_h[:, 4:12, :], haT[:], [B_ha], [], cd)
                dma(dbg_h[:, 12:16, :], hcT[:], [B_hc], [], cd)
                S.emit()

            if True:
                B_mT = [Buf() for _ in range(4)]
                sb.set([(M_END, TOP)])
                wg = sb("e_wg", [128, 2, 3, 8, 128], BF16)
                wbr = sb("e_wb", [128, 2, 16, 128], BF16)
                SG = sb("e_sg", [128, 2, 3, 512], F32)
                Mt = sb("e_M", [128, 2, 2, 512], F32)
                if True:
                    B_wg = [Buf(), Buf()]; B_wb = [Buf(), Buf()]; B_sg = [Buf(), Buf()]; B_M = [Buf(), Buf()]

                    def load_e(cc):
                        s = cc % 2
                        for k in range(3):
                            load_w(wg[:, s, k, :, :], win_cols(O_GB + k * 1024 + cc * 128, 128), 8, 128, B_wg[s], ("eg", cc, k))
                        load_w(wbr[:, s, 0:4, :], w_bm[:, cc * 128:(cc + 1) * 128].rearrange("(kc p) n -> p kc n", p=128), 4, 128, B_wb[s], ("ebm", cc))
                        load_w(wbr[:, s, 4:12, :], w_ba[:, cc * 128:(cc + 1) * 128].rearrange("(kc p) n -> p kc n", p=128), 8, 128, B_wb[s], ("eba", cc))
                        load_w(wbr[:, s, 12:16, :], w_bc[:, cc * 128:(cc + 1) * 128].rearrange("(kc p) n -> p kc n", p=128), 4, 128, B_wb[s], ("ebc", cc))

                    load_e(0)
                    for cc in range(8):
                        s = cc % 2
                        if cc + 1 < 8:
                            load_e(cc + 1)
                        for g in range(4):
                            u = (cc * 4 + g) % 2
                            gsl = slice(g * 512, (g + 1) * 512)
                            for k in range(3):
                                for kc in range(8):
                                    mm(ps[:, k, :], wg[:, s, k, kc, :], xx[:, kc, gsl], kc == 0, kc == 7, [B_wg[s], B_xx[g]], [PB[k]])
                                act(SG[:, u, k, :], ps[:, k, :], AF.Sigmoid, [], [PB[k], B_sg[u]])
                            for kc in range(4):
                                mm(ps[:, 3, :], wbr[:, s, kc, :], hmT[:, kc, gsl], kc == 0, kc == 3, [B_wb[s], B_hm], [PB[3]])
                            for kc in range(8):
                                mm(ps[:, 4, :], wbr[:, s, 4 + kc, :], haT[:, kc, gsl], kc == 0, kc == 7, [B_wb[s], B_ha], [PB[4]])
                            for kc in range(4):
                                mm(ps[:, 5, :], wbr[:, s, 12 + kc, :], hcT[:, kc, gsl], kc == 0, kc == 3, [B_wb[s], B_hc], [PB[5]])
                            tt("dve", Mt[:, u, 0, :], ps[:, 3, :], SG[:, u, 0, :], ALU.mult, [B_sg[u]], [PB[3], B_M[u]])
                            tt("dve", Mt[:, u, 1, :], ps[:, 4, :], SG[:, u, 1, :], ALU.mult, [B_sg[u]], [PB[4], B_M[u]])
                            tt("dve", Mt[:, u, 0, :], Mt[:, u, 0, :], Mt[:, u, 1, :], ALU.add, [B_M[u]], [B_M[u]])
                            tt("dve", Mt[:, u, 1, :], ps[:, 5, :], SG[:, u, 2, :], ALU.mult, [B_sg[u]], [PB[5], B_M[u]])
                            tt("dve", mT[:, cc, gsl], Mt[:, u, 0, :], Mt[:, u, 1, :], ALU.add, [B_M[u]], [B_mT[g]])
                    S.emit()
                sb.set([(H0, H_END), (M_END, TOP)])
                wo = sb("e_wo", [128, 8, D], BF16)
                xt = sb("e_xt", [128, 2, D], F32)
                Z = sb("e_z", [128, 2, D], F32)
                ZH = sb("e_zh", [128, 2, D], F32)
                X1 = sb("e_x1", [128, 2, D], F32)
                est = sb("e_st", [128, 2, 16], F32)
                if True:
                    B_wo = Buf(); B_gb = Buf(); B_xt = [Buf(), Buf()]; B_z = [Buf(), Buf()]; B_zh = [Buf(), Buf()]; B_x1 = [Buf(), Buf()]
                    B_st = [Buf(), Buf()]
                    C_xt = slot_ctrs("ext")
                    C_x1 = slot_ctrs("ex1")
                    B_x1s = [Buf() for _ in range(NT)]
                    for q4 in range(4):
                        load_w(wo[:, 2 * q4:2 * q4 + 2, :], w_out[q4 * 256:(q4 + 1) * 256, :].rearrange("(kc p) n -> p kc n", p=128), 2, 1024, B_wo, ("wo", q4))
                    gbt = gbt_in
                    B_gb = B_const
                    def e2_a(t_):
                        s = t_ % 2
                        tsl = slice(t_ * 128, (t_ + 1) * 128)
                        dma(xt[:, s, :], xs[tsl, :], [], [B_xt[s]], C_xt[s])
                        b0 = 2 * s
                        for hf in range(2):
                            for kc in range(8):
                                mm(ps[:, b0 + hf, :], mT[:, kc, tsl], wo[:, kc, hf * 512:(hf + 1) * 512], kc == 0, False,
                                   [B_mT[t_ // 4], B_wo], [PB[b0 + hf]])
                            mm(ps[:, b0 + hf, :], ones_f[0:1, :], gbt[0:1, 1, hf * 512:(hf + 1) * 512], False, True, [B_const, B_gb], [PB[b0 + hf]])
                        act(xt[:, s, :], xt[:, s, :], AF.Identity, [B_xt[s], B_stats_in], [B_xt[s]], bias=stats_in[:, t_, 1:2], scale=stats_in[:, t_, 0:1])
                        tt("dve", xt[:, s, :], xt[:, s, :], gbt[:, 0, :], ALU.mult, [B_xt[s], B_gb], [B_xt[s]])

                    def e2_a2(t_):
                        s = t_ % 2
                        b0 = 2 * s
                        for hf in range(2):
                            tt("dve", Z[:, s, hf * 512:(hf + 1) * 512], ps[:, b0 + hf, :], xt[:, s, hf * 512:(hf + 1) * 512], ALU.add,
                               [B_xt[s]], [PB[b0 + hf], B_z[s]])
                        st = est[:, s, 0:12].rearrange("p (a b) -> p a b", b=6)
                        for i in range(2):
                            S.op("dve", lambda e, i=i: e.bn_stats(out=st[:, i, :], in_=Z[:, s, i * 512:(i + 1) * 512]), reads=[B_z[s]], writes=[B_st[s]])
                        S.op("dve", lambda e: e.bn_aggr(out=est[:, s, 12:14], in_=st), reads=[B_st[s]], writes=[B_st[s]])
                        ts("dve", est[:, s, 14:15], est[:, s, 13:14], EPS, None, ALU.add, None, [B_st[s]], [B_st[s]])
                        tt("pool", est[:, s, 14:15], est[:, s, 14:15], mhalf[:, 0:1], ALU.pow, [B_st[s], B_const], [B_st[s]])

                    def e2_b(t_):
                        s = t_ % 2
                        stt(est[:, s, 15:16], est[:, s, 12:13], -1.0, est[:, s, 14:15], ALU.mult, ALU.mult, [B_st[s]], [B_st[s]])
                        act(ZH[:, s, :], Z[:, s, :], AF.Identity, [B_z[s], B_st[s]], [B_zh[s]], bias=est[:, s, 15:16], scale=est[:, s, 14:15])

                    def e2_c(t_):
                        s = t_ % 2
                        tsl = slice(t_ * 128, (t_ + 1) * 128)
                        for kc in range(8):
                            bk = 4 + (kc // 4)
                            tr(ps[:, bk, (kc % 4) * 128:(kc % 4 + 1) * 128], ZH[:, s, kc * 128:(kc + 1) * 128], ident_f[:], [B_zh[s], B_const], [PB[bk]])
                        for kc in range(8):
                            bk = 4 + (kc // 4)
                            act(xx[:, kc, tsl], ps[:, bk, (kc % 4) * 128:(kc % 4 + 1) * 128], AF.Identity, [B_const], [PB[bk], B_xx[t_ // 4]],
                                bias=ln1_fm[:, 1, kc:kc + 1], scale=ln1_fm[:, 0, kc:kc + 1])
                        dma(x1s_d[tsl, :], ZH[:, s, :], [B_zh[s]], [B_x1s[t_]], C_x1[s])

                    pipeline(NT, [e2_a, e2_a2, e2_b, e2_c], oldest_first=True)
                    S.emit()


        sb.set([(H0, TOP)])
        wd = sb("f_wd", [128, 22, D], BF16)
        hT = sb("f_hT", [128, 22, 1024], BF16)
        wu = sb("f_wu", [128, 2, 2, 8, 128], BF16)
        Ab = sb("f_A", [128, 1024], BF16)
        halo = sb("f_halo", [128, 2, 4], F32)
        B_halo = [Buf(), Buf()]
        FPAIR = [0, 2, 5]
        gbt = sb("f_gb", [128, 4, D], F32)
        X1 = sb("f_x1", [128, 2, D], F32)
        Z = sb("f_z", [128, 2, D], F32)
        Y = Z
        fst = sb("f_st", [128, 2, 16], F32)
        if True:
            B_wd = Buf(); B_hT = Buf(); B_wu = [Buf(), Buf()]; B_pre = [Buf(), Buf()]; B_dg = [Buf(), Buf()]; B_A = [Buf(), Buf()]
            B_gb = Buf(); B_x1 = [Buf(), Buf()]; B_z = [Buf(), Buf()]; B_y = B_z; B_st = [Buf(), Buf()]
            C_x1 = slot_ctrs("fx1")
            C_y = slot_ctrs("fy")
            def load_u(j):
                s = j % 2
                load_w(wu[:, s, 0, :, :], w_up[:, j * 128:(j + 1) * 128].rearrange("(kc p) n -> p kc n", p=128), 8, 128, B_wu[s], ("wu", j, 0))
                load_w(wu[:, s, 1, :, :], w_up[:, DFF + j * 128:DFF + (j + 1) * 128].rearrange("(kc p) n -> p kc n", p=128), 8, 128, B_wu[s], ("wu", j, 1))

            load_u(0)
            B_gbl = [Buf() for _ in range(4)]
            for i, v_ in enumerate([ln2_g, ln2_b, ln1_g, ln1_b]):
                mdma(gbt[:, i, :], v_.partition_broadcast(128), [], [B_gbl[i]])
            ts("dve", gbt[:, 2:4, :], gbt[:, 2:4, :], ALPHA, None, ALU.mult, None, B_gbl, [B_gb])

            for half in range(2):
                h0 = half * 1024

                def f_U(j, half=half, h0=h0):
                    s = j % 2
                    if j + 1 < 22:
                        load_u(j + 1)
                    elif half == 0:
                        load_u(0)
                    if half == 0 and j % 2 == 0:
                        load_w(wd[:, j:j + 2, :], w_dn[j * 128:(j + 2) * 128, :].rearrange("(a p) n -> p a n", p=128), 2, 1024, B_wd, ("wd", j))
                    htok = 1024 if half == 0 else 1023
                    for part in range(2):
                        pb = FPAIR[(2 * j + part) % 3]
                        for g in range(2):
                            bk = pb + g
                            for kc in range(8):
                                mm(ps[:, bk, :], wu[:, s, part, kc, :], xx[:, kc, h0 + g * 512:h0 + (g + 1) * 512], kc == 0, kc == 7,
                                   [B_wu[s], B_xx[half * 2 + g]], [PB[bk]])
                        for kc in range(8):
                            mm(ps[:, 4, 2 * part:2 * part + 2], wu[:, s, part, kc, :], xx[:, kc, htok:htok + 2] if half == 0 else xx[:, kc, htok - 1:htok + 1],
                               kc == 0, kc == 7, [B_wu[s], B_xx[htok // 512], B_xx[(htok - 1) // 512]], [PB[4]])
                    cp("act", halo[:, s, :], ps[:, 4, 0:4], [], [PB[4], B_halo[s]])

                def f_V(j, half=half):
                    for part in range(2):
                        ch = part * 22 + j
                        w0 = fconv_fm[:, 0, ch:ch + 1]; w1 = fconv_fm[:, 1, ch:ch + 1]; w2 = fconv_fm[:, 2, ch:ch + 1]; bb = fconv_fm[:, 3, ch:ch + 1]
                        pb = FPAIR[(2 * j + part) % 3]
                        s = j % 2
                        v = ps[:, pb:pb + 2, :].rearrange("p a b -> p (a b)")
                        PBv = [PB[pb], PB[pb + 1]]
                        t0 = X1[:, part, :]; t1 = Z[:, part, :]
                        Bt0 = B_x1[part]; Bt1 = B_z[part]
                        hv = halo[:, s, 2 * part:2 * part + 1] if half == 0 else halo[:, s, 2 * part + 1:2 * part + 2]
                        act(t0, v, AF.Identity, [B_const], PBv + [Bt0], bias=bb, scale=w1)
                        stt(t1[:, 1:1024], v[:, 0:1023], w0, t0[:, 1:1024], ALU.mult, ALU.add, [B_const, Bt0], PBv + [Bt1])
                        if half == 0:
                            cp("dve", t1[:, 0:1], t0[:, 0:1], [Bt0], [Bt1])
                        else:
                            stt(t1[:, 0:1], hv, w0, t0[:, 0:1], ALU.mult, ALU.add, [B_const, Bt0, B_halo[s]], [Bt1])
                        stt(t0[:, 0:1023], v[:, 1:1024], w2, t1[:, 0:1023], ALU.mult, ALU.add, [B_const, Bt1], PBv + [Bt0])
                        if half == 0:
                            stt(t0[:, 1023:1024], hv, w2, t1[:, 1023:1024], ALU.mult, ALU.add, [B_const, Bt1, B_halo[s]], [Bt0])
                        else:
                            cp("dve", t0[:, 1023:1024], t1[:, 1023:1024], [Bt1], [Bt0])
                        if part == 0:
                            act(Ab[:], t0, AF.Gelu_apprx_tanh, [Bt0], [B_A[0]])
                        else:
                            tt("dve", hT[:, j, :], t0, Ab[:], ALU.mult, [Bt0, B_A[0]], [B_hT])

                for j in range(22):
                    f_U(j)
                    f_V(j)


                def f_a(t8, half=half):
                    t_ = half * 8 + t8
                    s = t_ % 2
                    tsl = slice(t_ * 128, (t_ + 1) * 128)
                    lsl = slice(t8 * 128, (t8 + 1) * 128)
                    dma(X1[:, s, :], x1s_d[tsl, :], [B_x1s[t_]], [B_x1[s]], C_x1[s])
                    b0 = 3 + 2 * s
                    for hf in range(2):
                        for j in range(22):
                            mm(ps[:, b0 + hf, :], hT[:, j, lsl], wd[:, j, hf * 512:(hf + 1) * 512], j == 0, False, [B_hT, B_wd], [PB[b0 + hf]])
                        mm(ps[:, b0 + hf, :], ones_f[0:1, :], gbt[0:1, 3, hf * 512:(hf + 1) * 512], False, True, [B_const, B_gb], [PB[b0 + hf]])
                    tt("dve", X1[:, s, :], X1[:, s, :], gbt[:, 2, :], ALU.mult, [B_x1[s], B_gb], [B_x1[s]])
                    for hf in range(2):
                        tt("dve", Z[:, s, hf * 512:(hf + 1) * 512], ps[:, b0 + hf, :], X1[:, s, hf * 512:(hf + 1) * 512], ALU.add,
                           [B_x1[s]], [PB[b0 + hf], B_z[s]])

                def f_b(t8, half=half):
                    t_ = half * 8 + t8
                    s = t_ % 2
                    tsl = slice(t_ * 128, (t_ + 1) * 128)
                    st = fst[:, s, 0:12].rearrange("p (a b) -> p a b", b=6)
                    layer_norm_stats(Z[:, s, :], st, fst[:, s, 12:14], fst[:, s, 14:15], fst[:, s, 15:16], B_z[s], B_st[s])
                    act(Y[:, s, :], Z[:, s, :], AF.Identity, [B_z[s], B_st[s]], [B_z[s]], bias=fst[:, s, 15:16], scale=fst[:, s, 14:15])
                    tt("dve", Y[:, s, :], Y[:, s, :], gbt[:, 0, :], ALU.mult, [B_y[s], B_gb], [B_y[s]])
                    tt("dve", Y[:, s, :], Y[:, s, :], gbt[:, 1, :], ALU.add, [B_y[s], B_gb], [B_y[s]])
                    dma(y_d[seq, tsl, :], Y[:, s, :], [B_y[s]], [], C_y[s])

                pipeline(8, [f_a, f_b], oldest_first=True)

            S.emit(final=(seq == nseq - 1))
    global LAST_MARKS
    LAST_MARKS = S.marks
    return nc


LAST_MARKS = None

def _consts():
    ident = np.eye(128, dtype=np.float32)
    s = np.arange(128)[:, None]
    t = np.arange(128)[None, :]
    maskF = (s <= t).astype(np.float32)
    maskB = (s >= t).astype(np.float32)
    tok = np.arange(T)
    row = (tok // 64).astype(np.float32)
    col = (tok % 64).astype(np.float32)
    inv = (np.float32(10000.0) ** (-np.arange(0, 64, 2, dtype=np.float32) / np.float32(64))).astype(np.float32)
    ar = (row[:, None] * inv[None, :]).astype(np.float32)
    ac = (col[:, None] * inv[None, :]).astype(np.float32)
    C = np.concatenate([np.cos(ar), np.cos(ar), np.cos(ac), np.cos(ac)], axis=1).astype(np.float32)
    Sg = np.concatenate([-np.sin(ar), np.sin(ar), -np.sin(ac), np.sin(ac)], axis=1).astype(np.float32)
    return {"c_ident": ident, "c_maskF": maskF, "c_maskB": maskB, "c_ropeC": C, "c_ropeS": Sg}


def _shared_inputs(inp):
    f = lambda a: np.ascontiguousarray(np.asarray(a, dtype=np.float32))
    d = {
        "w_in": f(inp["w_in"][0]), "w_mkv": f(inp["w_mem_kv"][0]), "w_bm": f(inp["w_branch_mlstm"][0]),
        "w_ba": f(inp["w_branch_attn"][0]), "w_bc": f(inp["w_branch_mem"][0]), "w_out": f(inp["w_out"][0]),
        "w_up": f(inp["w_ffn_up"][0]), "w_dn": f(inp["w_ffn_down"][0]),
        "lnin_g": f(inp["ln_in_g"]), "lnin_b": f(inp["ln_in_b"]),
        "ln1_g": f(inp["ln1_g"][0]), "ln1_b": f(inp["ln1_b"][0]), "ln2_g": f(inp["ln2_g"][0]), "ln2_b": f(inp["ln2_b"][0]),
        "gbias": f(inp["mlstm_gate_bias"][0]), "mconv_w": f(inp["mlstm_conv_w"][0]), "mconv_b": f(inp["mlstm_conv_b"][0]),
        "mnorm_g": f(inp["mlstm_norm_g"][0]), "qg": f(inp["attn_q_norm_g"][0]), "kg": f(inp["attn_k_norm_g"][0]),
        "fconv_w": f(inp["ffn_conv_w"][0]), "fconv_b": f(inp["ffn_conv_b"][0]),
    }
    d.update(_consts())
    return d


def kernel(**inp):
    ncores = 8
    xs = np.concatenate([np.asarray(inp["x_prompt"], np.float32), np.asarray(inp["x_sample"], np.float32)], axis=0)
    ms = np.concatenate([np.asarray(inp["mem_prompt"], np.float32), np.asarray(inp["mem_sample"], np.float32)], axis=0)
    nseq = xs.shape[0] // ncores
    shared = _shared_inputs(inp)
    nc = build(nseq)
    in_maps = []
    for c in range(ncores):
        m = dict(shared)
        m["x"] = np.ascontiguousarray(xs[c * nseq:(c + 1) * nseq])
        m["mem"] = np.ascontiguousarray(ms[c * nseq:(c + 1) * nseq])
        in_maps.append(m)
    res = run_bass_kernel_spmd(nc, in_maps, core_ids=list(range(ncores)))
    y = np.concatenate([np.asarray(r["y"], np.float32) for r in res.results], axis=0)
    nb = inp["x_prompt"].shape[0]
    return (y[:nb], y[nb:])
```

```python
import math
import numpy as np
import concourse.bass as bass
import concourse.mybir as mybir
from concourse.bass_utils import run_bass_kernel_spmd

F32 = mybir.dt.float32
BF16 = mybir.dt.bfloat16
AF = mybir.ActivationFunctionType
ALU = mybir.AluOpType
AX = mybir.AxisListType

T = 2048
D = 1024
NT = 16
DFF = 2816
EPS = 1e-5
ALPHA = 2.0 ** 0.25
O_QM, O_KM, O_VM, O_OM, O_GM, O_QA, O_KA, O_VA, O_QC, O_GB = 0, 512, 1024, 1536, 2048, 2064, 3088, 3344, 3600, 4112


class Ctr:
    __slots__ = ("sem", "step", "val", "name")

    def __init__(self, sem, step, name):
        self.sem, self.step, self.val, self.name = sem, step, 0, name


class Buf:
    __slots__ = ("name", "w", "r")

    def __init__(self, name=""):
        self.name = name
        self.w = None
        self.r = {}


class Sched:
    def __init__(self, nc):
        self.nc = nc
        self.names = ("pe", "act", "dve", "pool", "sp")
        self.q = {k: [] for k in self.names}
        self.ctr = {}
        for k in ("pe", "act", "dve", "pool"):
            self.ctr[k] = Ctr(nc.alloc_semaphore(name="s_" + k), 1, k)
        self.known = {k: {} for k in self.names}
        self.dma_ctrs = []
        self.marks = []

    def dma_ctr(self, name):
        c = Ctr(self.nc.alloc_semaphore(name="d_" + name), 16, name)
        self.dma_ctrs.append(c)
        return c

    def op(self, eng, fn, reads=(), writes=(), ctr=None):
        c = ctr if ctr is not None else self.ctr[eng]
        own = self.ctr.get(eng)
        need = {}

        def req(dep, raw):
            if dep is None:
                return
            dc, dv = dep
            if dc is own and not raw and eng == "pe":
                return
            if need.get(dc, 0) < dv:
                need[dc] = dv

        for b in reads:
            req(b.w, True)
        for b in writes:
            req(b.w, False)
            for dc, dv in b.r.items():
                req((dc, dv), False)
        kn = self.known[eng]
        waits = []
        for dc, dv in need.items():
            if kn.get(dc, 0) >= dv:
                continue
            kn[dc] = dv
            waits.append((dc, dv))
        c.val += c.step
        for b in reads:
            if b.r.get(c, 0) < c.val:
                b.r[c] = c.val
        for b in writes:
            b.w = (c, c.val)
            b.r = {}
        self.q[eng].append((waits, fn, c))

    def emit(self, final=False):
        nc = self.nc
        q = self.q
        self.marks.append({k: self.ctr[k].val for k in ("pe", "act", "dve", "pool")})
        self.q = {k: [] for k in self.names}
        with nc.Block(no_gpsimd_drain=True) as block:
            def body(name):
                def f(e):
                    for waits, fn, c in q[name]:
                        for dc, dv in waits:
                            e.wait_ge(dc.sem, dv)
                        fn(e).then_inc(c.sem, c.step)
                    if name == "sp":
                        for c in self.dma_ctrs:
                            if c.val:
                                e.wait_ge(c.sem, c.val)
                return f
            block.tensor(body("pe"))
            block.scalar(body("act"))
            block.vector(body("dve"))
            block.gpsimd(body("pool"))
            block.sync(body("sp"))


def build(nseq, dbg=False):
    nc = bass.Bass("TRN2", target_bir_lowering=False)
    S = Sched(nc)

    def din(name, shape):
        return nc.dram_tensor(name, list(shape), F32, kind="ExternalInput").ap()

    x_d = din("x", [nseq, T, D])
    mem_d = din("mem", [nseq, 256, D])
    w_in = din("w_in", [D, 7184])
    w_mkv = din("w_mkv", [D, 1024])
    w_bm = din("w_bm", [512, D])
    w_ba = din("w_ba", [1024, D])
    w_bc = din("w_bc", [512, D])
    w_out = din("w_out", [D, D])
    w_up = din("w_up", [D, 2 * DFF])
    w_dn = din("w_dn", [DFF, D])
    lnin_g = din("lnin_g", [D]); lnin_b = din("lnin_b", [D])
    ln1_g = din("ln1_g", [D]); ln1_b = din("ln1_b", [D])
    ln2_g = din("ln2_g", [D]); ln2_b = din("ln2_b", [D])
    gbias_d = din("gbias", [16])
    mconv_w = din("mconv_w", [3, 1024]); mconv_b = din("mconv_b", [1024])
    mnorm_g = din("mnorm_g", [512])
    qg_d = din("qg", [128]); kg_d = din("kg", [128])
    fconv_w = din("fconv_w", [3, 2 * DFF]); fconv_b = din("fconv_b", [2 * DFF])
    c_ident = din("c_ident", [128, 128])
    c_maskF = din("c_maskF", [128, 128])
    c_maskB = din("c_maskB", [128, 128])
    c_ropeC = din("c_ropeC", [T, 128])
    c_ropeS = din("c_ropeS", [T, 128])
    y_d = nc.dram_tensor("y", [nseq, T, D], F32, kind="ExternalOutput").ap()
    x1s_d = nc.dram_tensor("x1s", [T, D], F32, kind=("ExternalOutput" if dbg else "Internal")).ap()
    if dbg:
        dbg_h = nc.dram_tensor("dbg_h", [128, 16, T], BF16, kind="ExternalOutput").ap()

    class Arena:
        def __init__(self):
            self.segs = []
            self.n = 0

        def set(self, segs):
            self.segs = [[lo, hi] for lo, hi in segs]

        def __call__(self, name, shape, dt):
            nb = 2 if dt == BF16 else 4
            size = nb
            for s_ in shape[1:]:
                size *= s_
            size = (size + 31) // 32 * 32
            for sg in self.segs:
                if sg[0] + size <= sg[1]:
                    off = sg[0]
                    sg[0] += size
                    self.n += 1
                    return nc.alloc_sbuf_tensor_at("%s_%d" % (name, self.n), list(shape), dt, offset=off)
            raise RuntimeError("arena full: %s %s" % (name, shape))

    BASE = 16512
    TOP = 229344
    sb = Arena()
    sb.set([(BASE, TOP)])

    class _CM:
        def __init__(self, t):
            self.t = t

        def __enter__(self):
            return self.t

        def __exit__(self, *a):
            return False

    def sbt(name, shape, dt):
        return _CM(sb(name, shape, dt))

    def mm(out, lhsT, rhs, start, stop, R, W, sgc=False):
        if sgc:
            S.op("pe", lambda e: e.matmul(out, lhsT=lhsT, rhs=rhs, start=start, stop=stop, skip_group_check=True), reads=R, writes=W)
        else:
            S.op("pe", lambda e: e.matmul(out, lhsT=lhsT, rhs=rhs, start=start, stop=stop), reads=R, writes=W)

    def tr(out, in_, ident, R, W):
        S.op("pe", lambda e: e.transpose(out=out, in_=in_, identity=ident), reads=R, writes=W)

    def act(out, in_, func, R, W, bias=None, scale=None, accum=None):
        kw = {}
        if bias is not None:
            kw["bias"] = bias
        if scale is not None:
            kw["scale"] = scale
        if accum is not None:
            kw["accum_out"] = accum
        S.op("act", lambda e: e.activation(out=out, in_=in_, func=func, **kw), reads=R, writes=W)

    def ts(eng, out, in0, s1, s2, op0, op1, R, W):
        if op1 is None:
            S.op(eng, lambda e: e.tensor_scalar(out=out, in0=in0, scalar1=s1, scalar2=None, op0=op0), reads=R, writes=W)
        else:
            S.op(eng, lambda e: e.tensor_scalar(out=out, in0=in0, scalar1=s1, scalar2=s2, op0=op0, op1=op1), reads=R, writes=W)

    def tt(eng, out, in0, in1, op, R, W):
        S.op(eng, lambda e: e.tensor_tensor(out=out, in0=in0, in1=in1, op=op), reads=R, writes=W)

    def stt(out, in0, scalar, in1, op0, op1, R, W):
        S.op("dve", lambda e: e.scalar_tensor_tensor(out=out, in0=in0, scalar=scalar, in1=in1, op0=op0, op1=op1), reads=R, writes=W)

    def cp(eng, out, in_, R, W):
        if eng == "act":
            S.op("act", lambda e: e.copy(out=out, in_=in_), reads=R, writes=W)
        else:
            S.op(eng, lambda e: e.tensor_copy(out=out, in_=in_), reads=R, writes=W)

    def recip(out, in_, R, W):
        S.op("dve", lambda e: e.reciprocal(out=out, in_=in_), reads=R, writes=W)

    def memset(eng, ap, val, W):
        S.op(eng, lambda e: e.memset(ap, val), writes=W)

    def mdma(out, in_, R, W):
        k = mc_i[0] % NMC
        mc_i[0] += 1
        dma(out, in_, R, list(W) + [B_mc[k]], C_mc[k])

    def dma_q2(out, in_, R, W, ctr):
        S.op("pool", lambda e: e.dma_start(out=out, in_=in_), reads=R, writes=W, ctr=ctr)

    def dma(out, in_, R, W, ctr, slow=False):
        if slow:
            S.op("sp", lambda e: e.dma_start(out=out, in_=in_, allow_slow_non_contiguous=True), reads=R, writes=W, ctr=ctr)
        else:
            S.op("sp", lambda e: e.dma_start(out=out, in_=in_), reads=R, writes=W, ctr=ctr)

    def pipeline(n, stages, oldest_first=True, order=None):
        k = len(stages)
        for it in range(n + k - 1):
            if order is None:
                order = list(reversed(range(k))) if oldest_first else list(range(k))
            for si in order:
                t_ = it - si
                if 0 <= t_ < n:
                    stages[si](t_)

    ps = nc.alloc_psum_tensor("ps", [128, 7, 512], F32)
    psb = nc.alloc_psum_tensor("psb", [128, 1024], BF16)
    PB = [Buf("ps%d" % i) for i in range(7)]
    PBB = Buf("psb")

    ident_f = sb("ident_f", [128, 128], F32)
    ident_b = sb("ident_b", [128, 128], BF16)
    maskF = sb("maskF", [128, 128], F32)
    maskB = sb("maskB", [128, 128], F32)
    ones_f = sb("ones_f", [128, 128], F32)
    B_const = Buf("const")
    lnin_fm = sb("lnin_fm", [128, 2, 8], F32)
    ln1_fm = sb("ln1_fm", [128, 2, 8], F32)
    mconv_fm = sb("mconv_fm", [128, 4, 8], F32)
    fconv_fm = sb("fconv_fm", [128, 4, 44], F32)
    negc = sb("negc", [128, 1], F32)
    ctmp = sb("ctmp", [128, 4], F32)
    mhalf = sb("mhalf", [128, 16], F32)
    gbt_in = sb("gbt_in", [128, 2, D], F32)

    NSTG = 3
    stg = [sb("stg%d" % i, [128, 2048], F32) for i in range(NSTG)]
    B_stg = [Buf("stg%d" % i) for i in range(NSTG)]
    C_stg = [S.dma_ctr("stg%d" % i) for i in range(NSTG)]
    stg_i = [0]

    xx = sb("xx", [128, 8, T], BF16)
    B_xx = [Buf("xx%d" % g) for g in range(4)]
    stats_in = sb("stats_in", [128, NT, 2], F32)
    B_stats_in = Buf("stats_in")

    c_misc = S.dma_ctr("misc")
    NMC = 8
    C_mc = [S.dma_ctr("mc%d" % i) for i in range(NMC)]
    B_mc = [Buf("mc%d" % i) for i in range(NMC)]
    mc_i = [0]
    C_slots = {}

    def slot_ctrs(name):
        if name not in C_slots:
            C_slots[name] = [S.dma_ctr(name + "0"), S.dma_ctr(name + "1")]
        return C_slots[name]
    P0_END = sb.segs[0][0]
    H0 = P0_END
    hmT = nc.alloc_sbuf_tensor_at("hmT", [128, 4, T], BF16, offset=H0)
    hcT = nc.alloc_sbuf_tensor_at("hcT", [128, 4, T], BF16, offset=H0 + 16384)
    haT = nc.alloc_sbuf_tensor_at("haT", [128, 8, T], BF16, offset=H0 + 32768)
    H_END = H0 + 65536
    M0 = H_END
    mT = nc.alloc_sbuf_tensor_at("mT", [128, 8, T], BF16, offset=M0)
    M_END = M0 + 32768
    assert M_END + 40 * 1024 < TOP, (P0_END, M_END, TOP)
    sb.set([(P0_END, TOP)])

    WS_TOTAL = 160 * 1024
    wscr = nc.dram_tensor("wscr", [128, WS_TOTAL], BF16, kind="Internal").ap()
    wcache = {}
    ws_off = [0]
    NWC = 6
    C_wc = [S.dma_ctr("wc%d" % i) for i in range(NWC)]
    B_wc = [Buf("wc%d" % i) for i in range(NWC)]
    wc_i = [0]

    def load_w(dst, src, n_a, n_b, B_dst, key):
        L = n_a * n_b
        k = wc_i[0] % NWC
        wc_i[0] += 1
        if key in wcache:
            off, Bc = wcache[key]
            dma(dst, wscr[:, off:off + L].rearrange("p (a b) -> p a b", b=n_b), [Bc], [B_dst, B_wc[k]], C_wc[k])
            return
        i = stg_i[0] % NSTG
        stg_i[0] += 1
        sv = stg[i][:, 0:L].rearrange("p (a b) -> p a b", b=n_b)
        dma(sv, src, [], [B_stg[i]], C_stg[i])
        cp("act" if i % 3 != 2 else "dve", dst, sv, [B_stg[i]], [B_dst])
        off = ws_off[0]
        ws_off[0] += L
        assert ws_off[0] <= WS_TOTAL
        Bc = Buf("wscr")
        wcache[key] = (off, Bc)
        dma(wscr[:, off:off + L].rearrange("p (a b) -> p a b", b=n_b), dst, [B_dst], [Bc, B_wc[k]], C_wc[k])

    def win_cols(c0, n):
        return w_in[:, c0:c0 + n].rearrange("(kc p) n -> p kc n", p=128)

    dma(ident_f[:], c_ident, [], [B_const], c_misc)
    mdma(maskF[:], c_maskF, [], [B_const])
    mdma(maskB[:], c_maskB, [], [B_const])
    cm = S.dma_ctr("m3")
    dma(lnin_fm[:, 0, :], lnin_g.rearrange("(kc p) -> p kc", p=128), [], [B_const], cm, slow=True)
    for (dst, src) in [(lnin_fm[:, 1, :], lnin_b.rearrange("(kc p) -> p kc", p=128)),
                       (ln1_fm[:, 0, :], ln1_g.rearrange("(kc p) -> p kc", p=128)),
                       (ln1_fm[:, 1, :], ln1_b.rearrange("(kc p) -> p kc", p=128)),
                       (mconv_fm[:, 3, :], mconv_b.rearrange("(kc p) -> p kc", p=128)),
                       (fconv_fm[:, 3, :], fconv_b.rearrange("(kc p) -> p kc", p=128))]:
        dma(dst, src, [], [B_const], cm, slow=True)
    for j in range(3):
        dma(mconv_fm[:, j, :], mconv_w[j, :].rearrange("(kc p) -> p kc", p=128), [], [B_const], cm, slow=True)
        dma(fconv_fm[:, j, :], fconv_w[j, :].rearrange("(kc p) -> p kc", p=128), [], [B_const], cm, slow=True)
    cp("pool", ident_b[:], ident_f[:], [B_const], [B_const])
    memset("pool", ones_f[:], 1.0, [B_const])
    memset("pool", mhalf[:], -0.5, [B_const])
    for i_, v_ in enumerate([lnin_g, lnin_b]):
        mdma(gbt_in[:, i_, :], v_.partition_broadcast(128), [], [B_const])
    ts("dve", gbt_in[:], gbt_in[:], ALPHA, None, ALU.mult, None, [B_const], [B_const])
    memset("pool", ctmp[:, 2:3], -0.5 * math.log(128.0), [B_const])
    gtmp = sb("gtmp", [128, 2, 128], F32)
    if True:
        Bg = Buf("gtmp")
        mdma(gtmp[:, 0, :], qg_d.partition_broadcast(128), [], [Bg])
        mdma(gtmp[:, 1, :], kg_d.partition_broadcast(128), [], [Bg])
        S.op("dve", lambda e: e.tensor_reduce(out=ctmp[:, 0:2], in_=gtmp[:], axis=AX.X, op=ALU.max, apply_absolute_value=True),
             reads=[Bg], writes=[B_const])
        stt(negc[:], ctmp[:, 0:1], -math.sqrt(128.0), ctmp[:, 1:2], ALU.mult, ALU.mult, [B_const], [B_const])
        S.emit()

    def layer_norm_stats(src_ap, st, mv, rstd, nmr, Bsrc, Bst):
        for i in range(2):
            S.op("dve", lambda e, i=i: e.bn_stats(out=st[:, i, :], in_=src_ap[:, i * 512:(i + 1) * 512]), reads=[Bsrc], writes=[Bst])
        S.op("dve", lambda e: e.bn_aggr(out=mv, in_=st), reads=[Bst], writes=[Bst])
        ts("dve", rstd, mv[:, 1:2], EPS, None, ALU.add, None, [Bst], [Bst])
        tt("pool", rstd, rstd, mhalf[:, 0:1], ALU.pow, [Bst, B_const], [Bst])
        stt(nmr, mv[:, 0:1], -1.0, rstd, ALU.mult, ALU.mult, [Bst], [Bst])

    for seq in range(nseq):
        xs = x_d[seq]
        sb.set([(P0_END, TOP)])
        xt = sb("a_xt", [128, 2, D], F32)
        xh = sb("a_xh", [128, 2, D], F32)
        stt_t = sb("a_st", [128, 2, 16], F32)
        if True:
            B_xt = [Buf(), Buf()]; B_xh = [Buf(), Buf()]; B_st = [Buf(), Buf()]
            C_xt = slot_ctrs("axt")
            def a_1(t_):
                s = t_ % 2
                dma(xt[:, s, :], xs[t_ * 128:(t_ + 1) * 128, :], [], [B_xt[s]], C_xt[s])
                st = stt_t[:, s, 0:12].rearrange("p (a b) -> p a b", b=6)
                mv = stt_t[:, s, 12:14]
                for i in range(2):
                    S.op("dve", lambda e, i=i: e.bn_stats(out=st[:, i, :], in_=xt[:, s, i * 512:(i + 1) * 512]), reads=[B_xt[s]], writes=[B_st[s]])
                S.op("dve", lambda e: e.bn_aggr(out=mv, in_=st), reads=[B_st[s]], writes=[B_st[s]])
                ts("dve", stats_in[:, t_, 0:1], mv[:, 1:2], EPS, None, ALU.add, None, [B_st[s]], [B_st[s], B_stats_in])
                tt("pool", stats_in[:, t_, 0:1], stats_in[:, t_, 0:1], mhalf[:, 0:1], ALU.pow, [B_st[s], B_const], [B_st[s], B_stats_in])

            def a_2(t_):
                s = t_ % 2
                mv = stt_t[:, s, 12:14]
                rstd = stats_in[:, t_, 0:1]
                nmr = stats_in[:, t_, 1:2]
                stt(nmr, mv[:, 0:1], -1.0, rstd, ALU.mult, ALU.mult, [B_st[s]], [B_st[s], B_stats_in])
                act(xh[:, s, :], xt[:, s, :], AF.Identity, [B_xt[s], B_st[s]], [B_xh[s]], bias=nmr, scale=rstd)

            def a_3(t_):
                s = t_ % 2
                b0 = 2 * s
                for kc in range(8):
                    tr(ps[:, b0 + kc // 4, (kc % 4) * 128:(kc % 4 + 1) * 128], xh[:, s, kc * 128:(kc + 1) * 128], ident_f[:],
                       [B_xh[s], B_const], [PB[b0 + kc // 4]])
                for kc in range(8):
                    if kc % 2 == 0:
                        ts("dve", xx[:, kc, t_ * 128:(t_ + 1) * 128], ps[:, b0 + kc // 4, (kc % 4) * 128:(kc % 4 + 1) * 128],
                           lnin_fm[:, 0, kc:kc + 1], lnin_fm[:, 1, kc:kc + 1], ALU.mult, ALU.add,
                           [B_const], [PB[b0 + kc // 4], B_xx[t_ // 4]])
                    else:
                        act(xx[:, kc, t_ * 128:(t_ + 1) * 128], ps[:, b0 + kc // 4, (kc % 4) * 128:(kc % 4 + 1) * 128], AF.Identity,
                            [B_const], [PB[b0 + kc // 4], B_xx[t_ // 4]], bias=lnin_fm[:, 1, kc:kc + 1], scale=lnin_fm[:, 0, kc:kc + 1])

            pipeline(NT, [a_1, a_2, a_3], oldest_first=True)

            S.emit()

        if True:
            B_hm = Buf("hm"); B_ha = Buf("ha"); B_hc = Buf("hc")

            sb.set([(H0 + 16384, TOP)])
            gw = sb("b_gw", [128, 8, 16], BF16)
            gbB = sb("b_gb", [128, 16], F32)
            GT = sb("b_GT", [128, 16, 16], F32)
            SPt = sb("b_SP", [128, 16, 8], F32)
            UU = sb("b_U", [128, 4, 16, 8], F32)
            gtm = sb("b_tmp", [128, 16, 8], F32)
            mgB = sb("b_mg", [128, 512], F32)
            if True:
                B_gw = Buf(); B_g = Buf("gates")
                load_w(gw[:], win_cols(O_GM, 16), 8, 16, B_gw, ("gw",))
                mdma(gbB[:], gbias_d.partition_broadcast(128), [], [B_g])
                mdma(mgB[:], mnorm_g.partition_broadcast(128), [], [B_g])
                for t_ in range(NT):
                    for kc in range(8):
                        mm(ps[:, 0, t_ * 16:(t_ + 1) * 16], xx[:, kc, t_ * 128:(t_ + 1) * 128], gw[:, kc, :], kc == 0, kc == 7,
                           [B_xx[t_ // 4], B_gw], [PB[0]])
                tt("dve", GT[:], ps[:, 0, 0:256].rearrange("p (a b) -> p a b", b=16), gbB[:].unsqueeze(1).broadcast_to([128, 16, 16]),
                   ALU.add, [B_g], [PB[0], B_g])
                GT5 = GT[:].rearrange("p c (d k h) -> p c d k h", d=2, k=2, h=4)
                SP4 = SPt[:].rearrange("p c (d h) -> p c d h", d=2)
                act(SP4, GT5[:, :, :, 1, :], AF.Exp, [B_g], [B_g], scale=-1.0)
                act(SPt[:], SPt[:], AF.Ln, [B_g], [B_g], bias=1.0)
                mm(ps[:, 1, 0:64], maskF[:], SP4[:, :, 0, :], True, True, [B_const, B_g], [PB[1]])
                mm(ps[:, 1, 64:128], maskB[:], SP4[:, :, 1, :], True, True, [B_const, B_g], [PB[1]])
                mm(ps[:, 1, 128:256], ones_f[:], SPt[:], True, True, [B_const, B_g], [PB[1]])
                U4 = UU[:].rearrange("p k c (d h) -> p k c d h", d=2)
                g4 = gtm[:].rearrange("p c (d h) -> p c d h", d=2)
                for d_ in range(2):
                    tt("dve", g4[:, :, d_, :], ps[:, 1, d_ * 64:(d_ + 1) * 64].rearrange("p (c h) -> p c h", h=4), GT5[:, :, d_, 0, :],
                       ALU.add, [B_g], [PB[1], B_g])
                    act(U4[:, 2, :, d_, :], ps[:, 1, d_ * 64:(d_ + 1) * 64].rearrange("p (c h) -> p c h", h=4), AF.Exp, [], [PB[1], B_g])
                act(UU[:, 0, :, :], gtm[:], AF.Exp, [B_g], [B_g], bias=ctmp[:, 2:3])
                tt("dve", gtm[:], gtm[:], ps[:, 1, 128:256].rearrange("p (c k) -> p c k", k=8), ALU.subtract, [B_g], [PB[1], B_g])
                act(UU[:, 1, :, :], gtm[:], AF.Exp, [B_g], [B_g], bias=ctmp[:, 2:3])
                act(UU[:, 3, :, :], ps[:, 1, 128:256].rearrange("p (c k) -> p c k", k=8), AF.Exp, [], [PB[1], B_g], scale=-1.0)

                mark_b = [list(s_) for s_ in sb.segs]
                B_w = [Buf() for _ in range(4)]; B_pre = [Buf(), Buf()]; B_dg = Buf(); B_qk = [Buf(), Buf()]
                B_va = Buf(); B_kw = Buf(); B_cm = [[Buf(), Buf()], [Buf(), Buf()]]; B_cb = [[Buf(), Buf()], [Buf(), Buf()]]
                B_at = [Buf(), Buf()]
                wb_ = sb("b_w", [128, 4, 8, 128], BF16)
                pre = sb("b_pre", [128, 2, 2050], BF16)
                dg = sb("b_dg", [128, 6, 128], BF16)
                qkT = sb("b_qk", [128, 2, T], BF16)
                VA = sb("b_va", [128, 16, 129], BF16)
                KW = sb("b_kw", [128, 16, 2, 128], BF16)
                CM = sb("b_cm", [128, 2, 2, 129], F32)
                CB = sb("b_cb", [128, 2, 2, 129], BF16)
                AT = sb("b_at", [128, 2, 16, 128], BF16)
                OS_l = [sb("b_os", [128, 16, 128], F32) for _ in range(2)]
                RAW_l = [sb("b_raw", [128, 2, 16, 129], F32) for _ in range(2)]
                sm2_l = [sb("b_sm2", [128, 2, 2, 16, 1], F32) for _ in range(2)]
                lnst_l = [sb("b_ln", [128, 16, 8], F32) for _ in range(2)]
                HB_l = [sb("b_hb", [128, 16, 128], BF16) for _ in range(2)]
                B_os_l = [Buf(), Buf()]; B_raw_l = [[Buf(), Buf()], [Buf(), Buf()]]; B_sm2_l = [Buf(), Buf()]; B_ln_l = [Buf(), Buf()]; B_hb_l = [Buf(), Buf()]

                def head_P(h):
                    OS = OS_l[h % 2]; RAW = RAW_l[h % 2]; sm2 = sm2_l[h % 2]; lnst = lnst_l[h % 2]; HB = HB_l[h % 2]
                    B_os = B_os_l[h % 2]; B_raw = B_raw_l[h % 2]; B_sm2 = B_sm2_l[h % 2]; B_ln = B_ln_l[h % 2]; B_hb = B_hb_l[h % 2]
                    for i, o in enumerate([O_QM, O_KM, O_VM, O_OM]):
                        load_w(wb_[:, i, :, :], win_cols(o + h * 128, 128), 8, 128, B_w[i], ("bw", h, i))
                    if h == 0:
                        memset("pool", pre[:], 0.0, B_pre)
                        memset("pool", VA[:, :, 128:129], 1.0, [B_va])
                    for i in range(2):
                        for j in range(3):
                            act(dg[:, i * 3 + j, :], ident_f[:], AF.Identity, [B_const], [B_dg], scale=mconv_fm[:, j, i * 4 + h:i * 4 + h + 1])
                    for i in range(2):
                        for g in range(4):
                            bk = 2 + (i * 4 + g) % 2
                            for kc in range(8):
                                mm(ps[:, bk, :], wb_[:, i, kc, :], xx[:, kc, g * 512:(g + 1) * 512], kc == 0, kc == 7,
                                   [B_w[i], B_xx[g]], [PB[bk]])
                            cp("act", pre[:, i, 1 + g * 512:1 + (g + 1) * 512], ps[:, bk, :], [], [PB[bk], B_pre[i]])
                    for i in (2, 3):
                        for g in range(4):
                            bk = 0 + (i * 4 + g) % 2
                            for t4 in range(4):
                                t_ = g * 4 + t4
                                for kc in range(8):
                                    mm(ps[:, bk, t4 * 128:(t4 + 1) * 128], xx[:, kc, t_ * 128:(t_ + 1) * 128], wb_[:, i, kc, :], kc == 0, kc == 7,
                                       [B_xx[g], B_w[i]], [PB[bk]])
                            src = ps[:, bk, :].rearrange("p (a b) -> p a b", b=128)
                            if i == 2:
                                cp("act", VA[:, g * 4:(g + 1) * 4, 0:128], src, [], [PB[bk], B_va])
                            else:
                                act(OS[:, g * 4:(g + 1) * 4, :], src, AF.Sigmoid, [], [PB[bk], B_os])
                    for i in range(2):
                        for g in range(4):
                            bk = 4 + (i * 4 + g) % 2
                            for j in range(3):
                                mm(ps[:, bk, :], dg[:, i * 3 + j, :], pre[:, i, g * 512 + j:g * 512 + j + 512], j == 0, j == 2,
                                   [B_dg, B_pre[i]], [PB[bk]])
                            act(qkT[:, i, g * 512:(g + 1) * 512], ps[:, bk, :], AF.Silu, [B_const], [PB[bk], B_qk[i]],
                                bias=mconv_fm[:, 3, i * 4 + h:i * 4 + h + 1])
                    for c8 in range(2):
                        for c in range(c8 * 8, c8 * 8 + 8):
                            tr(psb[:, (c % 8) * 128:(c % 8 + 1) * 128], qkT[:, 1, c * 128:(c + 1) * 128], ident_b[:], [B_qk[1], B_const], [PBB])
                        for c in range(c8 * 8, c8 * 8 + 8):
                            for d_ in range(2):
                                act(KW[:, c, d_, :], psb[:, (c % 8) * 128:(c % 8 + 1) * 128], AF.Identity, [B_g], [PBB, B_kw],
                                    scale=UU[:, 1, c, d_ * 4 + h:d_ * 4 + h + 1])

                def head_S(h):
                    OS = OS_l[h % 2]; RAW = RAW_l[h % 2]; sm2 = sm2_l[h % 2]; lnst = lnst_l[h % 2]; HB = HB_l[h % 2]
                    B_os = B_os_l[h % 2]; B_raw = B_raw_l[h % 2]; B_sm2 = B_sm2_l[h % 2]; B_ln = B_ln_l[h % 2]; B_hb = B_hb_l[h % 2]
                    for c in range(16):
                        sl = slice(c * 128, (c + 1) * 128)
                        pS = c % 2
                        mm(ps[:, pS, 0:128], qkT[:, 1, sl], qkT[:, 0, sl], True, True, [B_qk[0], B_qk[1]], [PB[pS]])
                        for d_ in range(2):
                            stt(AT[:, d_, c, :], ps[:, pS, 0:128], UU[:, 0, c, d_ * 4 + h:d_ * 4 + h + 1], (maskF if d_ == 0 else maskB)[:], ALU.mult, ALU.mult,
                                [B_g, B_const], [PB[pS], B_at[d_]])
                    for d_ in range(2):
                        memset("pool", CM[:, d_, 1, :], 0.0, [B_cm[d_][1]])
                    for ci in range(16):
                        for d_ in range(2):
                            c = ci if d_ == 0 else 15 - ci
                            dh = d_ * 4 + h
                            sl = slice(c * 128, (c + 1) * 128)
                            pO = (2 * ci + d_) % 2
                            pK = 2 + 2 * d_ + ci % 2
                            par = ci % 2
                            if ci < 15:
                                mm(ps[:, pK, 0:129], KW[:, c, d_, :], VA[:, c, :], True, True, [B_kw, B_va], [PB[pK]])
                            mm(ps[:, pO, 0:129], AT[:, d_, c, :], VA[:, c, :], True, ci == 0, [B_at[d_], B_va], [PB[pO]])
                            if ci > 0:
                                mm(ps[:, pO, 0:129], qkT[:, 0, sl], CB[:, d_, par ^ 1, :], False, True, [B_qk[0], B_cb[d_][par ^ 1]], [PB[pO]])
                            cp("act", RAW[:, d_, c, :], ps[:, pO, 0:129], [], [PB[pO], B_raw[d_]])
                            if ci < 15:
                                stt(CM[:, d_, par, :], CM[:, d_, par ^ 1, :], UU[:, 3, c, dh:dh + 1], ps[:, pK, 0:129], ALU.mult, ALU.add,
                                    [B_g, B_cm[d_][par ^ 1]], [PB[pK], B_cm[d_][par]])
                                cp("act", CB[:, d_, par, :], CM[:, d_, par, :], [B_cm[d_][par]], [B_cb[d_][par]])


                def head_T(h):
                    OS = OS_l[h % 2]; RAW = RAW_l[h % 2]; sm2 = sm2_l[h % 2]; lnst = lnst_l[h % 2]; HB = HB_l[h % 2]
                    B_os = B_os_l[h % 2]; B_raw = B_raw_l[h % 2]; B_sm2 = B_sm2_l[h % 2]; B_ln = B_ln_l[h % 2]; B_hb = B_hb_l[h % 2]
                    den = RAW[:, :, :, 128:129]
                    ENv = UU[:, 2, :, :].rearrange("p c (d k) -> p d c k", d=2)[:, :, :, h:h + 1]
                    tt("dve", sm2[:, 0, :, :, :], den, ENv, ALU.max, B_raw + [B_g], [B_sm2])
                    stt(sm2[:, 1, :, :, :], den, -1.0, sm2[:, 0, :, :, :], ALU.mult, ALU.max, B_raw + [B_sm2], [B_sm2])
                    recip(sm2[:, 0, :, :, :], sm2[:, 1, :, :, :], [B_sm2], [B_sm2])
                    for d_ in range(2):
                        tt("dve", RAW[:, d_, :, 0:128], RAW[:, d_, :, 0:128], sm2[:, 0, d_, :, :].broadcast_to([128, 16, 128]), ALU.mult,
                           [B_raw[d_], B_sm2], [B_raw[d_]])
                    H = RAW[:, 0, :, 0:128]
                    tt("dve", H, H, RAW[:, 1, :, 0:128], ALU.add, B_raw, [B_raw[0]])
                    B_H = [B_raw[0]] * 16

                    for c in range(16):
                        S.op("dve", lambda e, c=c: e.bn_stats(out=lnst[:, c, 0:6], in_=H[:, c, :]), reads=[B_H[c]], writes=[B_ln])
                    for c in range(16):
                        S.op("dve", lambda e, c=c: e.bn_aggr(out=lnst[:, c, 6:8], in_=lnst[:, c, 0:6]), reads=[B_ln], writes=[B_ln])
                    ts("dve", lnst[:, :, 7:8], lnst[:, :, 7:8], EPS, None, ALU.add, None, [B_ln], [B_ln])
                    tt("pool", lnst[:, :, 7:8], lnst[:, :, 7:8], mhalf[:, 0:16].unsqueeze(2), ALU.pow, [B_ln, B_const], [B_ln])
                    for c in range(16):
                        ts("dve", H[:, c, :], H[:, c, :], lnst[:, c, 6:7], lnst[:, c, 7:8], ALU.subtract, ALU.mult, [B_ln, B_H[c]], [B_H[c]])
                    tt("dve", H, H, mgB[:, h * 128:(h + 1) * 128].unsqueeze(1).broadcast_to([128, 16, 128]), ALU.mult, [B_raw[0], B_g], [B_raw[0]])
                    tt("dve", HB[:], H, OS[:], ALU.mult, [B_raw[0], B_os], [B_hb])
                    for c in range(16):
                        tr(psb[:, (c % 8) * 128:(c % 8 + 1) * 128], HB[:, c, :], ident_b[:], [B_hb, B_const], [PBB])
                        if c % 8 == 7:
                            cp("act", hmT[:, h, (c - 7) * 128:(c + 1) * 128], psb[:], [], [PBB, B_hm])

                for h in range(5):
                    if h < 4:
                        head_P(h)
                    if h >= 1:
                        head_T(h - 1)
                    if h < 4:
                        head_S(h)
                S.emit()

            if True:
                for hk in range(2):
                    sb.set([(H_END, TOP)])
                    QT = sb("c_QT", [128, 4, T], BF16); KT = sb("c_KT", [128, T], BF16); VA = sb("c_VA", [128, 16, 129], BF16)
                    mark_c = [list(s_) for s_ in sb.segs]
                    rope = sb("c_rope", [128, 2, 16, 128], F32); gqk = sb("c_g", [128, 5, 128], F32)
                    wq = sb("c_wq", [128, 2, 8, 256], BF16); wkv = sb("c_wkv", [128, 2, 8, 128], BF16)
                    XQ = sb("c_XQ", [128, 2, 5, 128], F32); R1 = sb("c_R1", [128, 2, 5, 128], F32); R2 = sb("c_R2", [128, 2, 5, 128], F32)
                    QR = sb("c_QR", [128, 2, 5, 128], BF16); ss = sb("c_ss", [128, 2, 8], F32)
                    B_rope = [Buf() for _ in range(7)]
                    if True:
                        B_wq = Buf(); B_wkv = Buf(); B_QT = Buf(); B_KT = Buf(); B_VA = Buf()
                        B_XQ = [Buf(), Buf()]; B_R1 = [Buf(), Buf()]; B_R2 = [Buf(), Buf()]; B_QR = [Buf(), Buf()]; B_ss = [Buf(), Buf()]
                        B_PT = [Buf(), Buf()]; B_HA = [Buf(), Buf()]; B_rr = [Buf(), Buf()]
                        load_w(wq[:, 0, :, :], win_cols(O_QA + hk * 512, 256), 8, 256, B_wq, ("cq", hk, 0))
                        load_w(wq[:, 1, :, :], win_cols(O_QA + hk * 512 + 256, 256), 8, 256, B_wq, ("cq", hk, 1))
                        load_w(wkv[:, 0, :, :], win_cols(O_KA + hk * 128, 128), 8, 128, B_wkv, ("ck", hk))
                        load_w(wkv[:, 1, :, :], win_cols(O_VA + hk * 128, 128), 8, 128, B_wkv, ("cv", hk))
                        for i_ in range(5):
                            mdma(gqk[:, i_, :], (qg_d if i_ < 4 else kg_d).partition_broadcast(128), [], [B_rope[i_]])
                        mdma(rope[:, 0, :, :], c_ropeC.rearrange("(t p) d -> p t d", p=128), [], [B_rope[5]])
                        mdma(rope[:, 1, :, :], c_ropeS.rearrange("(t p) d -> p t d", p=128), [], [B_rope[6]])
                        memset("pool", VA[:, :, 128:129], 1.0, [B_VA])
                        def c1_A(t_):
                            s = t_ % 2
                            bq = 0 + s
                            bkv = 2 + s
                            tsl = slice(t_ * 128, (t_ + 1) * 128)
                            for hf in range(2):
                                for kc in range(8):
                                    mm(ps[:, bq, hf * 256:(hf + 1) * 256], xx[:, kc, tsl], wq[:, hf, kc, :], kc == 0, kc == 7,
                                       [B_xx[t_ // 4], B_wq], [PB[bq]])
                            for i in range(2):
                                for kc in range(8):
                                    mm(ps[:, bkv, i * 128:(i + 1) * 128], xx[:, kc, tsl], wkv[:, i, kc, :], kc == 0, kc == 7,
                                       [B_xx[t_ // 4], B_wkv], [PB[bkv]])
                            cp("act", XQ[:, s, 0:4, :], ps[:, bq, :].rearrange("p (a b) -> p a b", b=128), [], [PB[bq], B_XQ[s]])
                            cp("act", XQ[:, s, 4, :], ps[:, bkv, 0:128], [], [PB[bkv], B_XQ[s]])
                            cp("act", VA[:, t_, 0:128], ps[:, bkv, 128:256], [], [PB[bkv], B_VA])
                            for i in range(5):
                                src_ = ps[:, bq, i * 128:(i + 1) * 128] if i < 4 else ps[:, bkv, 0:128]
                                act(R2[:, s, i, :], src_, AF.Square, [], [PB[bq] if i < 4 else PB[bkv], B_R2[s], B_ss[s]], accum=ss[:, s, i:i + 1])

                        def c1_B(t_):
                            s = t_ % 2
                            tt("dve", XQ[:, s, :, :], XQ[:, s, :, :], gqk[:], ALU.mult, [B_XQ[s]] + B_rope, [B_XQ[s]])
                            tt("dve", R1[:, s, :, :], XQ[:, s, :, :], rope[:, 0, t_, :].unsqueeze(1).broadcast_to([128, 5, 128]), ALU.mult,
                               [B_XQ[s]] + B_rope, [B_R1[s]])
                            xv = XQ[:, s, :, :].rearrange("p h (a s d) -> p h a s d", a=2, s=2)
                            rv = R2[:, s, :, :].rearrange("p h (a s d) -> p h a s d", a=2, s=2)
                            sv = rope[:, 1, t_, :].rearrange("p (a s d) -> p a s d", a=2, s=2)
                            for hf in range(2):
                                tt("dve", rv[:, :, :, hf, :], xv[:, :, :, 1 - hf, :], sv[:, :, hf, :].unsqueeze(1).broadcast_to([128, 5, 2, 32]),
                                   ALU.mult, [B_XQ[s]] + B_rope, [B_R2[s]])
                            ts("dve", ss[:, s, 0:5], ss[:, s, 0:5], 1.0 / 128.0, EPS, ALU.mult, ALU.add, [B_ss[s]], [B_ss[s]])
                            tt("pool", ss[:, s, 0:5], ss[:, s, 0:5], mhalf[:, 0:5], ALU.pow, [B_ss[s], B_const], [B_ss[s]])
                            tt("dve", R1[:, s, :, :], R1[:, s, :, :], R2[:, s, :, :], ALU.add, [B_R1[s], B_R2[s]], [B_R1[s]])
                            tt("dve", QR[:, s, :, :], R1[:, s, :, :], ss[:, s, 0:5].unsqueeze(2).broadcast_to([128, 5, 128]), ALU.mult,
                               [B_R1[s], B_ss[s]], [B_QR[s]])

                        def c1_C(t_):
                            s = t_ % 2
                            tsl = slice(t_ * 128, (t_ + 1) * 128)
                            for i in range(5):
                                tr(psb[:, i * 128:(i + 1) * 128], QR[:, s, i, :], ident_b[:], [B_QR[s], B_const], [PBB])
                            cp("act", QT[:, :, tsl], psb[:, 0:512].rearrange("p (a b) -> p a b", b=128), [], [PBB, B_QT])
                            cp("act", KT[:, tsl], psb[:, 512:640], [], [PBB, B_KT])

                        pipeline(NT, [c1_A, c1_B, c1_C], oldest_first=True)
                        S.emit()
                        sb.set(mark_c)
                        PT = sb("c_PT", [128, 2, 16, 512], BF16); HA = sb("c_ha", [128, 2, 4, 128], BF16); rr = sb("c_rr", [128, 2, 4], F32)

                        def c2_SP(it):
                            qb = it
                            s = qb % 2
                            qsl = slice((qb % NT) * 128, (qb % NT + 1) * 128)
                            pq = it - 1
                            sp_ = pq % 2
                            for kt in range(NT):
                                if qb < NT:
                                    bA = 2 + 2 * ((kt // 2) % 2)
                                    bk = bA + kt % 2
                                    mm(ps[:, bk, :], KT[:, kt * 128:(kt + 1) * 128], QT[:, :, qsl], True, True, [B_KT, B_QT], [PB[bk]])
                                    if kt % 2 == 1:
                                        act(PT[:, s, kt - 1:kt + 1, :], ps[:, bA:bA + 2, :], AF.Exp, [B_const], [PB[bA], PB[bA + 1], B_PT[s]],
                                            bias=negc[:], scale=1.0 / math.sqrt(128.0))
                                if pq >= 0:
                                    for g in range(4):
                                        ob = g // 2
                                        oc = (g % 2) * 256
                                        mm(ps[:, ob, oc:oc + 129], PT[:, sp_, kt, g * 128:(g + 1) * 128], VA[:, kt, :], kt == 0 and g % 2 == 0, kt == NT - 1,
                                           [B_PT[sp_], B_VA], [PB[ob]], sgc=True)
                            if pq >= 0:
                                for g in range(4):
                                    ob = g // 2
                                    oc = (g % 2) * 256
                                    recip(rr[:, sp_, g:g + 1], ps[:, ob, oc + 128:oc + 129], [], [PB[ob], B_rr[sp_]])
                                    ts("dve", HA[:, sp_, g, :], ps[:, ob, oc:oc + 128], rr[:, sp_, g:g + 1], None, ALU.mult, None, [B_rr[sp_]], [PB[ob], B_HA[sp_]])

                        def c2_T(qb):
                            s = qb % 2
                            qsl = slice(qb * 128, (qb + 1) * 128)
                            for g in range(4):
                                tr(psb[:, g * 128:(g + 1) * 128], HA[:, s, g, :], ident_b[:], [B_HA[s], B_const], [PBB])
                            cp("dve", haT[:, hk * 4:(hk + 1) * 4, qsl], psb[:, 0:512].rearrange("p (a b) -> p a b", b=128), [], [PBB, B_ha])

                        for it in range(NT + 2):
                            if it <= NT:
                                c2_SP(it)
                            if it - 2 >= 0:
                                c2_T(it - 2)

                        S.emit()


            sb.set([(H_END, TOP)])
            memt = sb("d_mem", [128, 2, D], F32)
            memT = sb("d_memT", [128, 8, 256], BF16)
            wkv = sb("d_wkv", [128, 8, 1024], BF16)
            wqc = sb("d_wq", [128, 8, 512], BF16)
            kcT = sb("d_kT", [128, 4, 256], BF16)
            vc = sb("d_v", [128, 2, 512], BF16)
            qcT = sb("d_qT", [128, 4, T], BF16)
            E = sb("d_E", [128, 2, 4, 256], F32)
            Pn = sb("d_P", [128, 2, 4, 256], BF16)
            PcT = sb("d_PT", [128, 2, 8, 128], BF16)
            dsm = sb("d_s", [128, 2, 8], F32)
            dsm2 = sb("d_s2", [128, 2, 8], F32)
            if True:
                B_mem = Buf(); B_memT = Buf(); B_wkv = Buf(); B_wq = Buf(); B_k = Buf(); B_v = Buf(); B_q = Buf()
                B_E = [Buf(), Buf()]; B_P = [Buf(), Buf()]; B_PcT = [Buf(), Buf()]; B_s = [Buf(), Buf()]; B_s2 = [Buf(), Buf()]
                mdma(memt[:], mem_d[seq].rearrange("(a p) d -> p a d", p=128), [], [B_mem])
                for q4 in range(4):
                    load_w(wkv[:, 2 * q4:2 * q4 + 2, :], w_mkv[q4 * 256:(q4 + 1) * 256, :].rearrange("(kc p) n -> p kc n", p=128), 2, 1024, B_wkv, ("dkv", q4))
                for q4 in range(2):
                    load_w(wqc[:, 4 * q4:4 * q4 + 4, :], w_in[q4 * 512:(q4 + 1) * 512, O_QC:O_QC + 512].rearrange("(kc p) n -> p kc n", p=128), 4, 512, B_wq, ("dq", q4))
                for a in range(2):
                    for kc in range(8):
                        bk = (a * 8 + kc) // 4
                        tr(ps[:, bk, (kc % 4) * 128:(kc % 4 + 1) * 128], memt[:, a, kc * 128:(kc + 1) * 128], ident_f[:], [B_mem, B_const], [PB[bk]])
                    for hh in range(2):
                        cp("dve", memT[:, hh * 4:(hh + 1) * 4, a * 128:(a + 1) * 128],
                           ps[:, a * 2 + hh, :].rearrange("p (a b) -> p a b", b=128), [], [PB[a * 2 + hh], B_memT])
                for hh in range(4):
                    bk = 4 + hh % 2
                    for kc in range(8):
                        mm(ps[:, bk, 0:256], wkv[:, kc, hh * 128:(hh + 1) * 128], memT[:, kc, :], kc == 0, kc == 7, [B_wkv, B_memT], [PB[bk]])
                    cp("act", kcT[:, hh, :], ps[:, bk, 0:256], [], [PB[bk], B_k])
                for a in range(2):
                    bk = 4 + a
                    for kc in range(8):
                        mm(ps[:, bk, :], memT[:, kc, a * 128:(a + 1) * 128], wkv[:, kc, 512:1024], kc == 0, kc == 7, [B_wkv, B_memT], [PB[bk]])
                    cp("act", vc[:, a, :], ps[:, bk, :], [], [PB[bk], B_v])
                for hh in range(4):
                    for g in range(4):
                        bk = (hh * 4 + g) % 4
                        for kc in range(8):
                            mm(ps[:, bk, :], wqc[:, kc, hh * 128:(hh + 1) * 128], xx[:, kc, g * 512:(g + 1) * 512], kc == 0, kc == 7,
                               [B_wq, B_xx[g]], [PB[bk]])
                        cp("act" if g % 2 else "dve", qcT[:, hh, g * 512:(g + 1) * 512], ps[:, bk, :], [], [PB[bk], B_q])
                sc = 1.0 / math.sqrt(128.0)

                def d_1(t_):
                    s = t_ % 2
                    tsl = slice(t_ * 128, (t_ + 1) * 128)
                    for hh in range(4):
                        bS = 2 * s + hh // 2
                        mm(ps[:, bS, (hh % 2) * 256:(hh % 2 + 1) * 256], qcT[:, hh, tsl], kcT[:, hh, :], True, True, [B_q, B_k], [PB[bS]])
                    S4 = ps[:, 2 * s:2 * s + 2, :].rearrange("p a (b m) -> p (a b) m", m=256)
                    S.op("dve", lambda e: e.tensor_reduce(out=dsm[:, s, 0:4], in_=S4, axis=AX.X, op=ALU.max),
                         reads=[], writes=[PB[2 * s], PB[2 * s + 1], B_s[s]])
                    ts("dve", dsm[:, s, 4:8], dsm[:, s, 0:4], -sc, None, ALU.mult, None, [B_s[s]], [B_s[s]])
                    for hh in range(4):
                        bS = 2 * s + hh // 2
                        act(E[:, s, hh, :], ps[:, bS, (hh % 2) * 256:(hh % 2 + 1) * 256], AF.Exp, [B_s[s]], [PB[bS], B_E[s], B_s2[s]],
                            bias=dsm[:, s, 4 + hh:5 + hh], scale=sc, accum=dsm2[:, s, hh:hh + 1])

                def d_2(t_):
                    s = t_ % 2
                    recip(dsm2[:, s, 4:8], dsm2[:, s, 0:4], [B_s2[s]], [B_s2[s]])
                    tt("dve", Pn[:, s, :, :], E[:, s, :, :], dsm2[:, s, 4:8].unsqueeze(2).broadcast_to([128, 4, 256]), ALU.mult, [B_E[s], B_s2[s]], [B_P[s]])
                    for hh in range(4):
                        for a in range(2):
                            tr(psb[:, (hh * 2 + a) * 128:(hh * 2 + a + 1) * 128], Pn[:, s, hh, a * 128:(a + 1) * 128], ident_b[:], [B_P[s], B_const], [PBB])

                def d_3(t_):
                    s = t_ % 2
                    tsl = slice(t_ * 128, (t_ + 1) * 128)
                    cp("act", PcT[:, s, :, :], psb[:].rearrange("p (a b) -> p a b", b=128), [], [PBB, B_PcT[s]])
                    bO = 4 + s
                    for hh in range(4):
                        for a in range(2):
                            mm(ps[:, bO, hh * 128:(hh + 1) * 128], vc[:, a, hh * 128:(hh + 1) * 128], PcT[:, s, hh * 2 + a, :], a == 0, a == 1,
                               [B_v, B_PcT[s]], [PB[bO]])

                def d_4(t_):
                    s = t_ % 2
                    tsl = slice(t_ * 128, (t_ + 1) * 128)
                    bO = 4 + s
                    cp("act", hcT[:, :, tsl], ps[:, bO, :].rearrange("p (a b) -> p a b", b=128), [], [PB[bO], B_hc])

                pipeline(NT, [d_1, d_2, d_3, d_4], order=[3, 2, 0, 1])
                S.emit()


            if dbg and seq == 0:
                cd = S.dma_ctr("dbg")
                dma(dbg_h[:, 0:4, :], hmT[:], [B_hm], [], cd)
                dma(dbg_h[:, 4:12, :], haT[:], [B_ha], [], cd)
                dma(dbg_h[:, 12:16, :], hcT[:], [B_hc], [], cd)
                S.emit()

            if True:
                B_mT = [Buf() for _ in range(4)]
                sb.set([(M_END, TOP)])
                wg = sb("e_wg", [128, 2, 3, 8, 128], BF16)
                wbr = sb("e_wb", [128, 2, 16, 128], BF16)
                SG = sb("e_sg", [128, 2, 3, 512], F32)
                Mt = sb("e_M", [128, 2, 2, 512], F32)
                if True:
                    B_wg = [Buf(), Buf()]; B_wb = [Buf(), Buf()]; B_sg = [Buf(), Buf()]; B_M = [Buf(), Buf()]

                    def load_e(cc):
                        s = cc % 2
                        for k in range(3):
                            load_w(wg[:, s, k, :, :], win_cols(O_GB + k * 1024 + cc * 128, 128), 8, 128, B_wg[s], ("eg", cc, k))
                        load_w(wbr[:, s, 0:4, :], w_bm[:, cc * 128:(cc + 1) * 128].rearrange("(kc p) n -> p kc n", p=128), 4, 128, B_wb[s], ("ebm", cc))
                        load_w(wbr[:, s, 4:12, :], w_ba[:, cc * 128:(cc + 1) * 128].rearrange("(kc p) n -> p kc n", p=128), 8, 128, B_wb[s], ("eba", cc))
                        load_w(wbr[:, s, 12:16, :], w_bc[:, cc * 128:(cc + 1) * 128].rearrange("(kc p) n -> p kc n", p=128), 4, 128, B_wb[s], ("ebc", cc))

                    load_e(0)
                    for cc in range(8):
                        s = cc % 2
                        if cc + 1 < 8:
                            load_e(cc + 1)
                        for g in range(4):
                            u = (cc * 4 + g) % 2
                            gsl = slice(g * 512, (g + 1) * 512)
                            for k in range(3):
                                for kc in range(8):
                                    mm(ps[:, k, :], wg[:, s, k, kc, :], xx[:, kc, gsl], kc == 0, kc == 7, [B_wg[s], B_xx[g]], [PB[k]])
                                act(SG[:, u, k, :], ps[:, k, :], AF.Sigmoid, [], [PB[k], B_sg[u]])
                            for kc in range(4):
                                mm(ps[:, 3, :], wbr[:, s, kc, :], hmT[:, kc, gsl], kc == 0, kc == 3, [B_wb[s], B_hm], [PB[3]])
                            for kc in range(8):
                                mm(ps[:, 4, :], wbr[:, s, 4 + kc, :], haT[:, kc, gsl], kc == 0, kc == 7, [B_wb[s], B_ha], [PB[4]])
                            for kc in range(4):
                                mm(ps[:, 5, :], wbr[:, s, 12 + kc, :], hcT[:, kc, gsl], kc == 0, kc == 3, [B_wb[s], B_hc], [PB[5]])
                            tt("dve", Mt[:, u, 0, :], ps[:, 3, :], SG[:, u, 0, :], ALU.mult, [B_sg[u]], [PB[3], B_M[u]])
                            tt("dve", Mt[:, u, 1, :], ps[:, 4, :], SG[:, u, 1, :], ALU.mult, [B_sg[u]], [PB[4], B_M[u]])
                            tt("dve", Mt[:, u, 0, :], Mt[:, u, 0, :], Mt[:, u, 1, :], ALU.add, [B_M[u]], [B_M[u]])
                            tt("dve", Mt[:, u, 1, :], ps[:, 5, :], SG[:, u, 2, :], ALU.mult, [B_sg[u]], [PB[5], B_M[u]])
                            tt("dve", mT[:, cc, gsl], Mt[:, u, 0, :], Mt[:, u, 1, :], ALU.add, [B_M[u]], [B_mT[g]])
                    S.emit()
                sb.set([(H0, H_END), (M_END, TOP)])
                wo = sb("e_wo", [128, 8, D], BF16)
                xt = sb("e_xt", [128, 2, D], F32)
                Z = sb("e_z", [128, 2, D], F32)
                ZH = sb("e_zh", [128, 2, D], F32)
                X1 = sb("e_x1", [128, 2, D], F32)
                est = sb("e_st", [128, 2, 16], F32)
                if True:
                    B_wo = Buf(); B_gb = Buf(); B_xt = [Buf(), Buf()]; B_z = [Buf(), Buf()]; B_zh = [Buf(), Buf()]; B_x1 = [Buf(), Buf()]
                    B_st = [Buf(), Buf()]
                    C_xt = slot_ctrs("ext")
                    C_x1 = slot_ctrs("ex1")
                    B_x1s = [Buf() for _ in range(NT)]
                    for q4 in range(4):
                        load_w(wo[:, 2 * q4:2 * q4 + 2, :], w_out[q4 * 256:(q4 + 1) * 256, :].rearrange("(kc p) n -> p kc n", p=128), 2, 1024, B_wo, ("wo", q4))
                    gbt = gbt_in
                    B_gb = B_const
                    def e2_a(t_):
                        s = t_ % 2
                        tsl = slice(t_ * 128, (t_ + 1) * 128)
                        dma(xt[:, s, :], xs[tsl, :], [], [B_xt[s]], C_xt[s])
                        b0 = 2 * s
                        for hf in range(2):
                            for kc in range(8):
                                mm(ps[:, b0 + hf, :], mT[:, kc, tsl], wo[:, kc, hf * 512:(hf + 1) * 512], kc == 0, False,
                                   [B_mT[t_ // 4], B_wo], [PB[b0 + hf]])
                            mm(ps[:, b0 + hf, :], ones_f[0:1, :], gbt[0:1, 1, hf * 512:(hf + 1) * 512], False, True, [B_const, B_gb], [PB[b0 + hf]])
                        act(xt[:, s, :], xt[:, s, :], AF.Identity, [B_xt[s], B_stats_in], [B_xt[s]], bias=stats_in[:, t_, 1:2], scale=stats_in[:, t_, 0:1])
                        tt("dve", xt[:, s, :], xt[:, s, :], gbt[:, 0, :], ALU.mult, [B_xt[s], B_gb], [B_xt[s]])

                    def e2_a2(t_):
                        s = t_ % 2
                        b0 = 2 * s
                        for hf in range(2):
                            tt("dve", Z[:, s, hf * 512:(hf + 1) * 512], ps[:, b0 + hf, :], xt[:, s, hf * 512:(hf + 1) * 512], ALU.add,
                               [B_xt[s]], [PB[b0 + hf], B_z[s]])
                        st = est[:, s, 0:12].rearrange("p (a b) -> p a b", b=6)
                        for i in range(2):
                            S.op("dve", lambda e, i=i: e.bn_stats(out=st[:, i, :], in_=Z[:, s, i * 512:(i + 1) * 512]), reads=[B_z[s]], writes=[B_st[s]])
                        S.op("dve", lambda e: e.bn_aggr(out=est[:, s, 12:14], in_=st), reads=[B_st[s]], writes=[B_st[s]])
                        ts("dve", est[:, s, 14:15], est[:, s, 13:14], EPS, None, ALU.add, None, [B_st[s]], [B_st[s]])
                        tt("pool", est[:, s, 14:15], est[:, s, 14:15], mhalf[:, 0:1], ALU.pow, [B_st[s], B_const], [B_st[s]])

                    def e2_b(t_):
                        s = t_ % 2
                        stt(est[:, s, 15:16], est[:, s, 12:13], -1.0, est[:, s, 14:15], ALU.mult, ALU.mult, [B_st[s]], [B_st[s]])
                        act(ZH[:, s, :], Z[:, s, :], AF.Identity, [B_z[s], B_st[s]], [B_zh[s]], bias=est[:, s, 15:16], scale=est[:, s, 14:15])

                    def e2_c(t_):
                        s = t_ % 2
                        tsl = slice(t_ * 128, (t_ + 1) * 128)
                        for kc in range(8):
                            bk = 4 + (kc // 4)
                            tr(ps[:, bk, (kc % 4) * 128:(kc % 4 + 1) * 128], ZH[:, s, kc * 128:(kc + 1) * 128], ident_f[:], [B_zh[s], B_const], [PB[bk]])
                        for kc in range(8):
                            bk = 4 + (kc // 4)
                            act(xx[:, kc, tsl], ps[:, bk, (kc % 4) * 128:(kc % 4 + 1) * 128], AF.Identity, [B_const], [PB[bk], B_xx[t_ // 4]],
                                bias=ln1_fm[:, 1, kc:kc + 1], scale=ln1_fm[:, 0, kc:kc + 1])
                        dma_q2(x1s_d[tsl, :], ZH[:, s, :], [B_zh[s]], [B_x1s[t_]], C_x1[s])

                    pipeline(NT, [e2_a, e2_a2, e2_b, e2_c], oldest_first=True)
                    S.emit()


        sb.set([(H0, TOP)])
        wd = sb("f_wd", [128, 22, D], BF16)
        hT = sb("f_hT", [128, 22, 1024], BF16)
        wu = sb("f_wu", [128, 2, 2, 8, 128], BF16)
        Ab = sb("f_A", [128, 1024], BF16)
        halo = sb("f_halo", [128, 2, 4], F32)
        B_halo = [Buf(), Buf()]
        FPAIR = [0, 2, 5]
        gbt = sb("f_gb", [128, 4, D], F32)
        X1 = sb("f_x1", [128, 2, D], F32)
        Z = sb("f_z", [128, 2, D], F32)
        Y = Z
        fst = sb("f_st", [128, 2, 16], F32)
        if True:
            B_wd = Buf(); B_hT = Buf(); B_wu = [Buf(), Buf()]; B_pre = [Buf(), Buf()]; B_dg = [Buf(), Buf()]; B_A = [Buf(), Buf()]
            B_gb = Buf(); B_x1 = [Buf(), Buf()]; B_z = [Buf(), Buf()]; B_y = B_z; B_st = [Buf(), Buf()]
            C_x1 = slot_ctrs("fx1")
            C_y = slot_ctrs("fy")
            def load_u(j):
                s = j % 2
                load_w(wu[:, s, 0, :, :], w_up[:, j * 128:(j + 1) * 128].rearrange("(kc p) n -> p kc n", p=128), 8, 128, B_wu[s], ("wu", j, 0))
                load_w(wu[:, s, 1, :, :], w_up[:, DFF + j * 128:DFF + (j + 1) * 128].rearrange("(kc p) n -> p kc n", p=128), 8, 128, B_wu[s], ("wu", j, 1))

            load_u(0)
            B_gbl = [Buf() for _ in range(4)]
            for i, v_ in enumerate([ln2_g, ln2_b, ln1_g, ln1_b]):
                mdma(gbt[:, i, :], v_.partition_broadcast(128), [], [B_gbl[i]])
            ts("dve", gbt[:, 2:4, :], gbt[:, 2:4, :], ALPHA, None, ALU.mult, None, B_gbl, [B_gb])

            for half in range(2):
                h0 = half * 1024

                def f_U(j, half=half, h0=h0):
                    s = j % 2
                    if j + 1 < 22:
                        load_u(j + 1)
                    elif half == 0:
                        load_u(0)
                    if half == 0 and j % 2 == 0:
                        load_w(wd[:, j:j + 2, :], w_dn[j * 128:(j + 2) * 128, :].rearrange("(a p) n -> p a n", p=128), 2, 1024, B_wd, ("wd", j))
                    htok = 1024 if half == 0 else 1023
                    for part in range(2):
                        pb = FPAIR[(2 * j + part) % 3]
                        for g in range(2):
                            bk = pb + g
                            for kc in range(8):
                                mm(ps[:, bk, :], wu[:, s, part, kc, :], xx[:, kc, h0 + g * 512:h0 + (g + 1) * 512], kc == 0, kc == 7,
                                   [B_wu[s], B_xx[half * 2 + g]], [PB[bk]])
                        for kc in range(8):
                            mm(ps[:, 4, 2 * part:2 * part + 2], wu[:, s, part, kc, :], xx[:, kc, htok:htok + 2] if half == 0 else xx[:, kc, htok - 1:htok + 1],
                               kc == 0, kc == 7, [B_wu[s], B_xx[htok // 512], B_xx[(htok - 1) // 512]], [PB[4]])
                    cp("act", halo[:, s, :], ps[:, 4, 0:4], [], [PB[4], B_halo[s]])

                def f_V(j, half=half):
                    for part in range(2):
                        ch = part * 22 + j
                        w0 = fconv_fm[:, 0, ch:ch + 1]; w1 = fconv_fm[:, 1, ch:ch + 1]; w2 = fconv_fm[:, 2, ch:ch + 1]; bb = fconv_fm[:, 3, ch:ch + 1]
                        pb = FPAIR[(2 * j + part) % 3]
                        s = j % 2
                        v = ps[:, pb:pb + 2, :].rearrange("p a b -> p (a b)")
                        PBv = [PB[pb], PB[pb + 1]]
                        t0 = X1[:, part, :]; t1 = Z[:, part, :]
                        Bt0 = B_x1[part]; Bt1 = B_z[part]
                        hv = halo[:, s, 2 * part:2 * part + 1] if half == 0 else halo[:, s, 2 * part + 1:2 * part + 2]
                        act(t0, v, AF.Identity, [B_const], PBv + [Bt0], bias=bb, scale=w1)
                        stt(t1[:, 1:1024], v[:, 0:1023], w0, t0[:, 1:1024], ALU.mult, ALU.add, [B_const, Bt0], PBv + [Bt1])
                        if half == 0:
                            cp("dve", t1[:, 0:1], t0[:, 0:1], [Bt0], [Bt1])
                        else:
                            stt(t1[:, 0:1], hv, w0, t0[:, 0:1], ALU.mult, ALU.add, [B_const, Bt0, B_halo[s]], [Bt1])
                        stt(t0[:, 0:1023], v[:, 1:1024], w2, t1[:, 0:1023], ALU.mult, ALU.add, [B_const, Bt1], PBv + [Bt0])
                        if half == 0:
                            stt(t0[:, 1023:1024], hv, w2, t1[:, 1023:1024], ALU.mult, ALU.add, [B_const, Bt1, B_halo[s]], [Bt0])
                        else:
                            cp("dve", t0[:, 1023:1024], t1[:, 1023:1024], [Bt1], [Bt0])
                        if part == 0:
                            act(Ab[:], t0, AF.Gelu_apprx_tanh, [Bt0], [B_A[0]])
                        else:
                            tt("dve", hT[:, j, :], t0, Ab[:], ALU.mult, [Bt0, B_A[0]], [B_hT])

                for j in range(22):
                    f_U(j)
                    f_V(j)


                def f_a(t8, half=half):
                    t_ = half * 8 + t8
                    s = t_ % 2
                    tsl = slice(t_ * 128, (t_ + 1) * 128)
                    lsl = slice(t8 * 128, (t8 + 1) * 128)
                    dma(X1[:, s, :], x1s_d[tsl, :], [B_x1s[t_]], [B_x1[s]], C_x1[s])
                    b0 = 3 + 2 * s
                    for hf in range(2):
                        for j in range(22):
                            mm(ps[:, b0 + hf, :], hT[:, j, lsl], wd[:, j, hf * 512:(hf + 1) * 512], j == 0, False, [B_hT, B_wd], [PB[b0 + hf]])
                        mm(ps[:, b0 + hf, :], ones_f[0:1, :], gbt[0:1, 3, hf * 512:(hf + 1) * 512], False, True, [B_const, B_gb], [PB[b0 + hf]])
                    tt("dve", X1[:, s, :], X1[:, s, :], gbt[:, 2, :], ALU.mult, [B_x1[s], B_gb], [B_x1[s]])
                    for hf in range(2):
                        tt("dve", Z[:, s, hf * 512:(hf + 1) * 512], ps[:, b0 + hf, :], X1[:, s, hf * 512:(hf + 1) * 512], ALU.add,
                           [B_x1[s]], [PB[b0 + hf], B_z[s]])

                def f_b(t8, half=half):
                    t_ = half * 8 + t8
                    s = t_ % 2
                    tsl = slice(t_ * 128, (t_ + 1) * 128)
                    st = fst[:, s, 0:12].rearrange("p (a b) -> p a b", b=6)
                    layer_norm_stats(Z[:, s, :], st, fst[:, s, 12:14], fst[:, s, 14:15], fst[:, s, 15:16], B_z[s], B_st[s])
                    act(Y[:, s, :], Z[:, s, :], AF.Identity, [B_z[s], B_st[s]], [B_z[s]], bias=fst[:, s, 15:16], scale=fst[:, s, 14:15])
                    tt("dve", Y[:, s, :], Y[:, s, :], gbt[:, 0, :], ALU.mult, [B_y[s], B_gb], [B_y[s]])
                    tt("dve", Y[:, s, :], Y[:, s, :], gbt[:, 1, :], ALU.add, [B_y[s], B_gb], [B_y[s]])
                    dma_q2(y_d[seq, tsl, :], Y[:, s, :], [B_y[s]], [], C_y[s])

                pipeline(8, [f_a, f_b], oldest_first=True)

            S.emit(final=(seq == nseq - 1))
    global LAST_MARKS
    LAST_MARKS = S.marks
    return nc


LAST_MARKS = None

def _consts():
    ident = np.eye(128, dtype=np.float32)
    s = np.arange(128)[:, None]
    t = np.arange(128)[None, :]
    maskF = (s <= t).astype(np.float32)
    maskB = (s >= t).astype(np.float32)
    tok = np.arange(T)
    row = (tok // 64).astype(np.float32)
    col = (tok % 64).astype(np.float32)
    inv = (np.float32(10000.0) ** (-np.arange(0, 64, 2, dtype=np.float32) / np.float32(64))).astype(np.float32)
    ar = (row[:, None] * inv[None, :]).astype(np.float32)
    ac = (col[:, None] * inv[None, :]).astype(np.float32)
    C = np.concatenate([np.cos(ar), np.cos(ar), np.cos(ac), np.cos(ac)], axis=1).astype(np.float32)
    Sg = np.concatenate([-np.sin(ar), np.sin(ar), -np.sin(ac), np.sin(ac)], axis=1).astype(np.float32)
    return {"c_ident": ident, "c_maskF": maskF, "c_maskB": maskB, "c_ropeC": C, "c_ropeS": Sg}


def _shared_inputs(inp):
    f = lambda a: np.ascontiguousarray(np.asarray(a, dtype=np.float32))
    d = {
        "w_in": f(inp["w_in"][0]), "w_mkv": f(inp["w_mem_kv"][0]), "w_bm": f(inp["w_branch_mlstm"][0]),
        "w_ba": f(inp["w_branch_attn"][0]), "w_bc": f(inp["w_branch_mem"][0]), "w_out": f(inp["w_out"][0]),
        "w_up": f(inp["w_ffn_up"][0]), "w_dn": f(inp["w_ffn_down"][0]),
        "lnin_g": f(inp["ln_in_g"]), "lnin_b": f(inp["ln_in_b"]),
        "ln1_g": f(inp["ln1_g"][0]), "ln1_b": f(inp["ln1_b"][0]), "ln2_g": f(inp["ln2_g"][0]), "ln2_b": f(inp["ln2_b"][0]),
        "gbias": f(inp["mlstm_gate_bias"][0]), "mconv_w": f(inp["mlstm_conv_w"][0]), "mconv_b": f(inp["mlstm_conv_b"][0]),
        "mnorm_g": f(inp["mlstm_norm_g"][0]), "qg": f(inp["attn_q_norm_g"][0]), "kg": f(inp["attn_k_norm_g"][0]),
        "fconv_w": f(inp["ffn_conv_w"][0]), "fconv_b": f(inp["ffn_conv_b"][0]),
    }
    d.update(_consts())
    return d


def kernel(**inp):
    ncores = 8
    xs = np.concatenate([np.asarray(inp["x_prompt"], np.float32), np.asarray(inp["x_sample"], np.float32)], axis=0)
    ms = np.concatenate([np.asarray(inp["mem_prompt"], np.float32), np.asarray(inp["mem_sample"], np.float32)], axis=0)
    nseq = xs.shape[0] // ncores
    shared = _shared_inputs(inp)
    nc = build(nseq)
    in_maps = []
    for c in range(ncores):
        m = dict(shared)
        m["x"] = np.ascontiguousarray(xs[c * nseq:(c + 1) * nseq])
        m["mem"] = np.ascontiguousarray(ms[c * nseq:(c + 1) * nseq])
        in_maps.append(m)
    res = run_bass_kernel_spmd(nc, in_maps, core_ids=list(range(ncores)))
    y = np.concatenate([np.asarray(r["y"], np.float32) for r in res.results], axis=0)
    nb = inp["x_prompt"].shape[0]
    return (y[:nb], y[nb:])
```
